# Optimizing a Trainium2 kernel written in Bass

```python
import math
import jax, jax.numpy as jnp
from jax import lax
import numpy as np

D_MODEL = 2048
BATCH = 2
SEQ = 16384
DEPTH = 2

GRID_W = 64
CTX_LEN = 256
HEAD_DIM = 128
A_HEADS = 8
B_HEADS = 8
B_KV_HEADS = 2
B_GROUP = B_HEADS // B_KV_HEADS
C_W = D_MODEL // 2
D_HEADS = 4
NA_ROWS = 8
NA_COLS = 16
ROPE_THETA = 10000.0
D_FF = 5632
Q_BLOCK = 128
EPS = 1e-6
N_EVEN = (DEPTH + 1) // 2
N_ODD = DEPTH // 2

A_W = A_HEADS * HEAD_DIM
B_W = B_HEADS * HEAD_DIM
B_KV_W = B_KV_HEADS * HEAD_DIM
AB_KV_START = A_W + B_W
AB_COLS = AB_KV_START + 2 * A_W + 2 * B_KV_W
D_QW = D_HEADS * 2 * HEAD_DIM
D_VW = D_HEADS * 2 * HEAD_DIM
CD_KV_START = 3 * C_W + D_QW
CD_COLS = CD_KV_START + D_QW + D_VW
F32 = jnp.float32

kernel_name = 'hybrid_natten_gqa_shortconv_diffattn_dit'


def rms_norm(x, g):
    xf = x.astype(F32)
    y = xf * lax.rsqrt(jnp.mean(xf * xf, axis=-1, keepdims=True) + EPS)
    return (y * g.astype(F32)).astype(x.dtype)


def modulate(h, shift, scale):
    return h * (1 + scale) + shift


def dwconv3(x, w):
    xp = jnp.pad(x, ((0, 0), (1, 1), (0, 0)))
    return xp[:, :-2] * w[0] + xp[:, 1:-1] * w[1] + xp[:, 2:] * w[2]


def axial_rope_tables(n_tokens):
    t = jnp.arange(n_tokens)
    axis_dim = HEAD_DIM // 2
    inv = 1.0 / (ROPE_THETA ** (jnp.arange(0, axis_dim, 2, dtype=F32) / axis_dim))
    ang_r = (t // GRID_W).astype(F32)[:, None] * inv
    ang_c = (t % GRID_W).astype(F32)[:, None] * inv
    ang = jnp.concatenate([ang_r, ang_r, ang_c, ang_c], axis=-1)
    return jnp.cos(ang), jnp.sin(ang)


def apply_rope(x, cos, sin):
    shape = (1, x.shape[1]) + (1,) * (x.ndim - 3) + (x.shape[-1],)
    xf = x.astype(F32)
    x1, x2, x3, x4 = jnp.split(xf, 4, axis=-1)
    rot = jnp.concatenate([-x2, x1, -x4, x3], axis=-1)
    return (xf * cos.reshape(shape) + rot * sin.reshape(shape)).astype(x.dtype)


def sweep_query_blocks(fn, q):
    b, s = q.shape[:2]
    qb = jnp.moveaxis(q.reshape((b, s // Q_BLOCK, Q_BLOCK) + q.shape[2:]), 1, 0)
    out = jnp.moveaxis(lax.map(fn, qb), 0, 1)
    return out.reshape((b, s) + out.shape[3:])


def gqa_attention(q, k, v):
    scale = q.shape[-1] ** -0.5

    def block(qb):
        s = jnp.einsum('bqgrd,bkgd->bgrqk', qb, k).astype(F32) * scale
        p = jax.nn.softmax(s, axis=-1).astype(v.dtype)
        return jnp.einsum('bgrqk,bkgd->bqgrd', p, v)

    return sweep_query_blocks(block, q)


def diff_attention(q, k, v, lam):
    scale = q.shape[-1] ** -0.5

    def block(qb):
        s = jnp.einsum('bqhcd,bkhcd->bhcqk', qb, k).astype(F32) * scale
        p = jax.nn.softmax(s, axis=-1)
        a = (p[:, :, 0] - lam * p[:, :, 1]).astype(v.dtype)
        return jnp.einsum('bhqk,bkhe->bqhe', a, v)

    return sweep_query_blocks(block, q)


def neighbourhood_attention(q, k, v, kc, vc, rpb):
    b, s, h, d = q.shape
    rows = s // GRID_W
    kh = min(NA_ROWS, rows)
    kw = NA_COLS
    scale = d ** -0.5
    qg = q.reshape(b, rows, GRID_W, h, d)
    kg = k.reshape(b, rows, GRID_W, h, d)
    vg = v.reshape(b, rows, GRID_W, h, d)
    col = jnp.arange(GRID_W)
    col_start = jnp.clip(col - kw // 2, 0, GRID_W - kw)
    col_idx = col_start[:, None] + jnp.arange(kw)[None, :]
    col_bias_idx = col_idx - col[:, None] + NA_COLS - 1

    def row_block(r):
        r0 = jnp.clip(r - kh // 2, 0, rows - kh)
        kn = lax.dynamic_slice_in_dim(kg, r0, kh, axis=1)[:, :, col_idx]
        vn = lax.dynamic_slice_in_dim(vg, r0, kh, axis=1)[:, :, col_idx]
        qr = lax.dynamic_index_in_dim(qg, r, axis=1, keepdims=False)
        row_bias_idx = r0 + jnp.arange(kh) - r + NA_ROWS - 1
        bias = rpb[:, row_bias_idx[:, None, None], col_bias_idx[None, :, :]]
        s_nb = jnp.einsum('bqhd,bjqwhd->bhqjw', qr, kn).astype(F32) * scale
        s_nb = s_nb + jnp.transpose(bias, (0, 2, 1, 3)).astype(F32)[None]
        s_ctx = jnp.einsum('bqhd,blhd->bhql', qr, kc).astype(F32) * scale
        sc = jnp.concatenate([s_nb.reshape(b, h, GRID_W, kh * kw), s_ctx], axis=-1)
        p = jax.nn.softmax(sc, axis=-1).astype(v.dtype)
        p_nb = p[..., :kh * kw].reshape(b, h, GRID_W, kh, kw)
        return (jnp.einsum('bhqjw,bjqwhd->bqhd', p_nb, vn)
                + jnp.einsum('bhql,blhd->bqhd', p[..., kh * kw:], vc))

    out = lax.map(row_block, jnp.arange(rows))
    return jnp.moveaxis(out, 0, 1).reshape(b, s, h, d)


def ab_kv(p_kv):
    a_k, a_v, b_k, b_v = jnp.split(p_kv, [A_W, 2 * A_W, 2 * A_W + B_KV_W], axis=-1)
    lead = p_kv.shape[:2]
    return (a_k.reshape(lead + (A_HEADS, HEAD_DIM)), a_v.reshape(lead + (A_HEADS, HEAD_DIM)),
            b_k.reshape(lead + (B_KV_HEADS, HEAD_DIM)), b_v.reshape(lead + (B_KV_HEADS, HEAD_DIM)))


def mixer_ab(h_lat, h_ctx, w_in, w_out, rpb, q_gain, k_gain, cos, sin, ctx_out):
    b, s, _ = h_lat.shape
    p = h_lat @ w_in
    a_q = p[..., :A_W].reshape(b, s, A_HEADS, HEAD_DIM)
    b_q = p[..., A_W:AB_KV_START].reshape(b, s, B_KV_HEADS, B_GROUP, HEAD_DIM)
    a_k, a_v, b_k, b_v = ab_kv(p[..., AB_KV_START:])
    pc = h_ctx @ (w_in if ctx_out else w_in[:, AB_KV_START:])
    ac_k, ac_v, bc_k, bc_v = ab_kv(pc[..., AB_KV_START:] if ctx_out else pc)
    bc_k = rms_norm(bc_k, k_gain)
    b_q = apply_rope(rms_norm(b_q, q_gain), cos, sin)
    b_k = apply_rope(rms_norm(b_k, k_gain), cos, sin)
    o_a = neighbourhood_attention(a_q, a_k, a_v, ac_k, ac_v, rpb)
    o_b = gqa_attention(b_q, jnp.concatenate([bc_k, b_k], axis=1), jnp.concatenate([bc_v, b_v], axis=1))
    y_lat = jnp.concatenate([o_a.reshape(b, s, A_W), o_b.reshape(b, s, B_W)], axis=-1) @ w_out
    if not ctx_out:
        return y_lat, None
    l = h_ctx.shape[1]
    ac_q = pc[..., :A_W].reshape(b, l, A_HEADS, 1, HEAD_DIM)
    bc_q = rms_norm(pc[..., A_W:AB_KV_START].reshape(b, l, B_KV_HEADS, B_GROUP, HEAD_DIM), q_gain)
    oc_a = gqa_attention(ac_q, ac_k, ac_v)
    oc_b = gqa_attention(bc_q, bc_k, bc_v)
    y_ctx = jnp.concatenate([oc_a.reshape(b, l, A_W), oc_b.reshape(b, l, B_W)], axis=-1) @ w_out
    return y_lat, y_ctx


def short_conv(p_c, conv_w):
    u, g_b, g_c = jnp.split(p_c, 3, axis=-1)
    return g_b * dwconv3(g_c * u, conv_w)


def diff_kv(p_kv):
    lead = p_kv.shape[:2]
    k = p_kv[..., :D_QW].reshape(lead + (D_HEADS, 2, HEAD_DIM))
    v = p_kv[..., D_QW:].reshape(lead + (D_HEADS, 2 * HEAD_DIM))
    return k, v


def mixer_cd(h_lat, h_ctx, w_in, w_out, conv_w, lam, lam_init, subln_g, cos, sin, ctx_out):
    b, s, _ = h_lat.shape
    p = h_lat @ w_in
    y_c = short_conv(p[..., :3 * C_W], conv_w)
    d_q = apply_rope(p[..., 3 * C_W:CD_KV_START].reshape(b, s, D_HEADS, 2, HEAD_DIM), cos, sin)
    d_k, d_v = diff_kv(p[..., CD_KV_START:])
    d_k = apply_rope(d_k, cos, sin)
    pc = h_ctx @ (w_in if ctx_out else w_in[:, CD_KV_START:])
    dc_k, dc_v = diff_kv(pc[..., CD_KV_START:] if ctx_out else pc)
    o_d = diff_attention(d_q, jnp.concatenate([dc_k, d_k], axis=1), jnp.concatenate([dc_v, d_v], axis=1), lam)
    o_d = rms_norm(o_d, subln_g) * (1.0 - lam_init)
    y_lat = jnp.concatenate([y_c, o_d.reshape(b, s, D_VW)], axis=-1) @ w_out
    if not ctx_out:
        return y_lat, None
    l = h_ctx.shape[1]
    yc_c = short_conv(pc[..., :3 * C_W], conv_w)
    dc_q = pc[..., 3 * C_W:CD_KV_START].reshape(b, l, D_HEADS, 2, HEAD_DIM)
    oc_d = rms_norm(diff_attention(dc_q, dc_k, dc_v, lam), subln_g) * (1.0 - lam_init)
    y_ctx = jnp.concatenate([yc_c, oc_d.reshape(b, l, D_VW)], axis=-1) @ w_out
    return y_lat, y_ctx


def conv_glu(h, w_gate, w_up, conv_w, conv_b, w_down):
    g = dwconv3(h @ w_gate, conv_w) + conv_b
    return (jax.nn.silu(g) * (h @ w_up)) @ w_down


def setup_inputs(seed: int = 0) -> dict:
    key = jax.random.key(seed)
    ks = jax.random.split(key, 26)
    D = D_MODEL

    def nrm(k, shape, s):
        return jax.random.normal(k, shape, F32) * s

    return {
        'x': nrm(ks[0], (BATCH, SEQ, D), 1.0),
        'c': nrm(ks[1], (BATCH, D), 1.0),
        'ctx': nrm(ks[2], (BATCH, CTX_LEN, D), 1.0),
        'c_ctx': nrm(ks[3], (D,), 1.0),
        'ada_w': nrm(ks[4], (DEPTH, D, 6 * D), 0.5 * D ** -0.5),
        'ada_b': nrm(ks[5], (DEPTH, 6 * D), 0.02),
        'norm_g': 1.0 + nrm(ks[6], (DEPTH, 2, D), 0.02),
        'ab_w_in': nrm(ks[7], (N_EVEN, D, AB_COLS), D ** -0.5),
        'ab_w_out': nrm(ks[8], (N_EVEN, A_W + B_W, D), (A_W + B_W) ** -0.5),
        'na_rpb': nrm(ks[9], (N_EVEN, A_HEADS, 2 * NA_ROWS - 1, 2 * NA_COLS - 1), 0.1),
        'gqa_q_gain': 1.0 + nrm(ks[10], (N_EVEN, HEAD_DIM), 0.02),
        'gqa_k_gain': 1.0 + nrm(ks[11], (N_EVEN, HEAD_DIM), 0.02),
        'cd_w_in': nrm(ks[12], (N_ODD, D, CD_COLS), D ** -0.5),
        'cd_w_out': nrm(ks[13], (N_ODD, C_W + D_VW, D), (C_W + D_VW) ** -0.5),
        'sconv_w': nrm(ks[14], (N_ODD, 3, C_W), 0.5),
        'diff_lq1': nrm(ks[15], (N_ODD, HEAD_DIM), 0.1),
        'diff_lk1': nrm(ks[16], (N_ODD, HEAD_DIM), 0.1),
        'diff_lq2': nrm(ks[17], (N_ODD, HEAD_DIM), 0.1),
        'diff_lk2': nrm(ks[18], (N_ODD, HEAD_DIM), 0.1),
        'diff_subln_g': 1.0 + nrm(ks[19], (N_ODD, 2 * HEAD_DIM), 0.02),
        'ffn_w_gate': nrm(ks[20], (DEPTH, D, D_FF), D ** -0.5),
        'ffn_w_up': nrm(ks[21], (DEPTH, D, D_FF), D ** -0.5),
        'ffn_conv_w': nrm(ks[22], (DEPTH, 3, D_FF), 0.5),
        'ffn_conv_b': nrm(ks[23], (DEPTH, D_FF), 0.02),
        'ffn_w_down': nrm(ks[24], (DEPTH, D_FF, D), D_FF ** -0.5),
        'final_g': 1.0 + nrm(ks[25], (D,), 0.02),
    }


def reference(x, c, ctx, c_ctx, ada_w, ada_b, norm_g, ab_w_in, ab_w_out, na_rpb, gqa_q_gain, gqa_k_gain,
              cd_w_in, cd_w_out, sconv_w, diff_lq1, diff_lk1, diff_lq2, diff_lk2, diff_subln_g,
              ffn_w_gate, ffn_w_up, ffn_conv_w, ffn_conv_b, ffn_w_down, final_g):
    cos, sin = axial_rope_tables(x.shape[1])
    h_ctx = ctx
    for layer in range(DEPTH):
        last = layer == DEPTH - 1
        i = layer // 2
        mod_lat = (jax.nn.silu(c) @ ada_w[layer] + ada_b[layer])[:, None, :]
        mod_ctx = (jax.nn.silu(c_ctx) @ ada_w[layer] + ada_b[layer])[None, None, :]
        sh1, sc1, g1, sh2, sc2, g2 = jnp.split(mod_lat, 6, axis=-1)
        csh1, csc1, cg1, csh2, csc2, cg2 = jnp.split(mod_ctx, 6, axis=-1)
        h_lat = modulate(rms_norm(x, norm_g[layer, 0]), sh1, sc1)
        h_c = modulate(rms_norm(h_ctx, norm_g[layer, 0]), csh1, csc1)
        if layer % 2 == 0:
            y_lat, y_ctx = mixer_ab(h_lat, h_c, ab_w_in[i], ab_w_out[i], na_rpb[i], gqa_q_gain[i],
                                    gqa_k_gain[i], cos, sin, not last)
        else:
            lam_init = 0.8 - 0.6 * math.exp(-0.3 * layer)
            lam = (jnp.exp(jnp.sum(diff_lq1[i].astype(F32) * diff_lk1[i].astype(F32)))
                   - jnp.exp(jnp.sum(diff_lq2[i].astype(F32) * diff_lk2[i].astype(F32))) + lam_init)
            y_lat, y_ctx = mixer_cd(h_lat, h_c, cd_w_in[i], cd_w_out[i], sconv_w[i], lam, lam_init,
                                    diff_subln_g[i], cos, sin, not last)
        x = x + g1 * y_lat
        x = x + g2 * conv_glu(modulate(rms_norm(x, norm_g[layer, 1]), sh2, sc2), ffn_w_gate[layer],
                              ffn_w_up[layer], ffn_conv_w[layer], ffn_conv_b[layer], ffn_w_down[layer])
        if not last:
            h_ctx = h_ctx + cg1 * y_ctx
            h_ctx = h_ctx + cg2 * conv_glu(modulate(rms_norm(h_ctx, norm_g[layer, 1]), csh2, csc2),
                                           ffn_w_gate[layer], ffn_w_up[layer], ffn_conv_w[layer],
                                           ffn_conv_b[layer], ffn_w_down[layer])
    return rms_norm(x, final_g)
```

```python
import math
from contextlib import ExitStack
import numpy as np
import concourse.bass as bass
import concourse.mybir as mybir
from concourse.bass_utils import run_bass_kernel_spmd

F32 = mybir.dt.float32
BF16 = mybir.dt.bfloat16
AF = mybir.ActivationFunctionType
ALU = mybir.AluOpType
EPS = 1e-6
NEG = -30000.0
GW_ = 64


class Cfg:
    def __init__(self, D=2048, DFF=5632, SEQ=16384, CW=1024):
        self.D, self.DFF, self.SEQ, self.CW = D, DFF, SEQ, CW
        self.KD, self.KF, self.KC = D // 128, DFF // 128, CW // 128
        self.T = SEQ // 4
        self.NT = self.T // 512
        self.NCH6 = 6 * D // 128
        self.L = 2
        self.N0 = 4608
        self.N1 = 3 * CW + 3072
        self.FT = 510
        self.XW = self.T + 2 + 258


class Ev:
    __slots__ = ("sem", "val")

    def __init__(self, sem, val):
        self.sem, self.val = sem, val


class Buf:
    def __init__(self, t, name):
        self.t, self.name = t, name
        self.lw = {}
        self.rd = []
        self.dsem = None
        self.dcnt = 0


class Sched:
    def __init__(self, nc):
        self.nc = nc
        self.eng = {"pe": nc.tensor, "act": nc.scalar, "dve": nc.vector, "pool": nc.gpsimd, "sp": nc.sync}
        self.esem = {e: nc.alloc_semaphore(name="e_" + e) for e in self.eng}
        self.cnt = {e: 0 for e in self.eng}
        self.seen = {e: {} for e in self.eng}
        self.bufs = []
        self.nb = 0

    def buf(self, t, name=None):
        b = Buf(t, name or "b%d" % len(self.bufs))
        self.bufs.append(b)
        return b

    def sb(self, name, shape, dt, es=None):
        self.uid = getattr(self, "uid", 0) + 1
        name = "%s_%d" % (name, self.uid)
        if es is not None:
            return self.buf(es.enter_context(self.nc.sbuf_tensor(name, list(shape), dt)), name)
        return self.buf(self.nc.alloc_sbuf_tensor(name, list(shape), dt), name)

    def ps(self, name):
        return self.buf(self.nc.alloc_psum_tensor(name, [128, 512], F32), name)

    def _wait(self, e, ev):
        if e == "pe" and ev.sem is self.esem["pe"]:
            return
        k = id(ev.sem)
        if self.seen[e].get(k, 0) >= ev.val:
            return
        self.eng[e].wait_ge(ev.sem, ev.val)
        self.seen[e][k] = ev.val

    def _deps(self, e, r, w):
        for b in r:
            for ev in b.lw.values():
                self._wait(e, ev)
        for b in w:
            for ev in b.lw.values():
                self._wait(e, ev)
            for ev in b.rd:
                self._wait(e, ev)

    def _post(self, ev, key, r, w):
        for b in r:
            b.rd.append(ev)
            if len(b.rd) > 24:
                b.rd = b.rd[-24:] if False else b.rd
        for b in w:
            b.lw[key] = ev
            b.rd = []

    def _skip(self):
        import os, sys
        if not hasattr(self, "limit"):
            self.limit = int(os.environ.get("KLIMIT", "1000000000"))
            self.total = 0
            self.log = []
        self.total += 1
        if self.total > self.limit:
            return True
        f = sys._getframe(2)
        ln = [f.f_lineno]
        while f.f_back is not None and len(ln) < 4:
            f = f.f_back
            ln.append(f.f_lineno)
        self.log.append((self.total, ln))
        return False

    def op(self, e, fn, r=(), w=()):
        if self._skip():
            return None
        self._deps(e, r, w)
        ins = fn()
        self.cnt[e] += 1
        ins.then_inc(self.esem[e], 1)
        ev = Ev(self.esem[e], self.cnt[e])
        for b in r:
            b.rd = [x for x in b.rd if x.sem is not ev.sem]
        self._post(ev, e, r, w)
        return ins

    def pe(self, fn, r=(), w=()):
        return self.op("pe", fn, r, w)

    def act(self, fn, r=(), w=()):
        return self.op("act", fn, r, w)

    def dve(self, fn, r=(), w=()):
        return self.op("dve", fn, r, w)

    def pool(self, fn, r=(), w=()):
        return self.op("pool", fn, r, w)

    def dma(self, q, out, in_, sbuf, r=(), w=()):
        if self._skip():
            return None
        self._deps(q, r, w)
        if sbuf.dsem is None:
            sbuf.dsem = self.nc.alloc_semaphore(name="d_" + sbuf.name)
        ins = self.eng[q].dma_start(out=out, in_=in_, allow_slow_non_contiguous=True)
        ins.then_inc(sbuf.dsem, 16)
        sbuf.dcnt += 16
        ev = Ev(sbuf.dsem, sbuf.dcnt)
        for b in r:
            b.rd = [x for x in b.rd if x.sem is not ev.sem]
        self._post(ev, "d%d" % id(sbuf), r, w)
        return ins

    def coll(self, kind, op, groups, src, dst, cbuf):
        e = "pool"
        if self._skip():
            return None
        if cbuf.dsem is None:
            cbuf.dsem = self.nc.alloc_semaphore(name="c_" + cbuf.name)
        ins = self.nc.gpsimd.collective_compute(kind, op, replica_groups=groups, ins=[src], outs=[dst])
        ins.then_inc(cbuf.dsem)
        cbuf.dcnt += 1
        self.nc.gpsimd.wait_ge(cbuf.dsem, cbuf.dcnt)
        self.op("pool", lambda: self.nc.gpsimd.memset(cbuf.t[:], 0.0), w=[cbuf])
        return ins

    def barrier(self):
        evs = [Ev(self.esem[e], self.cnt[e]) for e in self.eng if self.cnt[e] > 0]
        for b in self.bufs:
            if b.dsem is not None and b.dcnt > 0 and not b.name.startswith("coll"):
                evs.append(Ev(b.dsem, b.dcnt))
        for e in self.eng:
            for ev in evs:
                if ev.sem is self.esem[e]:
                    continue
                self._wait(e, ev)
        for b in self.bufs:
            b.lw = {}
            b.rd = []
        self.nb += 1


class WMat:
    def __init__(self, nc, name, K, N, gw):
        self.name, self.K, self.N, self.gw = name, K, N, gw
        self.KCH = K // 128
        self.KL = self.KCH // 4
        assert self.KL * 4 == self.KCH and N % gw == 0
        self.G = N // gw
        self.rows, self.cols = self.G * 128, self.KL * gw
        self.src = nc.dram_tensor(name + "_f32", [self.rows, self.cols], F32, kind="ExternalInput")
        self.gpc = max(1, (1 << 20) // (128 * self.cols * 2))
        self.chunks = []
        g0 = 0
        while g0 < self.G:
            ng = min(self.gpc, self.G - g0)
            b16 = nc.dram_tensor("%s_b16_%d" % (name, g0), [ng * 128, self.cols], BF16)
            full = nc.dram_tensor("%s_full_%d" % (name, g0), [4 * ng * 128, self.cols], BF16)
            self.chunks.append((g0, ng, b16, full))
            g0 += ng

    def host(self, w, r):
        KL, gw, G = self.KL, self.gw, self.G
        ws = w[r * KL * 128:(r + 1) * KL * 128, :]
        a = ws.reshape(KL, 128, G, gw).transpose(2, 1, 0, 3)
        return np.ascontiguousarray(a.reshape(G * 128, KL * gw))

    def tile_shape(self):
        return [128, 4, self.KL * self.gw]

    def src_ap(self, g):
        g0, ng, b16, full = self.chunks[g // self.gpc]
        v = full.ap().rearrange("(r g p) c -> g p r c", r=4, g=ng, p=128)
        return v[g - g0]

    def lhs(self, wt, k, c0, c1):
        r, kl = k // self.KL, k % self.KL
        return wt.t[:, r, kl * self.gw + c0: kl * self.gw + c1]


class Stream:
    def __init__(self, S, bufs, loads):
        self.S, self.bufs, self.loads = S, bufs, loads
        self.issued = 0

    def get(self, i):
        nb = len(self.bufs)
        while self.issued < min(len(self.loads), i + nb):
            j = self.issued
            self.loads[j](self.bufs[j % nb])
            self.issued += 1
        return self.bufs[i % nb]


def build(cfg):
    r = _build(cfg)
    return r


def _build(cfg):
    import os
    STOP = int(os.environ.get('KSTOP', '99'))
    DBG = int(os.environ.get('KDBG', '0'))
    nc = bass.Bass("TRN2", target_bir_lowering=False)
    S = Sched(nc)
    global _LASTS
    _LASTS = S
    D, DFF, T, KD, KF, KC, L = cfg.D, cfg.DFF, cfg.T, cfg.KD, cfg.KF, cfg.KC, cfg.L
    NCH6, XW, NT = cfg.NCH6, cfg.XW, cfg.NT
    NKT = T // 128
    GRP = [[0, 1, 2, 3], [4, 5, 6, 7]]
    ALL8 = [list(range(8))]
    SC = 128 ** -0.5

    x_tm = nc.dram_tensor("x_tm", [T, D], F32, kind="ExternalInput")
    xh_tm = nc.dram_tensor("xh_tm", [512, D], F32, kind="ExternalInput")
    ctx_tm = nc.dram_tensor("ctx_tm", [256, D], F32, kind="ExternalInput")
    out_tm = nc.dram_tensor("out_tm", [T, D], F32, kind="ExternalOutput")
    sm_off, sm_w = small_layout(cfg)
    small = nc.dram_tensor("small", [128, sm_w], F32, kind="ExternalInput")
    cos_d = nc.dram_tensor("cosT", [128, T], F32, kind="ExternalInput")
    sin_d = nc.dram_tensor("sinT", [128, T], F32, kind="ExternalInput")
    cst_d = nc.dram_tensor("consts", [128, 256], F32, kind="ExternalInput")
    braw_d = nc.dram_tensor("braw", [8, 15, 64, 64], F32, kind="ExternalInput")
    adaw_d = nc.dram_tensor("adaw", [L, D, NCH6 * 32], F32, kind="ExternalInput")
    NCA = NCH6 // 4
    W = {}
    W["in0"] = WMat(nc, "w_in0", D, cfg.N0, 256)
    W["out0"] = WMat(nc, "w_out0", 2048, D, min(512, D))
    W["in1"] = WMat(nc, "w_in1", D, cfg.N1, 256)
    W["out1"] = WMat(nc, "w_out1", cfg.CW + 1024, D, min(512, D))
    for l in range(L):
        W["g%d" % l] = WMat(nc, "w_g%d" % l, D, DFF, 256)
        W["u%d" % l] = WMat(nc, "w_u%d" % l, D, DFF, 256)
        W["d%d" % l] = WMat(nc, "w_d%d" % l, DFF, D, 256)

    XT = [nc.dram_tensor("XT%d" % i, [KD, 128, XW], F32) for i in range(2)]
    XH = nc.dram_tensor("XH", [KD, 128, 512], F32)
    LAT0, CTX0 = 1, T + 3
    QT0 = nc.dram_tensor("QT0", [16, 128, T + 256], BF16)
    KaT = nc.dram_tensor("KaT", [8, 128, T + 512], BF16)
    KaTc = nc.dram_tensor("KaTc", [8, 128, 256], BF16)
    NKE = (T + 512) // 128
    Va = nc.dram_tensor("Va", [128, 8, NKE, 128], BF16)
    Vac = nc.dram_tensor("Vac", [128, 8, 2, 128], BF16)
    KbT_src = [nc.dram_tensor("KbT_src%d" % i, [128, T], BF16) for i in range(2)]
    KbT_all = [nc.dram_tensor("KbT_all%d" % i, [4 * 128, T], BF16) for i in range(2)]
    KbTc = nc.dram_tensor("KbTc", [2, 128, 256], BF16)
    Vb_src = [nc.dram_tensor("Vb_src%d" % i, [128, NKT * 128], BF16) for i in range(2)]
    Vb_all = [nc.dram_tensor("Vb_all%d" % i, [4 * 128, NKT * 128], BF16) for i in range(2)]
    Vbc = nc.dram_tensor("Vbc", [128, 2, 2, 128], BF16)
    QT1 = nc.dram_tensor("QT1", [8, 128, T], BF16)
    KdT_src = [nc.dram_tensor("KdT_src%d" % i, [128, T], BF16) for i in range(8)]
    KdT_all = [nc.dram_tensor("KdT_all%d" % i, [4 * 128, T], BF16) for i in range(8)]
    KdTc = nc.dram_tensor("KdTc", [8, 128, 256], BF16)
    NKH = NKT // 2
    Vd_src = [nc.dram_tensor("Vd_src%d" % i, [128, NKH * 256], BF16) for i in range(8)]
    Vd_all = [nc.dram_tensor("Vd_all%d" % i, [4 * 128, NKH * 256], BF16) for i in range(8)]
    Vdc = nc.dram_tensor("Vdc", [128, 4, 2, 256], BF16)
    GCU = nc.dram_tensor("GCU", [KC, 128, T + 2], F32)
    GB = nc.dram_tensor("GB", [KC, 128, T], F32)
    HXW = 2 * max(KD, KC)
    hx_src = nc.dram_tensor("hx_src", [128, HXW], F32)
    hx_all = nc.dram_tensor("hx_all", [4 * 128, HXW], F32)
    NAtab = nc.dram_tensor("NAtab", [3, 8, 128, 1536], BF16)
    MODW = L * 2 * (NCH6 // 4)
    modsrc = nc.dram_tensor("modsrc", [128, MODW], F32)
    moddst = nc.dram_tensor("moddst", [4 * 128, MODW], F32)

    smt = S.sb("smt", [128, sm_w], F32)
    cst = S.sb("cst", [128, 256], F32)
    cstb = S.sb("cstb", [128, 256], BF16)
    onesb = S.sb("onesb", [128, 128], BF16)
    onesf = S.sb("onesf", [128, 128], F32)
    drv = S.sb("drv", [128, L * 2 * 6 * KD + 8], F32)
    lamt = S.sb("lamt", [128, 8], F32)
    fixw = S.sb("fixw", [128, L * 4 * KF], F32)
    PS = [S.ps("ps%d" % i) for i in range(8)]
    collb = S.sb("coll", [128, 1], F32)
    collb.name = "coll"
    gsc = S.sb("gsc", [128, 2], F32)

    def sm(name, a=None, b=None):
        o, w = sm_off[name]
        if a is None:
            return smt.t[:, o:o + w]
        return smt.t[:, o + a:o + (b if b is not None else a + 1)]

    ident = cst.t[:, 0:128]
    Rmb = cstb.t[:, 128:256]
    identb = cstb.t[:, 0:128]

    def DRV(l, tt, i, k=None):
        o = ((l * 2 + tt) * 6 + i) * KD
        if k is None:
            return drv.t[:, o:o + KD]
        return drv.t[:, o + k:o + k + 1]

    es0 = ExitStack()
    S.dma("sp", smt.t[:], small[:, :], smt, w=[smt])
    S.dma("sp", cst.t[:], cst_d[:, :], cst, w=[cst])
    S.dve(lambda: nc.vector.tensor_copy(cstb.t[:], cst.t[:]), r=[cst], w=[cstb])
    S.dve(lambda: nc.vector.memset(onesb.t[:], 1.0), w=[onesb])
    S.dve(lambda: nc.vector.memset(onesf.t[:], 1.0), w=[onesf])

    wcast = S.buf(None, "wcast")
    for key in ["in0", "out0", "g0", "u0", "d0", "in1", "out1", "g1", "u1", "d1"]:
        wm = W[key]
        for (g0, ng, b16, full) in wm.chunks:
            S.dma("pool", b16[:, :], wm.src[g0 * 128:(g0 + ng) * 128, :], wcast)
            if wcast.dsem is not None and (wcast.dcnt // 16) % 4 == 0:
                nc.gpsimd.wait_ge(wcast.dsem, wcast.dcnt)
    if wcast.dsem is not None:
        nc.gpsimd.wait_ge(wcast.dsem, wcast.dcnt)
    if STOP == 1:
        return nc, W
    for key in ["in0", "out0", "g0", "u0", "d0", "in1", "out1", "g1", "u1", "d1"]:
        wm = W[key]
        for (g0, ng, b16, full) in wm.chunks:
            S.coll("AllGather", ALU.bypass, GRP, b16.ap().opt(), full.ap().opt(), collb)
    if STOP == 2:
        return nc, W
    scv = S.sb("scv", [128, KD * 2], F32, es0)
    S.act(lambda: nc.scalar.activation(out=scv.t[:], in_=sm("cv"), func=AF.Silu), r=[smt], w=[scv])
    modloc = S.sb("modloc", [128, L * 2 * NCA], F32, es0)
    Z = S.sb("Z", [128, 4, L * 2 * NCA], F32, es0)
    CB = 4
    adaw = [S.sb("adaw%d" % i, [128, KD, CB * 128], F32, es0) for i in range(2)]
    ai = 0
    for l in range(L):
        ps = PS[l % 2]
        for cb in range(0, NCA, CB):
            nb = min(CB, NCA - cb)
            aw = adaw[ai % 2]
            ai += 1
            S.dma("sp", aw.t[:, :, :nb * 128],
                  adaw_d.ap()[l, :, cb * 128:(cb + nb) * 128].rearrange("(k p) n -> p k n", p=128), aw, w=[aw])
            for ci in range(nb):
                cc = cb + ci
                for k in range(KD):
                    S.pe(lambda k=k, ci=ci, cc=cc, ps=ps, aw=aw: nc.tensor.matmul(
                        ps.t[:, cc * 2:cc * 2 + 2], lhsT=aw.t[:, k, ci * 128:(ci + 1) * 128],
                        rhs=scv.t[:, k * 2:(k + 1) * 2], start=(k == 0), stop=(k == KD - 1)), r=[aw, scv], w=[ps])
        for v in range(2):
            o = (l * 2 + v) * NCA
            S.dve(lambda o=o, v=v, ps=ps: nc.vector.tensor_copy(
                modloc.t[:, o:o + NCA], ps.t[:, v:2 * NCA:2]), r=[ps], w=[modloc])
    S.dma("sp", modsrc[:, :], modloc.t[:], modloc, r=[modloc])
    if modloc.dsem is not None:
        nc.gpsimd.wait_ge(modloc.dsem, modloc.dcnt)
    S.coll("AllGather", ALU.bypass, GRP, modsrc.ap().opt(), moddst.ap().opt(), collb)
    S.dma("sp", Z.t[:], moddst.ap().rearrange("(r p) c -> p r c", p=128), Z, r=[collb], w=[Z])
    modv = S.sb("modv", [128, L * 2 * NCH6], F32, es0)
    for l in range(L):
        for v in range(2):
            for r in range(4):
                o = (l * 2 + v) * NCH6 + r * NCA
                zi = (l * 2 + v) * NCA
                ab = sm("adab", l * NCH6 + r * NCA, l * NCH6 + (r + 1) * NCA)
                S.dve(lambda o=o, zi=zi, r=r, ab=ab: nc.vector.tensor_tensor(
                    modv.t[:, o:o + NCA], Z.t[:, r, zi:zi + NCA], ab, op=ALU.add), r=[Z, smt], w=[modv])
        for tt in range(2):
            mo = (l * 2 + tt) * NCH6
            for half in range(2):
                sh = modv.t[:, mo + (3 * half) * KD: mo + (3 * half + 1) * KD]
                scl = modv.t[:, mo + (3 * half + 1) * KD: mo + (3 * half + 2) * KD]
                gt = modv.t[:, mo + (3 * half + 2) * KD: mo + (3 * half + 3) * KD]
                ng = sm("ng", (l * 2 + half) * KD, (l * 2 + half + 1) * KD)
                S.dve(lambda l=l, tt=tt, half=half, scl=scl, ng=ng: nc.vector.scalar_tensor_tensor(
                    DRV(l, tt, 3 * half), scl, 1.0, ng, op0=ALU.add, op1=ALU.mult), r=[modv, smt], w=[drv])
                S.dve(lambda l=l, tt=tt, half=half, sh=sh: nc.vector.tensor_copy(DRV(l, tt, 3 * half + 1), sh),
                      r=[modv], w=[drv])
                S.dve(lambda l=l, tt=tt, half=half, gt=gt: nc.vector.tensor_copy(DRV(l, tt, 3 * half + 2), gt),
                      r=[modv], w=[drv])
    if STOP == 3:
        return nc, W
    lam_init = 0.8 - 0.6 * math.exp(-0.3 * 1)
    S.dve(lambda: nc.vector.tensor_tensor(lamt.t[:, 0:1], sm("lq1"), sm("lk1"), op=ALU.mult), r=[smt], w=[lamt])
    S.dve(lambda: nc.vector.tensor_tensor(lamt.t[:, 1:2], sm("lq2"), sm("lk2"), op=ALU.mult), r=[smt, lamt], w=[lamt])
    S.pe(lambda: nc.tensor.matmul(PS[0].t[:, 0:2], lhsT=onesf.t[:], rhs=lamt.t[:, 0:2], start=True, stop=True),
         r=[onesf, lamt], w=[PS[0]])
    S.act(lambda: nc.scalar.activation(out=lamt.t[:, 2:4], in_=PS[0].t[:, 0:2], func=AF.Exp), r=[PS[0]], w=[lamt])
    S.dve(lambda: nc.vector.tensor_tensor(lamt.t[:, 4:5], lamt.t[:, 2:3], lamt.t[:, 3:4], op=ALU.subtract),
          r=[lamt], w=[lamt])
    S.dve(lambda: nc.vector.tensor_scalar(lamt.t[:, 5:6], lamt.t[:, 4:5], lam_init, -1.0, op0=ALU.add, op1=ALU.mult),
          r=[lamt], w=[lamt])
    S.dve(lambda: nc.vector.tensor_scalar(lamt.t[:, 6:8], sm("slg"), 1.0 - lam_init, None, op0=ALU.mult),
          r=[smt, lamt], w=[lamt])
    S.dve(lambda: nc.vector.tensor_scalar(gsc.t[:, 0:1], sm("qg"), SC, None, op0=ALU.mult), r=[smt], w=[gsc])
    S.dve(lambda: nc.vector.tensor_copy(gsc.t[:, 1:2], sm("kg")), r=[smt, gsc], w=[gsc])
    for l in range(L):
        for i, (wi, mk) in enumerate([(0, "nhl"), (2, "nhr")]):
            o = (l * 4 + i) * KF
            wv = sm("fcw", (l * 3 + wi) * KF, (l * 3 + wi + 1) * KF)
            S.dve(lambda o=o, wv=wv, mk=mk: nc.vector.tensor_scalar(
                fixw.t[:, o:o + KF], wv, sm(mk), -1.0, op0=ALU.mult, op1=ALU.mult), r=[smt, fixw], w=[fixw])
            o2 = (l * 4 + 2 + i) * KF
            S.dve(lambda o2=o2, wv=wv: nc.vector.tensor_scalar(
                fixw.t[:, o2:o2 + KF], wv, -1.0, None, op0=ALU.mult), r=[smt, fixw], w=[fixw])

    if STOP == 4:
        return nc, W
    S.barrier()
    es0.close()
    es0 = ExitStack()
    tabs = [S.sb("tab%d" % i, [128, 8, 1536], BF16, es0) for i in range(3)]
    tmpb = S.sb("tmpb", [128, 8, 1536], BF16, es0)
    tstage = S.sb("tstage", [128, 8, 1536], F32, es0)

    def rowvalid(var, t, kr2, qr4):
        krel = -4 + 2 * t + kr2
        if var == 0:
            return 0 <= krel <= 7
        if var == 2:
            return -4 <= krel <= 3
        return -4 <= krel - qr4 <= 3

    for var in range(3):
        tb = tstage
        S.dve(lambda tb=tb: nc.vector.memset(tb.t[:], NEG), w=[tb])
        for t in range(6):
            for kr2 in range(2):
                q = 0
                while q < 4:
                    if not rowvalid(var, t, kr2, q):
                        q += 1
                        continue
                    q1 = q
                    while q1 + 1 < 4 and rowvalid(var, t, kr2, q1 + 1):
                        q1 += 1
                    nd0 = q + 11 - 2 * t - kr2
                    n = q1 - q + 1
                    for h in range(8):
                        src = braw_d.ap()[h, nd0:nd0 + n].rearrange("a k q -> k a q")
                        dst = tb.t[kr2 * 64:(kr2 + 1) * 64, h, t * 256 + q * 64: t * 256 + (q1 + 1) * 64]
                        dst = dst.rearrange("k (a q) -> k a q", a=n)
                        S.dma("sp", dst, src, tb, w=[tb])
                    q = q1 + 1
        for h in range(8):
            S.dve(lambda h=h, var=var: nc.vector.tensor_copy(tabs[var].t[:, h, :], tstage.t[:, h, :]),
                  r=[tstage], w=[tabs[var]])
    for vi, (src_t, mk, nmk) in enumerate([(tabs[0], "istop", "ntop"), (None, None, None), (tabs[2], "isbot", "nbot")]):
        for h in range(8):
            if src_t is None:
                S.dma("sp", NAtab[vi, h], tabs[1].t[:, h, :], tabs[1], r=[tabs[1]])
                continue
            S.dve(lambda h=h, nmk=nmk: nc.vector.tensor_scalar(
                tmpb.t[:, h, :], tabs[1].t[:, h, :], sm(nmk), None, op0=ALU.mult), r=[tabs[1], smt], w=[tmpb])
            S.dve(lambda h=h, mk=mk, src_t=src_t: nc.vector.scalar_tensor_tensor(
                tmpb.t[:, h, :], src_t.t[:, h, :], sm(mk), tmpb.t[:, h, :], op0=ALU.mult, op1=ALU.add),
                r=[src_t, smt, tmpb], w=[tmpb])
            S.dma("sp", NAtab[vi, h], tmpb.t[:, h, :], tmpb, r=[tmpb])
    S.barrier()
    es0.close()
    if STOP == 5:
        return nc, W

    epst = S.sb("epst", [128, 1], F32)
    S.dve(lambda: nc.vector.memset(epst.t[:], EPS), w=[epst])
    hxg = S.sb("hxg", [128, 4, HXW], F32)
    hxo = S.sb("hxo", [128, HXW], F32)
    hxs = S.sb("hxs", [128, HXW], F32)
    sq = S.sb("sq", [128, 512], F32)
    sqb = S.sb("sqb", [128, 512], BF16)
    rstd = S.sb("rstd", [128, 512], F32)
    tmp = S.sb("tmp", [128, 512], F32)
    tmp2 = S.sb("tmp2", [128, 512], F32)
    qn = S.sb("qn", [128, 512], F32)
    qb = S.sb("qb", [128, 512], BF16)
    stg = [S.sb("stg%d" % i, [128, 512], BF16) for i in range(3)]
    stf = [S.sb("stf%d" % i, [128, 512], F32) for i in range(2)]
    cnt = {"stg": 0, "stf": 0, "ps": 0}

    def nxt(lst, key):
        cnt[key] += 1
        return lst[cnt[key] % len(lst)]

    def XTv(i):
        return XT[i].ap().rearrange("k p t -> p k t")

    def transpose_in2(src_tm, ntok, dstv, col0, xin, xo):
        for j in range(0, ntok, 512):
            w = min(512, ntok - j)
            ns = w // 128
            for s in range(ns):
                xi = xin[s]
                S.dma("sp", xi.t[:], src_tm[j + s * 128: j + (s + 1) * 128, :], xi, w=[xi])
            o = xo[(j // 512) % len(xo)]
            for k in range(KD):
                ps = PS[k % 6]
                for s in range(ns):
                    xi = xin[s]
                    S.pe(lambda xi=xi, k=k, ps=ps, s=s: nc.tensor.transpose(
                        ps.t[:, s * 128:(s + 1) * 128], xi.t[:, k * 128:(k + 1) * 128], ident),
                        r=[xi, cst], w=[ps])
                if k % 2 == 0:
                    S.dve(lambda o=o, k=k, ps=ps, w=w: nc.vector.tensor_copy(o.t[:, k, :w], ps.t[:, :w]),
                          r=[ps], w=[o])
                else:
                    S.act(lambda o=o, k=k, ps=ps, w=w: nc.scalar.copy(o.t[:, k, :w], ps.t[:, :w]),
                          r=[ps], w=[o])
            S.dma("sp", dstv[:, :, col0 + j: col0 + j + w], o.t[:, :, :w], o, r=[o])

    def norm_mod(xt, wdt, scale_fn, bias_fn, ht, c0=0):
        ps = PS[7]
        for k in range(KD):
            S.act(lambda k=k: nc.scalar.activation(out=sqb.t[:, :wdt], in_=xt.t[:, k, c0:c0 + wdt], func=AF.Square),
                  r=[xt], w=[sqb])
            S.pe(lambda k=k: nc.tensor.matmul(ps.t[:, :wdt], lhsT=onesb.t[:], rhs=sqb.t[:, :wdt],
                                              start=(k == 0), stop=(k == KD - 1)), r=[sqb, onesb], w=[ps])
        S.act(lambda: nc.scalar.activation(out=rstd.t[:, :wdt], in_=ps.t[:, :wdt], func=AF.Sqrt,
                                           scale=1.0 / D, bias=epst.t[:, 0:1]), r=[ps, epst], w=[rstd])
        S.dve(lambda: nc.vector.reciprocal(rstd.t[:, :wdt], rstd.t[:, :wdt]), r=[rstd], w=[rstd])
        for k in range(KD):
            S.dve(lambda k=k: nc.vector.tensor_tensor(tmp.t[:, :wdt], xt.t[:, k, c0:c0 + wdt], rstd.t[:, :wdt],
                                                      op=ALU.mult), r=[xt, rstd], w=[tmp])
            if bias_fn is None:
                S.act(lambda k=k: nc.scalar.activation(out=ht.t[:, k, :wdt], in_=tmp.t[:, :wdt], func=AF.Identity,
                                                       scale=scale_fn(k)), r=[tmp, drv, smt], w=[ht])
            else:
                S.act(lambda k=k: nc.scalar.activation(out=ht.t[:, k, :wdt], in_=tmp.t[:, :wdt], func=AF.Identity,
                                                       scale=scale_fn(k), bias=bias_fn(k)), r=[tmp, drv, smt], w=[ht])

    def halo_exchange(n, put):
        S.dma("sp", hx_src[:, 0:2 * n], hxs.t[:, 0:2 * n], hxs, r=[hxs])
        if hxs.dsem is not None:
            nc.gpsimd.wait_ge(hxs.dsem, hxs.dcnt)
        S.coll("AllGather", ALU.bypass, GRP, hx_src.ap().opt(), hx_all.ap().opt(), collb)
        S.dma("sp", hxg.t[:], hx_all.ap().rearrange("(r p) c -> p r c", p=128), hxg, r=[collb], w=[hxg])
        for side, mk, off in ((0, "mL", n), (1, "mR", 0)):
            for r in range(4):
                if r == 0:
                    S.dve(lambda side=side, mk=mk, off=off, r=r: nc.vector.tensor_scalar(
                        hxo.t[:, side * n:(side + 1) * n], hxg.t[:, r, off:off + n], sm(mk, r), None, op0=ALU.mult),
                        r=[hxg, smt, hxo], w=[hxo])
                else:
                    S.dve(lambda side=side, mk=mk, off=off, r=r: nc.vector.scalar_tensor_tensor(
                        hxo.t[:, side * n:(side + 1) * n], hxg.t[:, r, off:off + n], sm(mk, r),
                        hxo.t[:, side * n:(side + 1) * n], op0=ALU.mult, op1=ALU.add),
                        r=[hxg, smt, hxo], w=[hxo])
        put(hxo.t[:, 0:n].rearrange("p (k o) -> p k o", o=1), hxo.t[:, n:2 * n].rearrange("p (k o) -> p k o", o=1))

    def qk_post(ps, w, gain, do_norm, rope, scale, outb):
        aux = PS[6]
        if do_norm:
            S.act(lambda: nc.scalar.activation(out=sqb.t[:, :w], in_=ps.t[:, :w], func=AF.Square), r=[ps], w=[sqb])
            S.pe(lambda: nc.tensor.matmul(aux.t[:, :w], lhsT=onesb.t[:], rhs=sqb.t[:, :w], start=True, stop=True),
                 r=[sqb, onesb], w=[aux])
            S.act(lambda: nc.scalar.activation(out=rstd.t[:, :w], in_=aux.t[:, :w], func=AF.Sqrt, scale=1.0 / 128,
                                               bias=epst.t[:, 0:1]), r=[aux, epst], w=[rstd])
            S.dve(lambda: nc.vector.reciprocal(rstd.t[:, :w], rstd.t[:, :w]), r=[rstd], w=[rstd])
            dst = qn if rope is not None else outb
            S.dve(lambda: nc.vector.scalar_tensor_tensor(dst.t[:, :w], ps.t[:, :w], gain, rstd.t[:, :w],
                                                         op0=ALU.mult, op1=ALU.mult), r=[ps, rstd, gsc], w=[dst])
        else:
            dst = qn if rope is not None else outb
            S.act(lambda: nc.scalar.activation(out=dst.t[:, :w], in_=ps.t[:, :w], func=AF.Copy, scale=scale),
                  r=[ps], w=[dst])
        if rope is not None:
            cs, sn = rope
            S.act(lambda: nc.scalar.copy(qb.t[:, :w], qn.t[:, :w]), r=[qn], w=[qb])
            S.pe(lambda: nc.tensor.matmul(aux.t[:, :w], lhsT=Rmb, rhs=qb.t[:, :w], start=True, stop=True),
                 r=[qb, cstb], w=[aux])
            S.dve(lambda: nc.vector.tensor_tensor(tmp.t[:, :w], qn.t[:, :w], cs.t[:, :w], op=ALU.mult),
                  r=[qn, cs], w=[tmp])
            S.dve(lambda: nc.vector.tensor_tensor(tmp2.t[:, :w], aux.t[:, :w], sn.t[:, :w], op=ALU.mult),
                  r=[aux, sn], w=[tmp2])
            S.dve(lambda: nc.vector.tensor_tensor(outb.t[:, :w], tmp.t[:, :w], tmp2.t[:, :w], op=ALU.add),
                  r=[tmp, tmp2], w=[outb])

    def attn(Qs, qdeps, Wq, keys, ndv, Ob, Lb, Sb, PTs):
        ncomp = len(Qs)
        N = len(keys)
        LA = 2
        for n in range(N + LA):
            if n < N:
                kt = keys[n]
                if kt.get("pre") is not None:
                    kt["pre"]()
                sbk = Sb[n % len(Sb)]
                pt = PTs[n % len(PTs)]
                for c in range(ncomp):
                    S.pe(lambda kt=kt, c=c, sbk=sbk: nc.tensor.matmul(
                        sbk.t[:, c * Wq:(c + 1) * Wq], lhsT=kt["K"][c], rhs=Qs[c], start=True,
                        stop=(kt["bias"] is None)), r=list(kt["deps"]) + list(qdeps), w=[sbk])
                    if kt["bias"] is not None:
                        S.pe(lambda kt=kt, c=c, sbk=sbk: nc.tensor.matmul(
                            sbk.t[:, c * Wq:(c + 1) * Wq], lhsT=identb, rhs=kt["bias"], start=False, stop=True),
                            r=list(kt["deps"]) + [cstb], w=[sbk])
                S.act(lambda sbk=sbk, pt=pt: nc.scalar.activation(out=pt.t[:, :ncomp * Wq], in_=sbk.t[:, :ncomp * Wq],
                                                                  func=AF.Exp), r=[sbk], w=[pt])
            if n >= LA:
                m = n - LA
                kt = keys[m]
                pt = PTs[m % len(PTs)]
                for c in range(ncomp):
                    for dv in range(ndv):
                        S.pe(lambda kt=kt, c=c, dv=dv, pt=pt, m=m: nc.tensor.matmul(
                            Ob[c][dv].t[:, 0:Wq], lhsT=kt["V"][dv], rhs=pt.t[:, c * Wq:(c + 1) * Wq],
                            start=(m == 0), stop=(m == N - 1)), r=list(kt["deps"]) + [pt], w=[Ob[c][dv]])
                    S.pe(lambda c=c, pt=pt, m=m: nc.tensor.matmul(
                        Lb[c].t[:, 0:Wq], lhsT=onesb.t[:], rhs=pt.t[:, c * Wq:(c + 1) * Wq],
                        start=(m == 0), stop=(m == N - 1)), r=[pt, onesb], w=[Lb[c]])

    def load_w(wm, wt, g):
        S.dma("sp", wt.t[:], wm.src_ap(g), wt, w=[wt])

    esx = ExitStack()
    xin = [S.sb("xin%d" % i, [128, D], F32, esx) for i in range(4)]
    xo = [S.sb("xo%d" % i, [128, KD, 512], F32, esx) for i in range(2)]
    if not os.environ.get("KNOTR"):
        transpose_in2(x_tm, T, XTv(0), LAT0, xin, xo)
        transpose_in2(ctx_tm, 256, XTv(0), CTX0, xin, xo)
        transpose_in2(xh_tm, 512, XH.ap().rearrange("k p t -> p k t"), 0, xin, xo)
    zt = S.sb("zt", [128, KD, 1], F32, esx)
    S.dve(lambda: nc.vector.memset(zt.t[:], 0.0), w=[zt])
    if not os.environ.get("KNOZT"):
        S.dma("sp", XTv(0)[:, :, CTX0 - 1:CTX0], zt.t[:], zt, r=[zt])
        S.dma("sp", XTv(0)[:, :, CTX0 + 256:CTX0 + 257], zt.t[:], zt, r=[zt])
    S.barrier()
    esx.close()
    if STOP == 6:
        return nc, W

    def inproj(l, cur):
        es = ExitStack()
        wm = W["in%d" % l]
        xt = [S.sb("ip_xt%d" % i, [128, KD, 512], F32, es) for i in range(2)]
        ht = S.sb("ip_ht", [128, KD, 512], BF16, es)
        wts = [S.sb("ip_w%d" % i, wm.tile_shape(), BF16, es) for i in range(2)]
        cs = S.sb("ip_cos", [128, 512], F32, es)
        sn = S.sb("ip_sin", [128, 512], F32, es)
        ub = S.sb("ip_ub", [128, max(KC, 1), 512], F32, es) if l == 1 else None
        tiles = [("lat", j) for j in range(NT)] + [("ctx", 0)] + ([("halo", 0)] if l == 0 else [])
        wi = 0
        for ti, (kind, j) in enumerate(tiles):
            w = 512 if kind != "ctx" else 256
            x = xt[ti % 2]
            if kind == "lat":
                S.dma("sp", x.t[:, :, :w], XTv(cur)[:, :, LAT0 + j * 512: LAT0 + j * 512 + w], x, w=[x])
                S.dma("sp", cs.t[:], cos_d[:, j * 512:(j + 1) * 512], cs, w=[cs])
                S.dma("sp", sn.t[:], sin_d[:, j * 512:(j + 1) * 512], sn, w=[sn])
            elif kind == "ctx":
                S.dma("sp", x.t[:, :, :w], XTv(cur)[:, :, CTX0: CTX0 + w], x, w=[x])
            else:
                S.dma("sp", x.t[:, :, :w], XH.ap().rearrange("k p t -> p k t"), x, w=[x])
            tt = 1 if kind == "ctx" else 0
            norm_mod(x, w, lambda k: DRV(l, tt, 0, k), lambda k: DRV(l, tt, 1, k), ht)
            if l == 0:
                if kind == "halo":
                    groups = list(range(8, 16))
                else:
                    groups = list(range(18))
            else:
                if kind == "ctx":
                    groups = list(range((3 * KC + 8) // 2, (3 * KC + 24) // 2))
                else:
                    groups = list(range(wm.G))
            for g in groups:
                wt = wts[wi % 2]
                wi += 1
                load_w(wm, wt, g)
                c0 = 2 * g
                kinds = [chunk_kind(l, c0), chunk_kind(l, c0 + 1)]
                if kinds[0][0] in ("av", "bv", "dv"):
                    for s in range(w // 128):
                        ps = nxt(PS[0:6], "ps")
                        for k in range(KD):
                            S.pe(lambda k=k, s=s, ps=ps, wt=wt: nc.tensor.matmul(
                                ps.t[:, 0:256], lhsT=ht.t[:, k, s * 128:(s + 1) * 128], rhs=wm.lhs(wt, k, 0, 256),
                                start=(k == 0), stop=(k == KD - 1)), r=[ht, wt], w=[ps])
                        ob = nxt(stg, "stg")
                        S.act(lambda ob=ob, ps=ps: nc.scalar.copy(ob.t[:, 0:256], ps.t[:, 0:256]), r=[ps], w=[ob])
                        kk, idx = kinds[0]
                        if kk == "av":
                            src = ob.t[:, 0:256].rearrange("p (h d) -> p h d", h=2)
                            if kind == "lat":
                                dst = Va[:, idx:idx + 2, 2 + j * 4 + s, :]
                            elif kind == "halo":
                                dst = Va[:, idx:idx + 2, (s if s < 2 else NKE - 4 + s), :]
                            else:
                                dst = Vac[:, idx:idx + 2, s, :]
                        elif kk == "bv":
                            src = ob.t[:, 0:256].rearrange("p (h d) -> p h d", h=2)
                            if kind == "lat":
                                S.dma("sp", Vb_src[0].ap().rearrange("p (k d) -> p k d", d=128)[:, j * 4 + s, :],
                                      ob.t[:, 0:128], ob, r=[ob])
                                S.dma("sp", Vb_src[1].ap().rearrange("p (k d) -> p k d", d=128)[:, j * 4 + s, :],
                                      ob.t[:, 128:256], ob, r=[ob])
                                continue
                            else:
                                dst = Vbc[:, :, s, :]
                        else:
                            src = ob.t[:, 0:256]
                            if kind == "lat":
                                kt_ = j * 4 + s
                                dst = Vd_src[idx * 2 + kt_ // NKH].ap().rearrange("p (k d) -> p k d", d=256)[:, kt_ % NKH, :]
                            else:
                                dst = Vdc[:, idx, s, :]
                        S.dma("sp", dst, src, ob, r=[ob])
                    continue
                for ci in range(2):
                    kk, idx = kinds[ci]
                    ps = nxt(PS[0:6], "ps")
                    for k in range(KD):
                        S.pe(lambda k=k, ps=ps, wt=wt, ci=ci: nc.tensor.matmul(
                            ps.t[:, :w], lhsT=wm.lhs(wt, k, ci * 128, (ci + 1) * 128), rhs=ht.t[:, k, :w],
                            start=(k == 0), stop=(k == KD - 1)), r=[ht, wt], w=[ps])
                    rope = (cs, sn) if kind == "lat" else None
                    if kk == "u":
                        S.act(lambda ps=ps, idx=idx: nc.scalar.copy(ub.t[:, idx, :w], ps.t[:, :w]), r=[ps], w=[ub])
                        continue
                    if kk == "gb":
                        of = nxt(stf, "stf")
                        S.act(lambda ps=ps, of=of: nc.scalar.copy(of.t[:, :w], ps.t[:, :w]), r=[ps], w=[of])
                        S.dma("sp", GB[idx, :, j * 512: j * 512 + w], of.t[:, :w], of, r=[of])
                        continue
                    if kk == "gc":
                        of = nxt(stf, "stf")
                        S.dve(lambda ps=ps, of=of, idx=idx: nc.vector.tensor_tensor(
                            of.t[:, :w], ps.t[:, :w], ub.t[:, idx, :w], op=ALU.mult), r=[ps, ub], w=[of])
                        S.dma("sp", GCU[idx, :, 1 + j * 512: 1 + j * 512 + w], of.t[:, :w], of, r=[of])
                        if j == 0:
                            S.dve(lambda of=of, idx=idx: nc.vector.tensor_copy(hxs.t[:, idx:idx + 1], of.t[:, 0:1]),
                                  r=[of, hxs], w=[hxs])
                        if j == NT - 1:
                            S.dve(lambda of=of, idx=idx: nc.vector.tensor_copy(
                                hxs.t[:, KC + idx:KC + idx + 1], of.t[:, w - 1:w]), r=[of, hxs], w=[hxs])
                        continue
                    ob = nxt(stg, "stg")
                    if kk == "aq":
                        qk_post(ps, w, None, False, None, SC, ob)
                        dst = QT0[idx, :, (j * 512 if kind == "lat" else T): (j * 512 if kind == "lat" else T) + w]
                    elif kk == "bq":
                        qk_post(ps, w, gsc.t[:, 0:1], True, rope, None, ob)
                        dst = QT0[8 + idx, :, (j * 512 if kind == "lat" else T): (j * 512 if kind == "lat" else T) + w]
                    elif kk == "ak":
                        qk_post(ps, w, None, False, None, 1.0, ob)
                        if kind == "lat":
                            dst = KaT[idx, :, 256 + j * 512: 256 + j * 512 + w]
                        elif kind == "ctx":
                            dst = KaTc[idx, :, :]
                        else:
                            S.dma("sp", KaT[idx, :, 0:256], ob.t[:, 0:256], ob, r=[ob])
                            dst = None
                            S.dma("sp", KaT[idx, :, T + 256: T + 512], ob.t[:, 256:512], ob, r=[ob])
                    elif kk == "bk":
                        qk_post(ps, w, gsc.t[:, 1:2], True, rope, None, ob)
                        dst = KbT_src[idx][:, j * 512: j * 512 + w] if kind == "lat" else KbTc[idx, :, :]
                    elif kk == "dq":
                        qk_post(ps, w, None, False, rope, SC, ob)
                        dst = QT1[idx, :, j * 512: j * 512 + w]
                    elif kk == "dk":
                        qk_post(ps, w, None, False, rope, 1.0, ob)
                        dst = KdT_src[idx][:, j * 512: j * 512 + w] if kind == "lat" else KdTc[idx, :, :]
                    if dst is not None:
                        S.dma("sp", dst, ob.t[:, :w], ob, r=[ob])
        S.barrier()
        es.close()

    def chunk_kind(l, c):
        if l == 0:
            if c < 8:
                return ("aq", c)
            if c < 16:
                return ("bq", c - 8)
            if c < 24:
                return ("ak", c - 16)
            if c < 32:
                return ("av", c - 24)
            if c < 34:
                return ("bk", c - 32)
            return ("bv", c - 34)
        if c < KC:
            return ("u", c)
        if c < 2 * KC:
            return ("gb", c - KC)
        if c < 3 * KC:
            return ("gc", c - 2 * KC)
        c -= 3 * KC
        if c < 8:
            return ("dq", c)
        if c < 16:
            return ("dk", c - 8)
        return ("dv", (c - 16) // 2)

    def outproj(es, l, cur, OT, nk, w, col0, tt, first, last, xr, wts, wctr):
        wm = W["out%d" % l]
        for g in range(wm.G):
            wt = wts[wctr[0] % len(wts)]
            wctr[0] += 1
            load_w(wm, wt, g)
            nm = wm.gw // 128
            x = xr[g % len(xr)]
            S.dma("sp", x.t[:, :nm, :w], XTv(cur)[:, g * nm:(g + 1) * nm, col0: col0 + w], x, w=[x])
            for mi in range(nm):
                m = g * nm + mi
                ps = nxt(PS[0:4], "ps")
                for k in range(nk):
                    S.pe(lambda k=k, ps=ps, wt=wt, mi=mi: nc.tensor.matmul(
                        ps.t[:, :w], lhsT=wm.lhs(wt, k, mi * 128, (mi + 1) * 128), rhs=OT.t[:, k, :w],
                        start=(k == 0), stop=(k == nk - 1)), r=[OT, wt], w=[ps])
                S.dve(lambda ps=ps, x=x, mi=mi, m=m: nc.vector.scalar_tensor_tensor(
                    x.t[:, mi, :w], ps.t[:, :w], DRV(l, tt, 2, m), x.t[:, mi, :w], op0=ALU.mult, op1=ALU.add),
                    r=[ps, x, drv], w=[x])
                if first:
                    S.dve(lambda x=x, mi=mi, m=m: nc.vector.tensor_copy(hxs.t[:, m:m + 1], x.t[:, mi, 0:1]),
                          r=[x, hxs], w=[hxs])
                if last:
                    S.dve(lambda x=x, mi=mi, m=m: nc.vector.tensor_copy(hxs.t[:, KD + m:KD + m + 1], x.t[:, mi, w - 1:w]),
                          r=[x, hxs], w=[hxs])
            S.dma("sp", XTv(cur)[:, g * nm:(g + 1) * nm, col0: col0 + w], x.t[:, :nm, :w], x, r=[x])

    def put_x_halo(cur):
        def put(left, right):
            S.dma("sp", XTv(cur)[:, :, LAT0 - 1:LAT0], left, hxo, r=[hxo])
            S.dma("sp", XTv(cur)[:, :, LAT0 + T:LAT0 + T + 1], right, hxo, r=[hxo])
        return put

    def mixer0(cur):
        l = 0
        es = ExitStack()
        wm = W["out0"]
        qt = S.sb("m_qt", [128, 16, 512], BF16, es)
        OT = S.sb("m_ot", [128, 16, 512], BF16, es)
        xr = [S.sb("m_xr%d" % i, [128, wm.gw // 128, 512], F32, es) for i in range(2)]
        wts = [S.sb("m_w%d" % i, wm.tile_shape(), BF16, es) for i in range(2)]
        kna = [S.sb("m_kna%d" % i, [128, 1024], BF16, es) for i in range(2)]
        vna = [S.sb("m_vna%d" % i, [128, 8, 128], BF16, es) for i in range(2)]
        tab = [S.sb("m_tab%d" % i, [128, 2, 1536], BF16, es) for i in range(2)]
        kch = [S.sb("m_kch%d" % i, [128, T], BF16, es) for i in range(2)]
        vch = [S.sb("m_vch%d" % i, [128, NKT, 128], BF16, es) for i in range(2)]
        kac = S.sb("m_kac", [128, 8, 256], BF16, es)
        vac = S.sb("m_vac", [128, 8, 2, 128], BF16, es)
        kbc = S.sb("m_kbc", [128, 2, 256], BF16, es)
        vbc = S.sb("m_vbc", [128, 2, 2, 128], BF16, es)
        PTs = [S.sb("m_pt%d" % i, [128, 512], BF16, es) for i in range(4)]
        rinv = S.sb("m_rinv", [128, 512], F32, es)
        S.dma("sp", kac.t[:], KaTc.ap().rearrange("h p t -> p h t"), kac, w=[kac])
        S.dma("sp", vac.t[:], Vac[:, :, :, :], vac, w=[vac])
        S.dma("sp", kbc.t[:], KbTc.ap().rearrange("g p t -> p g t"), kbc, w=[kbc])
        S.dma("sp", vbc.t[:], Vbc[:, :, :, :], vbc, w=[vbc])
        Ob, Lb, Sb = [[PS[4]]], [PS[5]], [PS[0], PS[1], PS[2], PS[3]]
        wctr = [0]
        ci = [0]

        def finish(Wq, dst_ap):
            S.dve(lambda: nc.vector.reciprocal(rinv.t[:, :Wq], Lb[0].t[:, :Wq]), r=[Lb[0]], w=[rinv])
            S.dve(lambda: nc.vector.tensor_tensor(dst_ap, Ob[0][0].t[:, :Wq], rinv.t[:, :Wq], op=ALU.mult),
                  r=[Ob[0][0], rinv], w=[OT])

        def ctx_keys_a(h):
            return [dict(K=[kac.t[:, h, kt * 128:(kt + 1) * 128]], V=[vac.t[:, h, kt, :]], bias=None, deps=[kac, vac])
                    for kt in range(2)]

        def ctx_keys_b(g):
            return [dict(K=[kbc.t[:, g, kt * 128:(kt + 1) * 128]], V=[vbc.t[:, g, kt, :]], bias=None, deps=[kbc, vbc])
                    for kt in range(2)]

        for j in range(NT):
            S.dma("sp", qt.t[:], QT0.ap().rearrange("h p t -> p h t")[:, :, j * 512:(j + 1) * 512], qt, w=[qt])
            for h in range(8):
                kn, vn, tb = kna[h % 2], vna[h % 2], tab[h % 2]
                S.dma("sp", kn.t[:], KaT[h, :, j * 512: j * 512 + 1024], kn, w=[kn])
                S.dma("sp", vn.t[:], Va[:, h, j * 4: j * 4 + 8, :], vn, w=[vn])
                for sub in range(2):
                    i = 2 * j + sub
                    var = 0 if i == 0 else (2 if i == 2 * NT - 1 else 1)
                    S.dma("sp", tb.t[:, sub, :], NAtab[var, h], tb, w=[tb])
                for sub in range(2):
                    keys = ctx_keys_a(h)
                    for t in range(6):
                        kt = 2 * sub + t
                        keys.append(dict(K=[kn.t[:, kt * 128:(kt + 1) * 128]], V=[vn.t[:, kt, :]],
                                         bias=tb.t[:, sub, t * 256:(t + 1) * 256], deps=[kn, vn, tb]))
                    attn([qt.t[:, h, sub * 256:(sub + 1) * 256]], [qt], 256, keys, 1, Ob, Lb, Sb, PTs)
                    finish(256, OT.t[:, h, sub * 256:(sub + 1) * 256])
            for h in range(8):
                g = h // 4
                keys = ctx_keys_b(g)

                def ld(r, g=g):
                    kc_, vc_ = kch[r % 2], vch[r % 2]
                    S.dma("sp", kc_.t[:], KbT_all[g][r * 128:(r + 1) * 128, :], kc_, w=[kc_])
                    S.dma("sp", vc_.t[:], Vb_all[g][r * 128:(r + 1) * 128, :].rearrange("p (k d) -> p k d", d=128),
                          vc_, w=[vc_])
                ld(0)
                for r in range(4):
                    kc_, vc_ = kch[r % 2], vch[r % 2]
                    for kt in range(NKT):
                        keys.append(dict(K=[kc_.t[:, kt * 128:(kt + 1) * 128]], V=[vc_.t[:, kt, :]], bias=None,
                                         deps=[kc_, vc_],
                                         pre=((lambda r=r: ld(r + 1)) if (kt == 3 and r < 3) else None)))
                attn([qt.t[:, 8 + h, :]], [qt], 512, keys, 1, Ob, Lb, Sb, PTs)
                finish(512, OT.t[:, 8 + h, :])
            outproj(es, 0, cur, OT, 16, 512, LAT0 + j * 512, 0, j == 0, j == NT - 1, xr, wts, wctr)
        S.dma("sp", qt.t[:, :, 0:256], QT0.ap().rearrange("h p t -> p h t")[:, :, T:T + 256], qt, w=[qt])
        for h in range(8):
            attn([qt.t[:, h, 0:256]], [qt], 256, ctx_keys_a(h), 1, Ob, Lb, Sb, PTs)
            finish(256, OT.t[:, h, 0:256])
        for h in range(8):
            attn([qt.t[:, 8 + h, 0:256]], [qt], 256, ctx_keys_b(h // 4), 1, Ob, Lb, Sb, PTs)
            finish(256, OT.t[:, 8 + h, 0:256])
        outproj(es, 0, cur, OT, 16, 256, CTX0, 1, False, False, xr, wts, wctr)
        halo_exchange(KD, put_x_halo(cur))
        S.barrier()
        es.close()


    def ffn(l, cur):
        es = ExitStack()
        wg, wu, wd = W["g%d" % l], W["u%d" % l], W["d%d" % l]
        xe = S.sb("f_xe", [128, KD, 512], F32, es)
        h2 = S.sb("f_h2", [128, KD, 512], BF16, es)
        wgt = [S.sb("f_wg%d" % i, wg.tile_shape(), BF16, es) for i in range(2)]
        wut = [S.sb("f_wu%d" % i, wu.tile_shape(), BF16, es) for i in range(2)]
        wdt = [S.sb("f_wd%d" % i, wd.tile_shape(), BF16, es) for i in range(1)]
        actT = S.sb("f_act", [128, KF, 512], BF16, es)
        tb = [S.sb("f_tb%d" % i, [128, 512], F32, es) for i in range(2)]
        sbb = [S.sb("f_sb%d" % i, [128, 512], F32, es) for i in range(2)]
        FT = cfg.FT
        tiles = []
        s0 = 0
        while s0 < T:
            n = min(FT, T - s0)
            tiles.append(("lat", s0, n))
            s0 += n
        if l < L - 1:
            tiles.append(("ctx", 0, 256))
        wi = 0

        def fcw(i, f):
            o = sm_off["fcw"][0] + (l * 3 + i) * KF + f
            return smt.t[:, o:o + 1]

        def fcb(f):
            o = sm_off["fcb"][0] + l * KF + f
            return smt.t[:, o:o + 1]

        def fx(i, f):
            o = (l * 4 + i) * KF + f
            return fixw.t[:, o:o + 1]

        for ti, (kind, s0, n) in enumerate(tiles):
            tt = 0 if kind == "lat" else 1
            c0 = (LAT0 + s0 - 1) if kind == "lat" else (CTX0 - 1)
            cw = n + 2
            S.dma("sp", xe.t[:, :, :cw], XTv(cur)[:, :, c0:c0 + cw], xe, w=[xe])
            norm_mod(xe, cw, lambda k: DRV(l, tt, 3, k), lambda k: DRV(l, tt, 4, k), h2)
            for g in range(wg.G):
                a, b = wgt[wi % 2], wut[wi % 2]
                wi += 1
                load_w(wg, a, g)
                load_w(wu, b, g)
                for ci in range(wg.gw // 128):
                    f = g * (wg.gw // 128) + ci
                    pg = PS[(2 * f) % 6]
                    pu = PS[(2 * f + 1) % 6]
                    for k in range(KD):
                        S.pe(lambda k=k, a=a, pg=pg, ci=ci: nc.tensor.matmul(
                            pg.t[:, :cw], lhsT=wg.lhs(a, k, ci * 128, (ci + 1) * 128), rhs=h2.t[:, k, :cw],
                            start=(k == 0), stop=(k == KD - 1)), r=[a, h2], w=[pg])
                    for k in range(KD):
                        S.pe(lambda k=k, b=b, pu=pu, ci=ci: nc.tensor.matmul(
                            pu.t[:, :cw], lhsT=wu.lhs(b, k, ci * 128, (ci + 1) * 128), rhs=h2.t[:, k, :cw],
                            start=(k == 0), stop=(k == KD - 1)), r=[b, h2], w=[pu])
                    t_ = tb[f % 2]
                    s_ = sbb[f % 2]
                    S.act(lambda pg=pg, t_=t_, f=f: nc.scalar.activation(
                        out=t_.t[:, :n], in_=pg.t[:, 1:n + 1], func=AF.Identity, scale=fcw(1, f), bias=fcb(f)),
                        r=[pg, smt], w=[t_])
                    S.dve(lambda pg=pg, t_=t_, f=f: nc.vector.scalar_tensor_tensor(
                        t_.t[:, :n], pg.t[:, 0:n], fcw(0, f), t_.t[:, :n], op0=ALU.mult, op1=ALU.add),
                        r=[pg, smt, t_], w=[t_])
                    S.dve(lambda pg=pg, t_=t_, f=f: nc.vector.scalar_tensor_tensor(
                        t_.t[:, :n], pg.t[:, 2:n + 2], fcw(2, f), t_.t[:, :n], op0=ALU.mult, op1=ALU.add),
                        r=[pg, smt, t_], w=[t_])
                    fl = (0 if kind == "lat" else 2) if (kind == "ctx" or s0 == 0) else None
                    fr = (1 if kind == "lat" else 3) if (kind == "ctx" or s0 + n == T) else None
                    if fl is not None:
                        S.dve(lambda pg=pg, t_=t_, f=f, fl=fl: nc.vector.scalar_tensor_tensor(
                            t_.t[:, 0:1], pg.t[:, 0:1], fx(fl, f), t_.t[:, 0:1], op0=ALU.mult, op1=ALU.add),
                            r=[pg, fixw, t_], w=[t_])
                    if fr is not None:
                        S.dve(lambda pg=pg, t_=t_, f=f, fr=fr: nc.vector.scalar_tensor_tensor(
                            t_.t[:, n - 1:n], pg.t[:, n + 1:n + 2], fx(fr, f), t_.t[:, n - 1:n], op0=ALU.mult,
                            op1=ALU.add), r=[pg, fixw, t_], w=[t_])
                    S.act(lambda t_=t_, s_=s_: nc.scalar.activation(out=s_.t[:, :n], in_=t_.t[:, :n], func=AF.Silu),
                          r=[t_], w=[s_])
                    S.dve(lambda s_=s_, pu=pu, f=f: nc.vector.tensor_tensor(
                        actT.t[:, f, :n], s_.t[:, :n], pu.t[:, 1:n + 1], op=ALU.mult), r=[s_, pu], w=[actT])
            for g in range(wd.G):
                d_ = wdt[0]
                load_w(wd, d_, g)
                for mi in range(wd.gw // 128):
                    m = g * (wd.gw // 128) + mi
                    ps = PS[m % 6]
                    for f in range(KF):
                        S.pe(lambda f=f, d_=d_, ps=ps, mi=mi: nc.tensor.matmul(
                            ps.t[:, :n], lhsT=wd.lhs(d_, f, mi * 128, (mi + 1) * 128), rhs=actT.t[:, f, :n],
                            start=(f == 0), stop=(f == KF - 1)), r=[d_, actT], w=[ps])
                    S.dve(lambda ps=ps, m=m: nc.vector.scalar_tensor_tensor(
                        xe.t[:, m, 1:n + 1], ps.t[:, :n], DRV(l, tt, 5, m), xe.t[:, m, 1:n + 1], op0=ALU.mult,
                        op1=ALU.add), r=[ps, xe, drv], w=[xe])
            oc = (LAT0 + s0) if kind == "lat" else CTX0
            S.dma("sp", XTv(1 - cur)[:, :, oc:oc + n], xe.t[:, :, 1:n + 1], xe, r=[xe])
        S.barrier()
        es.close()

    def mixer1(cur):
        l = 1
        es = ExitStack()
        wm = W["out1"]
        NK1 = KC + 8
        qt = S.sb("n_qt", [128, 8, 512], BF16, es)
        OT = S.sb("n_ot", [128, NK1, 512], BF16, es)
        xr = [S.sb("n_xr%d" % i, [128, wm.gw // 128, 512], F32, es) for i in range(1)]
        wts = [S.sb("n_w%d" % i, wm.tile_shape(), BF16, es) for i in range(1)]
        gcu = S.sb("n_gcu", [128, KC, 514], F32, es)
        gb = S.sb("n_gb", [128, KC, 512], F32, es)
        kch = [S.sb("n_kch%d" % i, [128, 2, T], BF16, es) for i in range(2)]
        vch = [S.sb("n_vch%d" % i, [128, NKT, 256], BF16, es) for i in range(2)]
        kdc = S.sb("n_kdc", [128, 8, 256], BF16, es)
        vdc = S.sb("n_vdc", [128, 4, 2, 256], BF16, es)
        PTs = [S.sb("n_pt%d" % i, [128, 512], BF16, es) for i in range(4)]
        rinv = S.sb("n_rinv", [128, 512], F32, es)
        o1 = S.sb("n_o1", [128, 512], F32, es)
        o2 = S.sb("n_o2", [128, 512], F32, es)
        S.dma("sp", kdc.t[:], KdTc.ap().rearrange("h p t -> p h t"), kdc, w=[kdc])
        S.dma("sp", vdc.t[:], Vdc[:, :, :, :], vdc, w=[vdc])
        Ob, Lb, Sb = [[PS[2], PS[3]], [PS[4], PS[5]]], [PS[6], PS[7]], [PS[0], PS[1]]
        wctr = [0]

        def scw(i, c):
            o = sm_off["scw"][0] + i * KC + c
            return smt.t[:, o:o + 1]

        for j in range(NT):
            S.dma("sp", qt.t[:], QT1.ap().rearrange("h p t -> p h t")[:, :, j * 512:(j + 1) * 512], qt, w=[qt])
            S.dma("sp", gcu.t[:], GCU.ap().rearrange("c p t -> p c t")[:, :, j * 512: j * 512 + 514], gcu, w=[gcu])
            S.dma("sp", gb.t[:], GB.ap().rearrange("c p t -> p c t")[:, :, j * 512:(j + 1) * 512], gb, w=[gb])
            for c in range(KC):
                S.act(lambda c=c: nc.scalar.activation(out=tmp.t[:, :512], in_=gcu.t[:, c, 1:513], func=AF.Copy,
                                                       scale=1.0), r=[gcu], w=[tmp])
                S.dve(lambda c=c: nc.vector.tensor_scalar(tmp.t[:, :512], tmp.t[:, :512], scw(1, c), None, op0=ALU.mult),
                      r=[tmp, smt], w=[tmp])
                S.dve(lambda c=c: nc.vector.scalar_tensor_tensor(
                    tmp.t[:, :512], gcu.t[:, c, 0:512], scw(0, c), tmp.t[:, :512], op0=ALU.mult, op1=ALU.add),
                    r=[gcu, smt, tmp], w=[tmp])
                S.dve(lambda c=c: nc.vector.scalar_tensor_tensor(
                    tmp.t[:, :512], gcu.t[:, c, 2:514], scw(2, c), tmp.t[:, :512], op0=ALU.mult, op1=ALU.add),
                    r=[gcu, smt, tmp], w=[tmp])
                S.dve(lambda c=c: nc.vector.tensor_tensor(OT.t[:, c, :], tmp.t[:, :512], gb.t[:, c, :], op=ALU.mult),
                      r=[tmp, gb], w=[OT])
            for h in range(4):
                for sub in range(2):
                    qs = [qt.t[:, 2 * h + c, sub * 256:(sub + 1) * 256] for c in range(2)]
                    keys = [dict(K=[kdc.t[:, 2 * h + c, kt * 128:(kt + 1) * 128] for c in range(2)],
                                 V=[vdc.t[:, h, kt, dv * 128:(dv + 1) * 128] for dv in range(2)], bias=None,
                                 deps=[kdc, vdc]) for kt in range(2)]

                    def ld(r, h=h):
                        kc_, vc_ = kch[r % 2], vch[r % 2]
                        for c in range(2):
                            S.dma("sp", kc_.t[:, c, :], KdT_all[2 * h + c][r * 128:(r + 1) * 128, :], kc_, w=[kc_])
                        for hf in range(2):
                            S.dma("sp", vc_.t[:, hf * NKH:(hf + 1) * NKH, :],
                                  Vd_all[2 * h + hf][r * 128:(r + 1) * 128, :].rearrange("p (k d) -> p k d", d=256),
                                  vc_, w=[vc_])
                    ld(0)
                    for r in range(4):
                        kc_, vc_ = kch[r % 2], vch[r % 2]
                        for kt in range(NKT):
                            keys.append(dict(K=[kc_.t[:, c, kt * 128:(kt + 1) * 128] for c in range(2)],
                                             V=[vc_.t[:, kt, dv * 128:(dv + 1) * 128] for dv in range(2)], bias=None,
                                             deps=[kc_, vc_],
                                             pre=((lambda r=r: ld(r + 1)) if (kt == 3 and r < 3) else None)))
                    attn(qs, [qt], 256, keys, 2, Ob, Lb, Sb, PTs)
                    for c in range(2):
                        S.dve(lambda c=c: nc.vector.reciprocal(rinv.t[:, c * 256:(c + 1) * 256], Lb[c].t[:, 0:256]),
                              r=[Lb[c], rinv], w=[rinv])
                    for dv in range(2):
                        S.dve(lambda dv=dv: nc.vector.tensor_tensor(
                            o1.t[:, dv * 256:(dv + 1) * 256], Ob[0][dv].t[:, 0:256], rinv.t[:, 0:256],
                            op=ALU.mult), r=[Ob[0][dv], rinv, o1], w=[o1])
                        S.dve(lambda dv=dv: nc.vector.tensor_tensor(
                            o2.t[:, dv * 256:(dv + 1) * 256], Ob[1][dv].t[:, 0:256], rinv.t[:, 256:512],
                            op=ALU.mult), r=[Ob[1][dv], rinv, o2], w=[o2])
                    S.dve(lambda: nc.vector.scalar_tensor_tensor(
                        o1.t[:, :512], o2.t[:, :512], lamt.t[:, 5:6], o1.t[:, :512], op0=ALU.mult, op1=ALU.add),
                        r=[o1, o2, lamt], w=[o1])
                    S.act(lambda: nc.scalar.activation(out=sqb.t[:, :512], in_=o1.t[:, :512], func=AF.Square),
                          r=[o1], w=[sqb])
                    aux = PS[0]
                    for dv in range(2):
                        S.pe(lambda dv=dv: nc.tensor.matmul(aux.t[:, :256], lhsT=onesb.t[:],
                                                            rhs=sqb.t[:, dv * 256:(dv + 1) * 256],
                                                            start=(dv == 0), stop=(dv == 1)), r=[sqb, onesb], w=[aux])
                    S.act(lambda: nc.scalar.activation(out=rstd.t[:, :256], in_=aux.t[:, :256], func=AF.Sqrt,
                                                       scale=1.0 / 256, bias=epst.t[:, 0:1]), r=[aux, epst], w=[rstd])
                    S.dve(lambda: nc.vector.reciprocal(rstd.t[:, :256], rstd.t[:, :256]), r=[rstd], w=[rstd])
                    for dv in range(2):
                        S.dve(lambda dv=dv: nc.vector.tensor_tensor(
                            tmp2.t[:, :256], o1.t[:, dv * 256:(dv + 1) * 256], rstd.t[:, :256], op=ALU.mult),
                            r=[o1, rstd], w=[tmp2])
                        S.act(lambda dv=dv, h=h, sub=sub: nc.scalar.activation(
                            out=OT.t[:, KC + 2 * h + dv, sub * 256:(sub + 1) * 256], in_=tmp2.t[:, :256],
                            func=AF.Identity, scale=lamt.t[:, 6 + dv:7 + dv]), r=[tmp2, lamt], w=[OT])
            outproj(es, 1, cur, OT, NK1, 512, LAT0 + j * 512, 0, j == 0, j == NT - 1, xr, wts, wctr)
        halo_exchange(KD, put_x_halo(cur))
        S.barrier()
        es.close()

    def final(cur, nonorm=False):
        es = ExitStack()
        xt = [S.sb("o_xt%d" % i, [128, KD, 512], F32, es) for i in range(2)]
        yT = S.sb("o_y", [128, KD, 512], F32, es)
        ot = [S.sb("o_ot%d" % i, [128, D], F32, es) for i in range(2)]
        oi = 0
        for j in range(NT):
            x = xt[j % 2]
            S.dma("sp", x.t[:], XTv(cur)[:, :, LAT0 + j * 512: LAT0 + (j + 1) * 512], x, w=[x])
            ps = PS[7]
            for k in range(KD):
                S.act(lambda k=k, x=x: nc.scalar.activation(out=sq.t[:, :512], in_=x.t[:, k, :], func=AF.Square),
                      r=[x], w=[sq])
                S.pe(lambda k=k: nc.tensor.matmul(ps.t[:, :512], lhsT=onesf.t[:], rhs=sq.t[:, :512],
                                                  start=(k == 0), stop=(k == KD - 1)), r=[sq, onesf], w=[ps])
            S.act(lambda: nc.scalar.activation(out=rstd.t[:, :512], in_=ps.t[:, :512], func=AF.Sqrt, scale=1.0 / D,
                                               bias=epst.t[:, 0:1]), r=[ps, epst], w=[rstd])
            S.dve(lambda: nc.vector.reciprocal(rstd.t[:, :512], rstd.t[:, :512]), r=[rstd], w=[rstd])
            for k in range(KD):
                if nonorm:
                    S.dve(lambda k=k, x=x: nc.vector.tensor_copy(yT.t[:, k, :], x.t[:, k, :]), r=[x], w=[yT])
                    continue
                S.dve(lambda k=k, x=x: nc.vector.scalar_tensor_tensor(
                    yT.t[:, k, :], x.t[:, k, :], sm("fg", k), rstd.t[:, :512], op0=ALU.mult, op1=ALU.mult),
                    r=[x, rstd, smt], w=[yT])
            for s in range(4):
                o = ot[oi % 2]
                oi += 1
                for k4 in range(0, KD, 4):
                    pb = PS[(k4 // 4) % 6]
                    for kk in range(min(4, KD - k4)):
                        k = k4 + kk
                        S.pe(lambda k=k, kk=kk, pb=pb, s=s: nc.tensor.transpose(
                            pb.t[:, kk * 128:(kk + 1) * 128], yT.t[:, k, s * 128:(s + 1) * 128], ident),
                            r=[yT, cst], w=[pb])
                    wd_ = min(4, KD - k4) * 128
                    if (k4 // 4) % 2 == 0:
                        S.dve(lambda k4=k4, pb=pb, o=o, wd_=wd_: nc.vector.tensor_copy(
                            o.t[:, k4 * 128:k4 * 128 + wd_], pb.t[:, :wd_]), r=[pb], w=[o])
                    else:
                        S.act(lambda k4=k4, pb=pb, o=o, wd_=wd_: nc.scalar.copy(
                            o.t[:, k4 * 128:k4 * 128 + wd_], pb.t[:, :wd_]), r=[pb], w=[o])
                S.dma("sp", out_tm[j * 512 + s * 128: j * 512 + (s + 1) * 128, :], o.t[:], o, r=[o])
        S.barrier()
        es.close()

    def gather_kv0():
        for i in range(2):
            S.coll("AllGather", ALU.bypass, GRP, KbT_src[i].ap().opt(), KbT_all[i].ap().opt(), collb)
            S.coll("AllGather", ALU.bypass, GRP, Vb_src[i].ap().opt(), Vb_all[i].ap().opt(), collb)
        S.barrier()

    def gather_kv1():
        for i in range(8):
            S.coll("AllGather", ALU.bypass, GRP, KdT_src[i].ap().opt(), KdT_all[i].ap().opt(), collb)
            S.coll("AllGather", ALU.bypass, GRP, Vd_src[i].ap().opt(), Vd_all[i].ap().opt(), collb)

        def put(left, right):
            S.dma("sp", GCU.ap().rearrange("c p t -> p c t")[:, :, 0:1], left, hxo, r=[hxo])
            S.dma("sp", GCU.ap().rearrange("c p t -> p c t")[:, :, T + 1:T + 2], right, hxo, r=[hxo])
        halo_exchange(KC, put)
        S.barrier()

    stop = cfg.stop if hasattr(cfg, "stop") else 99
    inproj(0, 0)
    if STOP == 7:
        return nc, W
    gather_kv0()
    if STOP == 8:
        return nc, W
    mixer0(0)
    if STOP == 9:
        if DBG:
            final(0, nonorm=True)
        return nc, W
    ffn(0, 0)
    if STOP == 10:
        if DBG:
            final(1, nonorm=True)
        return nc, W
    inproj(1, 1)
    if STOP == 11:
        return nc, W
    gather_kv1()
    if STOP == 12:
        return nc, W
    mixer1(1)
    if STOP == 13:
        if DBG:
            final(1, nonorm=True)
        return nc, W
    ffn(1, 1)
    if STOP == 14:
        if DBG:
            final(0, nonorm=True)
        return nc, W
    final(0)
    return nc, W


def small_layout(cfg):
    KD, KF, KC, L, NCH6 = cfg.KD, cfg.KF, cfg.KC, cfg.L, cfg.NCH6
    items = [("cv", KD * 2), ("m8", 8), ("mb", 2), ("mL", 4), ("mR", 4), ("nhl", 1), ("nhr", 1),
             ("istop", 1), ("ntop", 1), ("isbot", 1), ("nbot", 1),
             ("adab", L * NCH6), ("ng", L * 2 * KD), ("fg", KD), ("qg", 1), ("kg", 1),
             ("lq1", 1), ("lk1", 1), ("lq2", 1), ("lk2", 1), ("slg", 2), ("scw", 3 * KC),
             ("fcw", L * 3 * KF), ("fcb", L * KF)]
    off, o = {}, 0
    for n, w in items:
        off[n] = (o, w)
        o += w
    return off, o


def _pl(v, nch):
    return np.ascontiguousarray(np.asarray(v, np.float32).reshape(nch, 128).T)


def host_inputs(cfg, inp):
    D, DFF, T, KD, KF, KC, L, NCH6 = cfg.D, cfg.DFF, cfg.T, cfg.KD, cfg.KF, cfg.KC, cfg.L, cfg.NCH6
    f = lambda a: np.asarray(a, np.float32)
    x, c, ctx, c_ctx = f(inp["x"]), f(inp["c"]), f(inp["ctx"]), f(inp["c_ctx"])
    off, smw = small_layout(cfg)
    ident = np.eye(128, dtype=np.float32)
    Rm = np.zeros((128, 128), np.float32)
    for do in range(128):
        if (do % 64) < 32:
            Rm[do + 32, do] = -1.0
        else:
            Rm[do - 32, do] = 1.0
    consts = np.concatenate([ident, Rm], axis=1)
    inv = (1.0 / (10000.0 ** (np.arange(0, 64, 2, dtype=np.float32) / 64.0))).astype(np.float32)
    rpb = f(inp["na_rpb"])[0]
    qc = np.arange(64)
    cs = np.clip(qc - 8, 0, 48)
    kc = np.arange(64)
    valid = (kc[:, None] >= cs[None, :]) & (kc[:, None] < cs[None, :] + 16)
    cidx = np.clip(kc[:, None] - qc[None, :] + 15, 0, 30)
    braw = np.empty((8, 15, 64, 64), np.float32)
    for nd in range(15):
        braw[:, nd] = np.where(valid[None], rpb[:, 14 - nd][:, cidx], np.float32(NEG))
    wsrc = {"in0": f(inp["ab_w_in"])[0], "out0": f(inp["ab_w_out"])[0], "in1": f(inp["cd_w_in"])[0],
            "out1": f(inp["cd_w_out"])[0]}
    for l in range(L):
        wsrc["g%d" % l] = f(inp["ffn_w_gate"])[l]
        wsrc["u%d" % l] = f(inp["ffn_w_up"])[l]
        wsrc["d%d" % l] = f(inp["ffn_w_down"])[l]
    ada_w = f(inp["ada_w"])
    maps = []
    for core in range(8):
        b, q = core // 4, core % 4
        m = {}
        m["x_tm"] = np.ascontiguousarray(x[b, q * T:(q + 1) * T])
        xh = np.zeros((512, D), np.float32)
        if q > 0:
            xh[0:256] = x[b, q * T - 256:q * T]
        if q < 3:
            xh[256:512] = x[b, (q + 1) * T:(q + 1) * T + 256]
        m["xh_tm"] = xh
        m["ctx_tm"] = np.ascontiguousarray(ctx[b])
        sm = np.zeros((128, smw), np.float32)

        def put(name, arr):
            o, w = off[name]
            sm[:, o:o + w] = np.asarray(arr, np.float32).reshape(128, w)
        cv = np.stack([_pl(c[b], KD), _pl(c_ctx, KD)], axis=2)
        put("cv", cv.reshape(128, KD * 2))
        e8 = np.zeros(8, np.float32); e8[core] = 1
        put("m8", np.tile(e8, (128, 1)))
        eb = np.zeros(2, np.float32); eb[b] = 1
        put("mb", np.tile(eb, (128, 1)))
        el = np.zeros(4, np.float32); er = np.zeros(4, np.float32)
        if q > 0:
            el[q - 1] = 1
        if q < 3:
            er[q + 1] = 1
        put("mL", np.tile(el, (128, 1))); put("mR", np.tile(er, (128, 1)))
        put("nhl", np.full((128, 1), 1.0 if q == 0 else 0.0)); put("nhr", np.full((128, 1), 1.0 if q == 3 else 0.0))
        put("istop", np.full((128, 1), 1.0 if q == 0 else 0.0)); put("ntop", np.full((128, 1), 0.0 if q == 0 else 1.0))
        put("isbot", np.full((128, 1), 1.0 if q == 3 else 0.0)); put("nbot", np.full((128, 1), 0.0 if q == 3 else 1.0))
        put("adab", np.concatenate([_pl(f(inp["ada_b"])[l], NCH6) for l in range(L)], axis=1))
        put("ng", np.concatenate([_pl(f(inp["norm_g"])[l, i], KD) for l in range(L) for i in range(2)], axis=1))
        put("fg", _pl(f(inp["final_g"]), KD))
        put("qg", f(inp["gqa_q_gain"])[0].reshape(128, 1)); put("kg", f(inp["gqa_k_gain"])[0].reshape(128, 1))
        for nm in ("lq1", "lk1", "lq2", "lk2"):
            put(nm, f(inp["diff_" + nm])[0].reshape(128, 1))
        put("slg", _pl(f(inp["diff_subln_g"])[0], 2))
        put("scw", np.concatenate([_pl(f(inp["sconv_w"])[0, j], KC) for j in range(3)], axis=1))
        put("fcw", np.concatenate([_pl(f(inp["ffn_conv_w"])[l, j], KF) for l in range(L) for j in range(3)], axis=1))
        put("fcb", np.concatenate([_pl(f(inp["ffn_conv_b"])[l], KF) for l in range(L)], axis=1))
        m["small"] = sm
        t = np.arange(q * T, (q + 1) * T)
        ang_r = (t // GW_).astype(np.float32)[:, None] * inv
        ang_c = (t % GW_).astype(np.float32)[:, None] * inv
        ang = np.concatenate([ang_r, ang_r, ang_c, ang_c], axis=-1)
        m["cosT"] = np.ascontiguousarray(np.cos(ang).astype(np.float32).T)
        m["sinT"] = np.ascontiguousarray(np.sin(ang).astype(np.float32).T)
        m["consts"] = consts
        m["braw"] = braw
        ncs = 6 * D // 4
        m["adaw"] = np.ascontiguousarray(ada_w[:, :, q * ncs:(q + 1) * ncs])
        maps.append(m)
    return maps, wsrc


_CACHE = {}


def run(cfg, inp):
    key = (cfg.D, cfg.DFF, cfg.SEQ, cfg.CW)
    if key not in _CACHE:
        _CACHE[key] = build(cfg)
    nc, W = _CACHE[key]
    maps, wsrc = host_inputs(cfg, inp)
    for core in range(8):
        q = core % 4
        for k, wm in W.items():
            maps[core][wm.name + "_f32"] = wm.host(wsrc[k], q)
    res = run_bass_kernel_spmd(nc, maps, core_ids=list(range(8)))
    T = cfg.T
    out = np.empty((2, cfg.SEQ, cfg.D), np.float32)
    for core in range(8):
        b, q = core // 4, core % 4
        out[b, q * T:(q + 1) * T] = np.asarray(res.results[core]["out_tm"], np.float32)
    return out


def kernel(**inputs):
    return run(Cfg(), inputs)
```

```python
import math
from contextlib import ExitStack
import numpy as np
import concourse.bass as bass
import concourse.mybir as mybir
from concourse.bass_utils import run_bass_kernel_spmd

F32 = mybir.dt.float32
BF16 = mybir.dt.bfloat16
AF = mybir.ActivationFunctionType
ALU = mybir.AluOpType
EPS = 1e-6
NEG = -30000.0
GW_ = 64


class Cfg:
    def __init__(self, D=2048, DFF=5632, SEQ=16384, CW=1024):
        self.D, self.DFF, self.SEQ, self.CW = D, DFF, SEQ, CW
        self.KD, self.KF, self.KC = D // 128, DFF // 128, CW // 128
        self.T = SEQ // 4
        self.NT = self.T // 512
        self.NCH6 = 6 * D // 128
        self.L = 2
        self.N0 = 4608
        self.N1 = 3 * CW + 3072
        self.FT = 510
        self.XW = self.T + 2 + 258


class Ev:
    __slots__ = ("sem", "val")

    def __init__(self, sem, val):
        self.sem, self.val = sem, val


class Buf:
    def __init__(self, t, name):
        self.t, self.name = t, name
        self.lw = {}
        self.rd = []
        self.dsem = None
        self.dcnt = 0


class Sched:
    def __init__(self, nc):
        self.nc = nc
        self.eng = {"pe": nc.tensor, "act": nc.scalar, "dve": nc.vector, "pool": nc.gpsimd, "sp": nc.sync}
        self.esem = {e: nc.alloc_semaphore(name="e_" + e) for e in self.eng}
        self.cnt = {e: 0 for e in self.eng}
        self.seen = {e: {} for e in self.eng}
        self.bufs = []
        self.nb = 0

    def buf(self, t, name=None):
        b = Buf(t, name or "b%d" % len(self.bufs))
        self.bufs.append(b)
        return b

    def sb(self, name, shape, dt, es=None):
        self.uid = getattr(self, "uid", 0) + 1
        name = "%s_%d" % (name, self.uid)
        if es is not None:
            return self.buf(es.enter_context(self.nc.sbuf_tensor(name, list(shape), dt)), name)
        return self.buf(self.nc.alloc_sbuf_tensor(name, list(shape), dt), name)

    def ps(self, name):
        return self.buf(self.nc.alloc_psum_tensor(name, [128, 512], F32), name)

    def _wait(self, e, ev):
        if e == "pe" and ev.sem is self.esem["pe"]:
            return
        k = id(ev.sem)
        if self.seen[e].get(k, 0) >= ev.val:
            return
        self.eng[e].wait_ge(ev.sem, ev.val)
        self.seen[e][k] = ev.val

    def _deps(self, e, r, w):
        for b in r:
            for ev in b.lw.values():
                self._wait(e, ev)
        for b in w:
            for ev in b.lw.values():
                self._wait(e, ev)
            for ev in b.rd:
                self._wait(e, ev)

    def _post(self, ev, key, r, w):
        for b in r:
            b.rd.append(ev)
            if len(b.rd) > 24:
                b.rd = b.rd[-24:] if False else b.rd
        for b in w:
            b.lw[key] = ev
            b.rd = []

    def _skip(self):
        import os, sys
        if not hasattr(self, "limit"):
            self.limit = int(os.environ.get("KLIMIT", "1000000000"))
            self.total = 0
            self.log = []
        self.total += 1
        if self.total > self.limit:
            return True
        f = sys._getframe(2)
        ln = [f.f_lineno]
        while f.f_back is not None and len(ln) < 4:
            f = f.f_back
            ln.append(f.f_lineno)
        self.log.append((self.total, ln))
        return False

    def op(self, e, fn, r=(), w=()):
        if self._skip():
            return None
        self._deps(e, r, w)
        ins = fn()
        self.cnt[e] += 1
        ins.then_inc(self.esem[e], 1)
        ev = Ev(self.esem[e], self.cnt[e])
        for b in r:
            b.rd = [x for x in b.rd if x.sem is not ev.sem]
        self._post(ev, e, r, w)
        return ins

    def pe(self, fn, r=(), w=()):
        return self.op("pe", fn, r, w)

    def act(self, fn, r=(), w=()):
        return self.op("act", fn, r, w)

    def dve(self, fn, r=(), w=()):
        return self.op("dve", fn, r, w)

    def pool(self, fn, r=(), w=()):
        return self.op("pool", fn, r, w)

    def dma(self, q, out, in_, sbuf, r=(), w=()):
        if self._skip():
            return None
        self._deps(q, r, w)
        if sbuf.dsem is None:
            sbuf.dsem = self.nc.alloc_semaphore(name="d_" + sbuf.name)
        ins = self.eng[q].dma_start(out=out, in_=in_, allow_slow_non_contiguous=True)
        ins.then_inc(sbuf.dsem, 16)
        sbuf.dcnt += 16
        ev = Ev(sbuf.dsem, sbuf.dcnt)
        for b in r:
            b.rd = [x for x in b.rd if x.sem is not ev.sem]
        self._post(ev, "d%d" % id(sbuf), r, w)
        return ins

    def coll(self, kind, op, groups, src, dst, cbuf, tok=None):
        e = "pool"
        if self._skip():
            return None
        if cbuf.dsem is None:
            cbuf.dsem = self.nc.alloc_semaphore(name="c_" + cbuf.name)
        ins = self.nc.gpsimd.collective_compute(kind, op, replica_groups=groups, ins=[src], outs=[dst])
        ins.then_inc(cbuf.dsem)
        cbuf.dcnt += 1
        self.nc.gpsimd.wait_ge(cbuf.dsem, cbuf.dcnt)
        tb = tok if tok is not None else cbuf
        self.op("pool", lambda: self.nc.gpsimd.memset(tb.t[:], 0.0), w=[tb])
        return ins

    def barrier(self, pool=False):
        engs = [e for e in self.eng if pool or e != "pool"]
        evs = [Ev(self.esem[e], self.cnt[e]) for e in engs if self.cnt[e] > 0]
        for b in self.bufs:
            if b.dsem is not None and b.dcnt > 0 and not (b.name.startswith("coll") or b.name.startswith("wcast")):
                evs.append(Ev(b.dsem, b.dcnt))
        for e in engs:
            for ev in evs:
                if ev.sem is self.esem[e]:
                    continue
                self._wait(e, ev)
        for b in self.bufs:
            if getattr(b, "keep", False):
                continue
            b.lw = {}
            b.rd = []
        self.nb += 1


class WMat:
    def __init__(self, nc, name, K, N, gw):
        self.name, self.K, self.N, self.gw = name, K, N, gw
        self.KCH = K // 128
        self.KL = self.KCH // 4
        assert self.KL * 4 == self.KCH and N % gw == 0
        self.G = N // gw
        self.rows, self.cols = self.G * 128, self.KL * gw
        self.src = nc.dram_tensor(name + "_f32", [self.rows, self.cols], F32, kind="ExternalInput")
        self.gpc = max(1, (1 << 20) // (128 * self.cols * 2))
        self.chunks = []
        g0 = 0
        while g0 < self.G:
            ng = min(self.gpc, self.G - g0)
            b16 = nc.dram_tensor("%s_b16_%d" % (name, g0), [ng * 128, self.cols], BF16)
            full = nc.dram_tensor("%s_full_%d" % (name, g0), [4 * ng * 128, self.cols], BF16)
            self.chunks.append((g0, ng, b16, full))
            g0 += ng

    def host(self, w, r):
        KL, gw, G = self.KL, self.gw, self.G
        ws = w[r * KL * 128:(r + 1) * KL * 128, :]
        a = ws.reshape(KL, 128, G, gw).transpose(2, 1, 0, 3)
        return np.ascontiguousarray(a.reshape(G * 128, KL * gw))

    def tile_shape(self):
        return [128, 4, self.KL * self.gw]

    def src_ap(self, g):
        g0, ng, b16, full = self.chunks[g // self.gpc]
        v = full.ap().rearrange("(r g p) c -> g p r c", r=4, g=ng, p=128)
        return v[g - g0]

    def lhs(self, wt, k, c0, c1):
        r, kl = k // self.KL, k % self.KL
        return wt.t[:, r, kl * self.gw + c0: kl * self.gw + c1]


class Stream:
    def __init__(self, S, bufs, loads):
        self.S, self.bufs, self.loads = S, bufs, loads
        self.issued = 0

    def get(self, i):
        nb = len(self.bufs)
        while self.issued < min(len(self.loads), i + nb):
            j = self.issued
            self.loads[j](self.bufs[j % nb])
            self.issued += 1
        return self.bufs[i % nb]


def build(cfg):
    r = _build(cfg)
    return r


def _build(cfg):
    import os
    STOP = int(os.environ.get('KSTOP', '99'))
    DBG = int(os.environ.get('KDBG', '0'))
    nc = bass.Bass("TRN2", target_bir_lowering=False)
    S = Sched(nc)
    global _LASTS
    _LASTS = S
    D, DFF, T, KD, KF, KC, L = cfg.D, cfg.DFF, cfg.T, cfg.KD, cfg.KF, cfg.KC, cfg.L
    NCH6, XW, NT = cfg.NCH6, cfg.XW, cfg.NT
    NKT = T // 128
    GRP = [[0, 1, 2, 3], [4, 5, 6, 7]]
    ALL8 = [list(range(8))]
    SC = 128 ** -0.5

    x_tm = nc.dram_tensor("x_tm", [T, D], F32, kind="ExternalInput")
    xh_tm = nc.dram_tensor("xh_tm", [512, D], F32, kind="ExternalInput")
    ctx_tm = nc.dram_tensor("ctx_tm", [256, D], F32, kind="ExternalInput")
    out_tm = nc.dram_tensor("out_tm", [T, D], F32, kind="ExternalOutput")
    sm_off, sm_w = small_layout(cfg)
    small = nc.dram_tensor("small", [128, sm_w], F32, kind="ExternalInput")
    cos_d = nc.dram_tensor("cosT", [128, T], F32, kind="ExternalInput")
    sin_d = nc.dram_tensor("sinT", [128, T], F32, kind="ExternalInput")
    cst_d = nc.dram_tensor("consts", [128, 256], F32, kind="ExternalInput")
    braw_d = nc.dram_tensor("braw", [8, 15, 64, 64], F32, kind="ExternalInput")
    adaw_d = nc.dram_tensor("adaw", [L, D, NCH6 * 32], F32, kind="ExternalInput")
    NCA = NCH6 // 4
    W = {}
    W["in0"] = WMat(nc, "w_in0", D, cfg.N0, 256)
    W["out0"] = WMat(nc, "w_out0", 2048, D, min(512, D))
    W["in1"] = WMat(nc, "w_in1", D, cfg.N1, 256)
    W["out1"] = WMat(nc, "w_out1", cfg.CW + 1024, D, min(512, D))
    for l in range(L):
        W["g%d" % l] = WMat(nc, "w_g%d" % l, D, DFF, 256)
        W["u%d" % l] = WMat(nc, "w_u%d" % l, D, DFF, 256)
        W["d%d" % l] = WMat(nc, "w_d%d" % l, DFF, D, 256)
    for key_, wm_ in W.items():
        wm_.key = key_

    XT = [nc.dram_tensor("XT%d" % i, [KD, 128, XW], F32) for i in range(2)]
    XH = nc.dram_tensor("XH", [KD, 128, 512], F32)
    LAT0, CTX0 = 1, T + 3
    QT0 = nc.dram_tensor("QT0", [16, 128, T + 256], BF16)
    KaT = nc.dram_tensor("KaT", [8, 128, T + 512], BF16)
    KaTc = nc.dram_tensor("KaTc", [8, 128, 256], BF16)
    NKE = (T + 512) // 128
    Va = nc.dram_tensor("Va", [128, 8, NKE, 128], BF16)
    Vac = nc.dram_tensor("Vac", [128, 8, 2, 128], BF16)
    KbT_src = [nc.dram_tensor("KbT_src%d" % i, [128, T], BF16) for i in range(2)]
    KbT_all = [nc.dram_tensor("KbT_all%d" % i, [4 * 128, T], BF16) for i in range(2)]
    KbTc = nc.dram_tensor("KbTc", [2, 128, 256], BF16)
    Vb_src = [nc.dram_tensor("Vb_src%d" % i, [128, NKT * 128], BF16) for i in range(2)]
    Vb_all = [nc.dram_tensor("Vb_all%d" % i, [4 * 128, NKT * 128], BF16) for i in range(2)]
    Vbc = nc.dram_tensor("Vbc", [128, 2, 2, 128], BF16)
    QT1 = nc.dram_tensor("QT1", [8, 128, T], BF16)
    KdT_src = [nc.dram_tensor("KdT_src%d" % i, [128, T], BF16) for i in range(8)]
    KdT_all = [nc.dram_tensor("KdT_all%d" % i, [4 * 128, T], BF16) for i in range(8)]
    KdTc = nc.dram_tensor("KdTc", [8, 128, 256], BF16)
    NKH = NKT // 2
    Vd_src = [nc.dram_tensor("Vd_src%d" % i, [128, NKH * 256], BF16) for i in range(8)]
    Vd_all = [nc.dram_tensor("Vd_all%d" % i, [4 * 128, NKH * 256], BF16) for i in range(8)]
    Vdc = nc.dram_tensor("Vdc", [128, 4, 2, 256], BF16)
    GCU = nc.dram_tensor("GCU", [KC, 128, T + 2], F32)
    GB = nc.dram_tensor("GB", [KC, 128, T], F32)
    HXW = 2 * max(KD, KC)
    hx_src = nc.dram_tensor("hx_src", [128, HXW], F32)
    hx_all = nc.dram_tensor("hx_all", [4 * 128, HXW], F32)
    NAtab = nc.dram_tensor("NAtab", [3, 8, 128, 1536], BF16)
    MODW = L * 2 * (NCH6 // 4)
    modsrc = nc.dram_tensor("modsrc", [128, MODW], F32)
    moddst = nc.dram_tensor("moddst", [4 * 128, MODW], F32)

    smt = S.sb("smt", [128, sm_w], F32)
    cst = S.sb("cst", [128, 256], F32)
    cstb = S.sb("cstb", [128, 256], BF16)
    onesb = S.sb("onesb", [128, 128], BF16)
    onesf = S.sb("onesf", [128, 128], F32)
    drv = S.sb("drv", [128, L * 2 * 6 * KD + 8], F32)
    lamt = S.sb("lamt", [128, 8], F32)
    fixw = S.sb("fixw", [128, L * 4 * KF], F32)
    PS = [S.ps("ps%d" % i) for i in range(8)]
    collb = S.sb("coll", [128, 1], F32)
    collb.name = "coll"
    gsc = S.sb("gsc", [128, 2], F32)
    wtok = {}
    for key_, wm_ in W.items():
        for ci_ in range(len(wm_.chunks)):
            tk_ = S.sb("wtok", [128, 1], F32)
            tk_.keep = True
            wtok[(key_, ci_)] = tk_

    def sm(name, a=None, b=None):
        o, w = sm_off[name]
        if a is None:
            return smt.t[:, o:o + w]
        return smt.t[:, o + a:o + (b if b is not None else a + 1)]

    ident = cst.t[:, 0:128]
    Rmb = cstb.t[:, 128:256]
    identb = cstb.t[:, 0:128]

    def DRV(l, tt, i, k=None):
        o = ((l * 2 + tt) * 6 + i) * KD
        if k is None:
            return drv.t[:, o:o + KD]
        return drv.t[:, o + k:o + k + 1]

    es0 = ExitStack()
    S.dma("sp", smt.t[:], small[:, :], smt, w=[smt])
    S.dma("sp", cst.t[:], cst_d[:, :], cst, w=[cst])
    S.dve(lambda: nc.vector.tensor_copy(cstb.t[:], cst.t[:]), r=[cst], w=[cstb])
    S.dve(lambda: nc.vector.memset(onesb.t[:], 1.0), w=[onesb])
    S.dve(lambda: nc.vector.memset(onesf.t[:], 1.0), w=[onesf])

    wcast = S.buf(None, "wcast")

    def stage_weights(keys):
        for key in keys:
            wm = W[key]
            for ci, (g0, ng, b16, full) in enumerate(wm.chunks):
                S.dma("pool", b16[:, :], wm.src[g0 * 128:(g0 + ng) * 128, :], wcast)
                if wcast.dsem is not None:
                    nc.gpsimd.wait_ge(wcast.dsem, wcast.dcnt)
                tk = wtok[(key, ci)]
                S.coll("AllGather", ALU.bypass, GRP, b16.ap().opt(), full.ap().opt(), collb, tok=tk)

    stage_weights(["in0"])
    if STOP == 2:
        return nc, W
    scv = S.sb("scv", [128, KD * 2], F32, es0)
    S.act(lambda: nc.scalar.activation(out=scv.t[:], in_=sm("cv"), func=AF.Silu), r=[smt], w=[scv])
    modloc = S.sb("modloc", [128, L * 2 * NCA], F32, es0)
    Z = S.sb("Z", [128, 4, L * 2 * NCA], F32, es0)
    CB = 4
    adaw = [S.sb("adaw%d" % i, [128, KD, CB * 128], F32, es0) for i in range(2)]
    ai = 0
    for l in range(L):
        ps = PS[l % 2]
        for cb in range(0, NCA, CB):
            nb = min(CB, NCA - cb)
            aw = adaw[ai % 2]
            ai += 1
            S.dma("sp", aw.t[:, :, :nb * 128],
                  adaw_d.ap()[l, :, cb * 128:(cb + nb) * 128].rearrange("(k p) n -> p k n", p=128), aw, w=[aw])
            for ci in range(nb):
                cc = cb + ci
                for k in range(KD):
                    S.pe(lambda k=k, ci=ci, cc=cc, ps=ps, aw=aw: nc.tensor.matmul(
                        ps.t[:, cc * 2:cc * 2 + 2], lhsT=aw.t[:, k, ci * 128:(ci + 1) * 128],
                        rhs=scv.t[:, k * 2:(k + 1) * 2], start=(k == 0), stop=(k == KD - 1)), r=[aw, scv], w=[ps])
        for v in range(2):
            o = (l * 2 + v) * NCA
            S.dve(lambda o=o, v=v, ps=ps: nc.vector.tensor_copy(
                modloc.t[:, o:o + NCA], ps.t[:, v:2 * NCA:2]), r=[ps], w=[modloc])
    S.dma("sp", modsrc[:, :], modloc.t[:], modloc, r=[modloc])
    if modloc.dsem is not None:
        nc.gpsimd.wait_ge(modloc.dsem, modloc.dcnt)
    S.coll("AllGather", ALU.bypass, GRP, modsrc.ap().opt(), moddst.ap().opt(), collb)
    S.dma("sp", Z.t[:], moddst.ap().rearrange("(r p) c -> p r c", p=128), Z, r=[collb], w=[Z])
    stage_weights(["out0", "g0", "u0", "d0"])
    modv = S.sb("modv", [128, L * 2 * NCH6], F32, es0)
    for l in range(L):
        for v in range(2):
            for r in range(4):
                o = (l * 2 + v) * NCH6 + r * NCA
                zi = (l * 2 + v) * NCA
                ab = sm("adab", l * NCH6 + r * NCA, l * NCH6 + (r + 1) * NCA)
                S.dve(lambda o=o, zi=zi, r=r, ab=ab: nc.vector.tensor_tensor(
                    modv.t[:, o:o + NCA], Z.t[:, r, zi:zi + NCA], ab, op=ALU.add), r=[Z, smt], w=[modv])
        for tt in range(2):
            mo = (l * 2 + tt) * NCH6
            for half in range(2):
                sh = modv.t[:, mo + (3 * half) * KD: mo + (3 * half + 1) * KD]
                scl = modv.t[:, mo + (3 * half + 1) * KD: mo + (3 * half + 2) * KD]
                gt = modv.t[:, mo + (3 * half + 2) * KD: mo + (3 * half + 3) * KD]
                ng = sm("ng", (l * 2 + half) * KD, (l * 2 + half + 1) * KD)
                S.dve(lambda l=l, tt=tt, half=half, scl=scl, ng=ng: nc.vector.scalar_tensor_tensor(
                    DRV(l, tt, 3 * half), scl, 1.0, ng, op0=ALU.add, op1=ALU.mult), r=[modv, smt], w=[drv])
                S.dve(lambda l=l, tt=tt, half=half, sh=sh: nc.vector.tensor_copy(DRV(l, tt, 3 * half + 1), sh),
                      r=[modv], w=[drv])
                S.dve(lambda l=l, tt=tt, half=half, gt=gt: nc.vector.tensor_copy(DRV(l, tt, 3 * half + 2), gt),
                      r=[modv], w=[drv])
    if STOP == 3:
        return nc, W
    lam_init = 0.8 - 0.6 * math.exp(-0.3 * 1)
    S.dve(lambda: nc.vector.tensor_tensor(lamt.t[:, 0:1], sm("lq1"), sm("lk1"), op=ALU.mult), r=[smt], w=[lamt])
    S.dve(lambda: nc.vector.tensor_tensor(lamt.t[:, 1:2], sm("lq2"), sm("lk2"), op=ALU.mult), r=[smt, lamt], w=[lamt])
    S.pe(lambda: nc.tensor.matmul(PS[0].t[:, 0:2], lhsT=onesf.t[:], rhs=lamt.t[:, 0:2], start=True, stop=True),
         r=[onesf, lamt], w=[PS[0]])
    S.act(lambda: nc.scalar.activation(out=lamt.t[:, 2:4], in_=PS[0].t[:, 0:2], func=AF.Exp), r=[PS[0]], w=[lamt])
    S.dve(lambda: nc.vector.tensor_tensor(lamt.t[:, 4:5], lamt.t[:, 2:3], lamt.t[:, 3:4], op=ALU.subtract),
          r=[lamt], w=[lamt])
    S.dve(lambda: nc.vector.tensor_scalar(lamt.t[:, 5:6], lamt.t[:, 4:5], lam_init, -1.0, op0=ALU.add, op1=ALU.mult),
          r=[lamt], w=[lamt])
    S.dve(lambda: nc.vector.tensor_scalar(lamt.t[:, 6:8], sm("slg"), 1.0 - lam_init, None, op0=ALU.mult),
          r=[smt, lamt], w=[lamt])
    S.dve(lambda: nc.vector.tensor_scalar(gsc.t[:, 0:1], sm("qg"), SC, None, op0=ALU.mult), r=[smt], w=[gsc])
    S.dve(lambda: nc.vector.tensor_copy(gsc.t[:, 1:2], sm("kg")), r=[smt, gsc], w=[gsc])
    for l in range(L):
        for i, (wi, mk) in enumerate([(0, "nhl"), (2, "nhr")]):
            o = (l * 4 + i) * KF
            wv = sm("fcw", (l * 3 + wi) * KF, (l * 3 + wi + 1) * KF)
            S.dve(lambda o=o, wv=wv, mk=mk: nc.vector.tensor_scalar(
                fixw.t[:, o:o + KF], wv, sm(mk), -1.0, op0=ALU.mult, op1=ALU.mult), r=[smt, fixw], w=[fixw])
            o2 = (l * 4 + 2 + i) * KF
            S.dve(lambda o2=o2, wv=wv: nc.vector.tensor_scalar(
                fixw.t[:, o2:o2 + KF], wv, -1.0, None, op0=ALU.mult), r=[smt, fixw], w=[fixw])

    if STOP == 4:
        return nc, W
    S.barrier()
    es0.close()
    es0 = ExitStack()
    tabs = [S.sb("tab%d" % i, [128, 8, 1536], BF16, es0) for i in range(3)]
    tmpb = S.sb("tmpb", [128, 8, 1536], BF16, es0)
    tstage = S.sb("tstage", [128, 8, 1536], F32, es0)

    def rowvalid(var, t, kr2, qr4):
        krel = -4 + 2 * t + kr2
        if var == 0:
            return 0 <= krel <= 7
        if var == 2:
            return -4 <= krel <= 3
        return -4 <= krel - qr4 <= 3

    for var in range(3):
        tb = tstage
        S.dve(lambda tb=tb: nc.vector.memset(tb.t[:], NEG), w=[tb])
        for t in range(6):
            for kr2 in range(2):
                q = 0
                while q < 4:
                    if not rowvalid(var, t, kr2, q):
                        q += 1
                        continue
                    q1 = q
                    while q1 + 1 < 4 and rowvalid(var, t, kr2, q1 + 1):
                        q1 += 1
                    nd0 = q + 11 - 2 * t - kr2
                    n = q1 - q + 1
                    for h in range(8):
                        src = braw_d.ap()[h, nd0:nd0 + n].rearrange("a k q -> k a q")
                        dst = tb.t[kr2 * 64:(kr2 + 1) * 64, h, t * 256 + q * 64: t * 256 + (q1 + 1) * 64]
                        dst = dst.rearrange("k (a q) -> k a q", a=n)
                        S.dma("sp", dst, src, tb, w=[tb])
                    q = q1 + 1
        for h in range(8):
            S.dve(lambda h=h, var=var: nc.vector.tensor_copy(tabs[var].t[:, h, :], tstage.t[:, h, :]),
                  r=[tstage], w=[tabs[var]])
    for vi, (src_t, mk, nmk) in enumerate([(tabs[0], "istop", "ntop"), (None, None, None), (tabs[2], "isbot", "nbot")]):
        for h in range(8):
            if src_t is None:
                S.dma("sp", NAtab[vi, h], tabs[1].t[:, h, :], tabs[1], r=[tabs[1]])
                continue
            S.dve(lambda h=h, nmk=nmk: nc.vector.tensor_scalar(
                tmpb.t[:, h, :], tabs[1].t[:, h, :], sm(nmk), None, op0=ALU.mult), r=[tabs[1], smt], w=[tmpb])
            S.dve(lambda h=h, mk=mk, src_t=src_t: nc.vector.scalar_tensor_tensor(
                tmpb.t[:, h, :], src_t.t[:, h, :], sm(mk), tmpb.t[:, h, :], op0=ALU.mult, op1=ALU.add),
                r=[src_t, smt, tmpb], w=[tmpb])
            S.dma("sp", NAtab[vi, h], tmpb.t[:, h, :], tmpb, r=[tmpb])
    S.barrier()
    es0.close()
    if STOP == 5:
        return nc, W

    epst = S.sb("epst", [128, 1], F32)
    S.dve(lambda: nc.vector.memset(epst.t[:], EPS), w=[epst])
    hxg = S.sb("hxg", [128, 4, HXW], F32)
    hxo = S.sb("hxo", [128, HXW], F32)
    hxs = S.sb("hxs", [128, HXW], F32)
    sq = S.sb("sq", [128, 512], F32)
    sqb = S.sb("sqb", [128, 512], BF16)
    rstd = S.sb("rstd", [128, 512], F32)
    tmp = S.sb("tmp", [128, 512], F32)
    tmp2 = S.sb("tmp2", [128, 512], F32)
    qn = S.sb("qn", [128, 512], F32)
    qb = S.sb("qb", [128, 512], BF16)
    stg = [S.sb("stg%d" % i, [128, 512], BF16) for i in range(3)]
    stf = [S.sb("stf%d" % i, [128, 512], F32) for i in range(2)]
    cnt = {"stg": 0, "stf": 0, "ps": 0}

    def nxt(lst, key):
        cnt[key] += 1
        return lst[cnt[key] % len(lst)]

    def XTv(i):
        return XT[i].ap().rearrange("k p t -> p k t")

    def transpose_in2(src_tm, ntok, dstv, col0, xin, xo):
        for j in range(0, ntok, 512):
            w = min(512, ntok - j)
            ns = w // 128
            for s in range(ns):
                xi = xin[s]
                S.dma("sp", xi.t[:], src_tm[j + s * 128: j + (s + 1) * 128, :], xi, w=[xi])
            o = xo[(j // 512) % len(xo)]
            for k in range(KD):
                ps = PS[k % 6]
                for s in range(ns):
                    xi = xin[s]
                    S.pe(lambda xi=xi, k=k, ps=ps, s=s: nc.tensor.transpose(
                        ps.t[:, s * 128:(s + 1) * 128], xi.t[:, k * 128:(k + 1) * 128], ident),
                        r=[xi, cst], w=[ps])
                if k % 2 == 0:
                    S.dve(lambda o=o, k=k, ps=ps, w=w: nc.vector.tensor_copy(o.t[:, k, :w], ps.t[:, :w]),
                          r=[ps], w=[o])
                else:
                    S.act(lambda o=o, k=k, ps=ps, w=w: nc.scalar.copy(o.t[:, k, :w], ps.t[:, :w]),
                          r=[ps], w=[o])
            S.dma("sp", dstv[:, :, col0 + j: col0 + j + w], o.t[:, :, :w], o, r=[o])

    def norm_mod(xt, wdt, scale_fn, bias_fn, ht, c0=0):
        ps = PS[7]
        for k in range(KD):
            S.act(lambda k=k: nc.scalar.activation(out=sqb.t[:, :wdt], in_=xt.t[:, k, c0:c0 + wdt], func=AF.Square),
                  r=[xt], w=[sqb])
            S.pe(lambda k=k: nc.tensor.matmul(ps.t[:, :wdt], lhsT=onesb.t[:], rhs=sqb.t[:, :wdt],
                                              start=(k == 0), stop=(k == KD - 1)), r=[sqb, onesb], w=[ps])
        S.act(lambda: nc.scalar.activation(out=rstd.t[:, :wdt], in_=ps.t[:, :wdt], func=AF.Sqrt,
                                           scale=1.0 / D, bias=epst.t[:, 0:1]), r=[ps, epst], w=[rstd])
        S.dve(lambda: nc.vector.reciprocal(rstd.t[:, :wdt], rstd.t[:, :wdt]), r=[rstd], w=[rstd])
        for k in range(KD):
            S.dve(lambda k=k: nc.vector.tensor_tensor(tmp.t[:, :wdt], xt.t[:, k, c0:c0 + wdt], rstd.t[:, :wdt],
                                                      op=ALU.mult), r=[xt, rstd], w=[tmp])
            if bias_fn is None:
                S.act(lambda k=k: nc.scalar.activation(out=ht.t[:, k, :wdt], in_=tmp.t[:, :wdt], func=AF.Identity,
                                                       scale=scale_fn(k)), r=[tmp, drv, smt], w=[ht])
            else:
                S.act(lambda k=k: nc.scalar.activation(out=ht.t[:, k, :wdt], in_=tmp.t[:, :wdt], func=AF.Identity,
                                                       scale=scale_fn(k), bias=bias_fn(k)), r=[tmp, drv, smt], w=[ht])

    def halo_exchange(n, put):
        S.dma("sp", hx_src[:, 0:2 * n], hxs.t[:, 0:2 * n], hxs, r=[hxs])
        if hxs.dsem is not None:
            nc.gpsimd.wait_ge(hxs.dsem, hxs.dcnt)
        S.coll("AllGather", ALU.bypass, GRP, hx_src.ap().opt(), hx_all.ap().opt(), collb)
        S.dma("sp", hxg.t[:], hx_all.ap().rearrange("(r p) c -> p r c", p=128), hxg, r=[collb], w=[hxg])
        for side, mk, off in ((0, "mL", n), (1, "mR", 0)):
            for r in range(4):
                if r == 0:
                    S.dve(lambda side=side, mk=mk, off=off, r=r: nc.vector.tensor_scalar(
                        hxo.t[:, side * n:(side + 1) * n], hxg.t[:, r, off:off + n], sm(mk, r), None, op0=ALU.mult),
                        r=[hxg, smt, hxo], w=[hxo])
                else:
                    S.dve(lambda side=side, mk=mk, off=off, r=r: nc.vector.scalar_tensor_tensor(
                        hxo.t[:, side * n:(side + 1) * n], hxg.t[:, r, off:off + n], sm(mk, r),
                        hxo.t[:, side * n:(side + 1) * n], op0=ALU.mult, op1=ALU.add),
                        r=[hxg, smt, hxo], w=[hxo])
        put(hxo.t[:, 0:n].rearrange("p (k o) -> p k o", o=1), hxo.t[:, n:2 * n].rearrange("p (k o) -> p k o", o=1))

    def qk_post(ps, w, gain, do_norm, rope, scale, outb):
        aux = PS[6]
        if do_norm:
            S.act(lambda: nc.scalar.activation(out=sqb.t[:, :w], in_=ps.t[:, :w], func=AF.Square), r=[ps], w=[sqb])
            S.pe(lambda: nc.tensor.matmul(aux.t[:, :w], lhsT=onesb.t[:], rhs=sqb.t[:, :w], start=True, stop=True),
                 r=[sqb, onesb], w=[aux])
            S.act(lambda: nc.scalar.activation(out=rstd.t[:, :w], in_=aux.t[:, :w], func=AF.Sqrt, scale=1.0 / 128,
                                               bias=epst.t[:, 0:1]), r=[aux, epst], w=[rstd])
            S.dve(lambda: nc.vector.reciprocal(rstd.t[:, :w], rstd.t[:, :w]), r=[rstd], w=[rstd])
            dst = qn if rope is not None else outb
            S.dve(lambda: nc.vector.scalar_tensor_tensor(dst.t[:, :w], ps.t[:, :w], gain, rstd.t[:, :w],
                                                         op0=ALU.mult, op1=ALU.mult), r=[ps, rstd, gsc], w=[dst])
        else:
            dst = qn if rope is not None else outb
            S.act(lambda: nc.scalar.activation(out=dst.t[:, :w], in_=ps.t[:, :w], func=AF.Copy, scale=scale),
                  r=[ps], w=[dst])
        if rope is not None:
            cs, sn = rope
            S.act(lambda: nc.scalar.copy(qb.t[:, :w], qn.t[:, :w]), r=[qn], w=[qb])
            S.pe(lambda: nc.tensor.matmul(aux.t[:, :w], lhsT=Rmb, rhs=qb.t[:, :w], start=True, stop=True),
                 r=[qb, cstb], w=[aux])
            S.dve(lambda: nc.vector.tensor_tensor(tmp.t[:, :w], qn.t[:, :w], cs.t[:, :w], op=ALU.mult),
                  r=[qn, cs], w=[tmp])
            S.dve(lambda: nc.vector.tensor_tensor(tmp2.t[:, :w], aux.t[:, :w], sn.t[:, :w], op=ALU.mult),
                  r=[aux, sn], w=[tmp2])
            S.dve(lambda: nc.vector.tensor_tensor(outb.t[:, :w], tmp.t[:, :w], tmp2.t[:, :w], op=ALU.add),
                  r=[tmp, tmp2], w=[outb])

    def attn(Qs, qdeps, Wq, keys, ndv, Ob, Lb, Sb, PTs):
        ncomp = len(Qs)
        N = len(keys)
        LA = 2
        for n in range(N + LA):
            if n < N:
                kt = keys[n]
                if kt.get("pre") is not None:
                    kt["pre"]()
                sbk = Sb[n % len(Sb)]
                pt = PTs[n % len(PTs)]
                for c in range(ncomp):
                    S.pe(lambda kt=kt, c=c, sbk=sbk: nc.tensor.matmul(
                        sbk.t[:, c * Wq:(c + 1) * Wq], lhsT=kt["K"][c], rhs=Qs[c], start=True,
                        stop=(kt["bias"] is None)), r=list(kt["deps"]) + list(qdeps), w=[sbk])
                    if kt["bias"] is not None:
                        S.pe(lambda kt=kt, c=c, sbk=sbk: nc.tensor.matmul(
                            sbk.t[:, c * Wq:(c + 1) * Wq], lhsT=identb, rhs=kt["bias"], start=False, stop=True),
                            r=list(kt["deps"]) + [cstb], w=[sbk])
                S.act(lambda sbk=sbk, pt=pt: nc.scalar.activation(out=pt.t[:, :ncomp * Wq], in_=sbk.t[:, :ncomp * Wq],
                                                                  func=AF.Exp), r=[sbk], w=[pt])
            if n >= LA:
                m = n - LA
                kt = keys[m]
                pt = PTs[m % len(PTs)]
                for c in range(ncomp):
                    for dv in range(ndv):
                        S.pe(lambda kt=kt, c=c, dv=dv, pt=pt, m=m: nc.tensor.matmul(
                            Ob[c][dv].t[:, 0:Wq], lhsT=kt["V"][dv], rhs=pt.t[:, c * Wq:(c + 1) * Wq],
                            start=(m == 0), stop=(m == N - 1)), r=list(kt["deps"]) + [pt], w=[Ob[c][dv]])
                    S.pe(lambda c=c, pt=pt, m=m: nc.tensor.matmul(
                        Lb[c].t[:, 0:Wq], lhsT=onesb.t[:], rhs=pt.t[:, c * Wq:(c + 1) * Wq],
                        start=(m == 0), stop=(m == N - 1)), r=[pt, onesb], w=[Lb[c]])

    def load_w(wm, wt, g):
        S.dma("sp", wt.t[:], wm.src_ap(g), wt, r=[wtok[(wm.key, g // wm.gpc)]], w=[wt])

    esx = ExitStack()
    xin = [S.sb("xin%d" % i, [128, D], F32, esx) for i in range(4)]
    xo = [S.sb("xo%d" % i, [128, KD, 512], F32, esx) for i in range(2)]
    if not os.environ.get("KNOTR"):
        transpose_in2(x_tm, T, XTv(0), LAT0, xin, xo)
        transpose_in2(ctx_tm, 256, XTv(0), CTX0, xin, xo)
        transpose_in2(xh_tm, 512, XH.ap().rearrange("k p t -> p k t"), 0, xin, xo)
    zt = S.sb("zt", [128, KD, 1], F32, esx)
    S.dve(lambda: nc.vector.memset(zt.t[:], 0.0), w=[zt])
    if not os.environ.get("KNOZT"):
        S.dma("sp", XTv(0)[:, :, CTX0 - 1:CTX0], zt.t[:], zt, r=[zt])
        S.dma("sp", XTv(0)[:, :, CTX0 + 256:CTX0 + 257], zt.t[:], zt, r=[zt])
    S.barrier()
    esx.close()
    if STOP == 6:
        return nc, W

    def inproj(l, cur):
        es = ExitStack()
        wm = W["in%d" % l]
        xt = [S.sb("ip_xt%d" % i, [128, KD, 512], F32, es) for i in range(2)]
        ht = S.sb("ip_ht", [128, KD, 512], BF16, es)
        wts = [S.sb("ip_w%d" % i, wm.tile_shape(), BF16, es) for i in range(2)]
        cs = S.sb("ip_cos", [128, 512], F32, es)
        sn = S.sb("ip_sin", [128, 512], F32, es)
        ub = S.sb("ip_ub", [128, max(KC, 1), 512], F32, es) if l == 1 else None
        tiles = [("lat", j) for j in range(NT)] + [("ctx", 0)] + ([("halo", 0)] if l == 0 else [])
        wi = 0
        for ti, (kind, j) in enumerate(tiles):
            w = 512 if kind != "ctx" else 256
            x = xt[ti % 2]
            if kind == "lat":
                S.dma("sp", x.t[:, :, :w], XTv(cur)[:, :, LAT0 + j * 512: LAT0 + j * 512 + w], x, w=[x])
                S.dma("sp", cs.t[:], cos_d[:, j * 512:(j + 1) * 512], cs, w=[cs])
                S.dma("sp", sn.t[:], sin_d[:, j * 512:(j + 1) * 512], sn, w=[sn])
            elif kind == "ctx":
                S.dma("sp", x.t[:, :, :w], XTv(cur)[:, :, CTX0: CTX0 + w], x, w=[x])
            else:
                S.dma("sp", x.t[:, :, :w], XH.ap().rearrange("k p t -> p k t"), x, w=[x])
            tt = 1 if kind == "ctx" else 0
            norm_mod(x, w, lambda k: DRV(l, tt, 0, k), lambda k: DRV(l, tt, 1, k), ht)
            if l == 0:
                if kind == "halo":
                    groups = list(range(8, 16))
                else:
                    groups = list(range(18))
            else:
                if kind == "ctx":
                    groups = list(range((3 * KC + 8) // 2, (3 * KC + 24) // 2))
                else:
                    groups = list(range(wm.G))
            for g in groups:
                wt = wts[wi % 2]
                wi += 1
                load_w(wm, wt, g)
                c0 = 2 * g
                kinds = [chunk_kind(l, c0), chunk_kind(l, c0 + 1)]
                if kinds[0][0] in ("av", "bv", "dv"):
                    for s in range(w // 128):
                        ps = nxt(PS[0:6], "ps")
                        for k in range(KD):
                            S.pe(lambda k=k, s=s, ps=ps, wt=wt: nc.tensor.matmul(
                                ps.t[:, 0:256], lhsT=ht.t[:, k, s * 128:(s + 1) * 128], rhs=wm.lhs(wt, k, 0, 256),
                                start=(k == 0), stop=(k == KD - 1)), r=[ht, wt], w=[ps])
                        ob = nxt(stg, "stg")
                        S.act(lambda ob=ob, ps=ps: nc.scalar.copy(ob.t[:, 0:256], ps.t[:, 0:256]), r=[ps], w=[ob])
                        kk, idx = kinds[0]
                        if kk == "av":
                            src = ob.t[:, 0:256].rearrange("p (h d) -> p h d", h=2)
                            if kind == "lat":
                                dst = Va[:, idx:idx + 2, 2 + j * 4 + s, :]
                            elif kind == "halo":
                                dst = Va[:, idx:idx + 2, (s if s < 2 else NKE - 4 + s), :]
                            else:
                                dst = Vac[:, idx:idx + 2, s, :]
                        elif kk == "bv":
                            src = ob.t[:, 0:256].rearrange("p (h d) -> p h d", h=2)
                            if kind == "lat":
                                S.dma("sp", Vb_src[0].ap().rearrange("p (k d) -> p k d", d=128)[:, j * 4 + s, :],
                                      ob.t[:, 0:128], ob, r=[ob])
                                S.dma("sp", Vb_src[1].ap().rearrange("p (k d) -> p k d", d=128)[:, j * 4 + s, :],
                                      ob.t[:, 128:256], ob, r=[ob])
                                continue
                            else:
                                dst = Vbc[:, :, s, :]
                        else:
                            src = ob.t[:, 0:256]
                            if kind == "lat":
                                kt_ = j * 4 + s
                                dst = Vd_src[idx * 2 + kt_ // NKH].ap().rearrange("p (k d) -> p k d", d=256)[:, kt_ % NKH, :]
                            else:
                                dst = Vdc[:, idx, s, :]
                        S.dma("sp", dst, src, ob, r=[ob])
                    continue
                for ci in range(2):
                    kk, idx = kinds[ci]
                    ps = nxt(PS[0:6], "ps")
                    for k in range(KD):
                        S.pe(lambda k=k, ps=ps, wt=wt, ci=ci: nc.tensor.matmul(
                            ps.t[:, :w], lhsT=wm.lhs(wt, k, ci * 128, (ci + 1) * 128), rhs=ht.t[:, k, :w],
                            start=(k == 0), stop=(k == KD - 1)), r=[ht, wt], w=[ps])
                    rope = (cs, sn) if kind == "lat" else None
                    if kk == "u":
                        S.act(lambda ps=ps, idx=idx: nc.scalar.copy(ub.t[:, idx, :w], ps.t[:, :w]), r=[ps], w=[ub])
                        continue
                    if kk == "gb":
                        of = nxt(stf, "stf")
                        S.act(lambda ps=ps, of=of: nc.scalar.copy(of.t[:, :w], ps.t[:, :w]), r=[ps], w=[of])
                        S.dma("sp", GB[idx, :, j * 512: j * 512 + w], of.t[:, :w], of, r=[of])
                        continue
                    if kk == "gc":
                        of = nxt(stf, "stf")
                        S.dve(lambda ps=ps, of=of, idx=idx: nc.vector.tensor_tensor(
                            of.t[:, :w], ps.t[:, :w], ub.t[:, idx, :w], op=ALU.mult), r=[ps, ub], w=[of])
                        S.dma("sp", GCU[idx, :, 1 + j * 512: 1 + j * 512 + w], of.t[:, :w], of, r=[of])
                        if j == 0:
                            S.dve(lambda of=of, idx=idx: nc.vector.tensor_copy(hxs.t[:, idx:idx + 1], of.t[:, 0:1]),
                                  r=[of, hxs], w=[hxs])
                        if j == NT - 1:
                            S.dve(lambda of=of, idx=idx: nc.vector.tensor_copy(
                                hxs.t[:, KC + idx:KC + idx + 1], of.t[:, w - 1:w]), r=[of, hxs], w=[hxs])
                        continue
                    ob = nxt(stg, "stg")
                    if kk == "aq":
                        qk_post(ps, w, None, False, None, SC, ob)
                        dst = QT0[idx, :, (j * 512 if kind == "lat" else T): (j * 512 if kind == "lat" else T) + w]
                    elif kk == "bq":
                        qk_post(ps, w, gsc.t[:, 0:1], True, rope, None, ob)
                        dst = QT0[8 + idx, :, (j * 512 if kind == "lat" else T): (j * 512 if kind == "lat" else T) + w]
                    elif kk == "ak":
                        qk_post(ps, w, None, False, None, 1.0, ob)
                        if kind == "lat":
                            dst = KaT[idx, :, 256 + j * 512: 256 + j * 512 + w]
                        elif kind == "ctx":
                            dst = KaTc[idx, :, :]
                        else:
                            S.dma("sp", KaT[idx, :, 0:256], ob.t[:, 0:256], ob, r=[ob])
                            dst = None
                            S.dma("sp", KaT[idx, :, T + 256: T + 512], ob.t[:, 256:512], ob, r=[ob])
                    elif kk == "bk":
                        qk_post(ps, w, gsc.t[:, 1:2], True, rope, None, ob)
                        dst = KbT_src[idx][:, j * 512: j * 512 + w] if kind == "lat" else KbTc[idx, :, :]
                    elif kk == "dq":
                        qk_post(ps, w, None, False, rope, SC, ob)
                        dst = QT1[idx, :, j * 512: j * 512 + w]
                    elif kk == "dk":
                        qk_post(ps, w, None, False, rope, 1.0, ob)
                        dst = KdT_src[idx][:, j * 512: j * 512 + w] if kind == "lat" else KdTc[idx, :, :]
                    if dst is not None:
                        S.dma("sp", dst, ob.t[:, :w], ob, r=[ob])
        S.barrier()
        es.close()

    def chunk_kind(l, c):
        if l == 0:
            if c < 8:
                return ("aq", c)
            if c < 16:
                return ("bq", c - 8)
            if c < 24:
                return ("ak", c - 16)
            if c < 32:
                return ("av", c - 24)
            if c < 34:
                return ("bk", c - 32)
            return ("bv", c - 34)
        if c < KC:
            return ("u", c)
        if c < 2 * KC:
            return ("gb", c - KC)
        if c < 3 * KC:
            return ("gc", c - 2 * KC)
        c -= 3 * KC
        if c < 8:
            return ("dq", c)
        if c < 16:
            return ("dk", c - 8)
        return ("dv", (c - 16) // 2)

    def outproj(es, l, cur, OT, nk, w, col0, tt, first, last, xr, wts, wctr):
        wm = W["out%d" % l]
        for g in range(wm.G):
            wt = wts[wctr[0] % len(wts)]
            wctr[0] += 1
            load_w(wm, wt, g)
            nm = wm.gw // 128
            x = xr[g % len(xr)]
            S.dma("sp", x.t[:, :nm, :w], XTv(cur)[:, g * nm:(g + 1) * nm, col0: col0 + w], x, w=[x])
            for mi in range(nm):
                m = g * nm + mi
                ps = nxt(PS[0:4], "ps")
                for k in range(nk):
                    S.pe(lambda k=k, ps=ps, wt=wt, mi=mi: nc.tensor.matmul(
                        ps.t[:, :w], lhsT=wm.lhs(wt, k, mi * 128, (mi + 1) * 128), rhs=OT.t[:, k, :w],
                        start=(k == 0), stop=(k == nk - 1)), r=[OT, wt], w=[ps])
                S.dve(lambda ps=ps, x=x, mi=mi, m=m: nc.vector.scalar_tensor_tensor(
                    x.t[:, mi, :w], ps.t[:, :w], DRV(l, tt, 2, m), x.t[:, mi, :w], op0=ALU.mult, op1=ALU.add),
                    r=[ps, x, drv], w=[x])
                if first:
                    S.dve(lambda x=x, mi=mi, m=m: nc.vector.tensor_copy(hxs.t[:, m:m + 1], x.t[:, mi, 0:1]),
                          r=[x, hxs], w=[hxs])
                if last:
                    S.dve(lambda x=x, mi=mi, m=m: nc.vector.tensor_copy(hxs.t[:, KD + m:KD + m + 1], x.t[:, mi, w - 1:w]),
                          r=[x, hxs], w=[hxs])
            S.dma("sp", XTv(cur)[:, g * nm:(g + 1) * nm, col0: col0 + w], x.t[:, :nm, :w], x, r=[x])

    def put_x_halo(cur):
        def put(left, right):
            S.dma("sp", XTv(cur)[:, :, LAT0 - 1:LAT0], left, hxo, r=[hxo])
            S.dma("sp", XTv(cur)[:, :, LAT0 + T:LAT0 + T + 1], right, hxo, r=[hxo])
        return put

    def mixer0(cur):
        l = 0
        es = ExitStack()
        wm = W["out0"]
        qt = S.sb("m_qt", [128, 16, 512], BF16, es)
        OT = S.sb("m_ot", [128, 16, 512], BF16, es)
        xr = [S.sb("m_xr%d" % i, [128, wm.gw // 128, 512], F32, es) for i in range(2)]
        wts = [S.sb("m_w%d" % i, wm.tile_shape(), BF16, es) for i in range(2)]
        kna = [S.sb("m_kna%d" % i, [128, 1024], BF16, es) for i in range(2)]
        vna = [S.sb("m_vna%d" % i, [128, 8, 128], BF16, es) for i in range(2)]
        tab = [S.sb("m_tab%d" % i, [128, 2, 1536], BF16, es) for i in range(2)]
        kch = [S.sb("m_kch%d" % i, [128, T], BF16, es) for i in range(2)]
        vch = [S.sb("m_vch%d" % i, [128, NKT, 128], BF16, es) for i in range(2)]
        kac = S.sb("m_kac", [128, 8, 256], BF16, es)
        vac = S.sb("m_vac", [128, 8, 2, 128], BF16, es)
        kbc = S.sb("m_kbc", [128, 2, 256], BF16, es)
        vbc = S.sb("m_vbc", [128, 2, 2, 128], BF16, es)
        PTs = [S.sb("m_pt%d" % i, [128, 512], BF16, es) for i in range(4)]
        rinv = S.sb("m_rinv", [128, 512], F32, es)
        S.dma("sp", kac.t[:], KaTc.ap().rearrange("h p t -> p h t"), kac, w=[kac])
        S.dma("sp", vac.t[:], Vac[:, :, :, :], vac, w=[vac])
        S.dma("sp", kbc.t[:], KbTc.ap().rearrange("g p t -> p g t"), kbc, w=[kbc])
        S.dma("sp", vbc.t[:], Vbc[:, :, :, :], vbc, w=[vbc])
        Ob, Lb, Sb = [[PS[4]]], [PS[5]], [PS[0], PS[1], PS[2], PS[3]]
        wctr = [0]
        ci = [0]

        def finish(Wq, dst_ap):
            S.dve(lambda: nc.vector.reciprocal(rinv.t[:, :Wq], Lb[0].t[:, :Wq]), r=[Lb[0]], w=[rinv])
            S.dve(lambda: nc.vector.tensor_tensor(dst_ap, Ob[0][0].t[:, :Wq], rinv.t[:, :Wq], op=ALU.mult),
                  r=[Ob[0][0], rinv], w=[OT])

        def ctx_keys_a(h):
            return [dict(K=[kac.t[:, h, kt * 128:(kt + 1) * 128]], V=[vac.t[:, h, kt, :]], bias=None, deps=[kac, vac])
                    for kt in range(2)]

        def ctx_keys_b(g):
            return [dict(K=[kbc.t[:, g, kt * 128:(kt + 1) * 128]], V=[vbc.t[:, g, kt, :]], bias=None, deps=[kbc, vbc])
                    for kt in range(2)]

        for j in range(NT):
            S.dma("sp", qt.t[:], QT0.ap().rearrange("h p t -> p h t")[:, :, j * 512:(j + 1) * 512], qt, w=[qt])
            for h in range(8):
                kn, vn, tb = kna[h % 2], vna[h % 2], tab[h % 2]
                S.dma("sp", kn.t[:], KaT[h, :, j * 512: j * 512 + 1024], kn, w=[kn])
                S.dma("sp", vn.t[:], Va[:, h, j * 4: j * 4 + 8, :], vn, w=[vn])
                for sub in range(2):
                    i = 2 * j + sub
                    var = 0 if i == 0 else (2 if i == 2 * NT - 1 else 1)
                    S.dma("sp", tb.t[:, sub, :], NAtab[var, h], tb, w=[tb])
                for sub in range(2):
                    keys = ctx_keys_a(h)
                    for t in range(6):
                        kt = 2 * sub + t
                        keys.append(dict(K=[kn.t[:, kt * 128:(kt + 1) * 128]], V=[vn.t[:, kt, :]],
                                         bias=tb.t[:, sub, t * 256:(t + 1) * 256], deps=[kn, vn, tb]))
                    attn([qt.t[:, h, sub * 256:(sub + 1) * 256]], [qt], 256, keys, 1, Ob, Lb, Sb, PTs)
                    finish(256, OT.t[:, h, sub * 256:(sub + 1) * 256])
            for h in range(8):
                g = h // 4
                keys = ctx_keys_b(g)

                def ld(r, g=g):
                    kc_, vc_ = kch[r % 2], vch[r % 2]
                    S.dma("sp", kc_.t[:], KbT_all[g][r * 128:(r + 1) * 128, :], kc_, w=[kc_])
                    S.dma("sp", vc_.t[:], Vb_all[g][r * 128:(r + 1) * 128, :].rearrange("p (k d) -> p k d", d=128),
                          vc_, w=[vc_])
                ld(0)
                for r in range(4):
                    kc_, vc_ = kch[r % 2], vch[r % 2]
                    for kt in range(NKT):
                        keys.append(dict(K=[kc_.t[:, kt * 128:(kt + 1) * 128]], V=[vc_.t[:, kt, :]], bias=None,
                                         deps=[kc_, vc_],
                                         pre=((lambda r=r: ld(r + 1)) if (kt == 3 and r < 3) else None)))
                attn([qt.t[:, 8 + h, :]], [qt], 512, keys, 1, Ob, Lb, Sb, PTs)
                finish(512, OT.t[:, 8 + h, :])
            outproj(es, 0, cur, OT, 16, 512, LAT0 + j * 512, 0, j == 0, j == NT - 1, xr, wts, wctr)
        S.dma("sp", qt.t[:, :, 0:256], QT0.ap().rearrange("h p t -> p h t")[:, :, T:T + 256], qt, w=[qt])
        for h in range(8):
            attn([qt.t[:, h, 0:256]], [qt], 256, ctx_keys_a(h), 1, Ob, Lb, Sb, PTs)
            finish(256, OT.t[:, h, 0:256])
        for h in range(8):
            attn([qt.t[:, 8 + h, 0:256]], [qt], 256, ctx_keys_b(h // 4), 1, Ob, Lb, Sb, PTs)
            finish(256, OT.t[:, 8 + h, 0:256])
        outproj(es, 0, cur, OT, 16, 256, CTX0, 1, False, False, xr, wts, wctr)
        halo_exchange(KD, put_x_halo(cur))
        S.barrier()
        es.close()


    def ffn(l, cur):
        es = ExitStack()
        wg, wu, wd = W["g%d" % l], W["u%d" % l], W["d%d" % l]
        xe = S.sb("f_xe", [128, KD, 512], F32, es)
        h2 = S.sb("f_h2", [128, KD, 512], BF16, es)
        wgt = [S.sb("f_wg%d" % i, wg.tile_shape(), BF16, es) for i in range(2)]
        wut = [S.sb("f_wu%d" % i, wu.tile_shape(), BF16, es) for i in range(2)]
        wdt = [S.sb("f_wd%d" % i, wd.tile_shape(), BF16, es) for i in range(2)]
        actT = S.sb("f_act", [128, KF, 512], BF16, es)
        tb = [S.sb("f_tb%d" % i, [128, 512], F32, es) for i in range(2)]
        sbb = [S.sb("f_sb%d" % i, [128, 512], F32, es) for i in range(2)]
        FT = cfg.FT
        tiles = []
        s0 = 0
        while s0 < T:
            n = min(FT, T - s0)
            tiles.append(("lat", s0, n))
            s0 += n
        if l < L - 1:
            tiles.append(("ctx", 0, 256))
        wi = 0

        def fcw(i, f):
            o = sm_off["fcw"][0] + (l * 3 + i) * KF + f
            return smt.t[:, o:o + 1]

        def fcb(f):
            o = sm_off["fcb"][0] + l * KF + f
            return smt.t[:, o:o + 1]

        def fx(i, f):
            o = (l * 4 + i) * KF + f
            return fixw.t[:, o:o + 1]

        for ti, (kind, s0, n) in enumerate(tiles):
            tt = 0 if kind == "lat" else 1
            c0 = (LAT0 + s0 - 1) if kind == "lat" else (CTX0 - 1)
            cw = n + 2
            S.dma("sp", xe.t[:, :, :cw], XTv(cur)[:, :, c0:c0 + cw], xe, w=[xe])
            norm_mod(xe, cw, lambda k: DRV(l, tt, 3, k), lambda k: DRV(l, tt, 4, k), h2)
            for g in range(wg.G):
                a, b = wgt[wi % 2], wut[wi % 2]
                wi += 1
                load_w(wg, a, g)
                load_w(wu, b, g)
                for ci in range(wg.gw // 128):
                    f = g * (wg.gw // 128) + ci
                    pg = PS[(2 * f) % 6]
                    pu = PS[(2 * f + 1) % 6]
                    for k in range(KD):
                        S.pe(lambda k=k, a=a, pg=pg, ci=ci: nc.tensor.matmul(
                            pg.t[:, :cw], lhsT=wg.lhs(a, k, ci * 128, (ci + 1) * 128), rhs=h2.t[:, k, :cw],
                            start=(k == 0), stop=(k == KD - 1)), r=[a, h2], w=[pg])
                    for k in range(KD):
                        S.pe(lambda k=k, b=b, pu=pu, ci=ci: nc.tensor.matmul(
                            pu.t[:, :cw], lhsT=wu.lhs(b, k, ci * 128, (ci + 1) * 128), rhs=h2.t[:, k, :cw],
                            start=(k == 0), stop=(k == KD - 1)), r=[b, h2], w=[pu])
                    t_ = tb[f % 2]
                    s_ = sbb[f % 2]
                    S.act(lambda pg=pg, t_=t_, f=f: nc.scalar.activation(
                        out=t_.t[:, :n], in_=pg.t[:, 1:n + 1], func=AF.Identity, scale=fcw(1, f), bias=fcb(f)),
                        r=[pg, smt], w=[t_])
                    S.dve(lambda pg=pg, t_=t_, f=f: nc.vector.scalar_tensor_tensor(
                        t_.t[:, :n], pg.t[:, 0:n], fcw(0, f), t_.t[:, :n], op0=ALU.mult, op1=ALU.add),
                        r=[pg, smt, t_], w=[t_])
                    S.dve(lambda pg=pg, t_=t_, f=f: nc.vector.scalar_tensor_tensor(
                        t_.t[:, :n], pg.t[:, 2:n + 2], fcw(2, f), t_.t[:, :n], op0=ALU.mult, op1=ALU.add),
                        r=[pg, smt, t_], w=[t_])
                    fl = (0 if kind == "lat" else 2) if (kind == "ctx" or s0 == 0) else None
                    fr = (1 if kind == "lat" else 3) if (kind == "ctx" or s0 + n == T) else None
                    if fl is not None:
                        S.dve(lambda pg=pg, t_=t_, f=f, fl=fl: nc.vector.scalar_tensor_tensor(
                            t_.t[:, 0:1], pg.t[:, 0:1], fx(fl, f), t_.t[:, 0:1], op0=ALU.mult, op1=ALU.add),
                            r=[pg, fixw, t_], w=[t_])
                    if fr is not None:
                        S.dve(lambda pg=pg, t_=t_, f=f, fr=fr: nc.vector.scalar_tensor_tensor(
                            t_.t[:, n - 1:n], pg.t[:, n + 1:n + 2], fx(fr, f), t_.t[:, n - 1:n], op0=ALU.mult,
                            op1=ALU.add), r=[pg, fixw, t_], w=[t_])
                    S.act(lambda t_=t_, s_=s_: nc.scalar.activation(out=s_.t[:, :n], in_=t_.t[:, :n], func=AF.Silu),
                          r=[t_], w=[s_])
                    S.dve(lambda s_=s_, pu=pu, f=f: nc.vector.tensor_tensor(
                        actT.t[:, f, :n], s_.t[:, :n], pu.t[:, 1:n + 1], op=ALU.mult), r=[s_, pu], w=[actT])
            for g in range(wd.G):
                d_ = wdt[g % 2]
                load_w(wd, d_, g)
                for mi in range(wd.gw // 128):
                    m = g * (wd.gw // 128) + mi
                    ps = PS[m % 6]
                    for f in range(KF):
                        S.pe(lambda f=f, d_=d_, ps=ps, mi=mi: nc.tensor.matmul(
                            ps.t[:, :n], lhsT=wd.lhs(d_, f, mi * 128, (mi + 1) * 128), rhs=actT.t[:, f, :n],
                            start=(f == 0), stop=(f == KF - 1)), r=[d_, actT], w=[ps])
                    S.dve(lambda ps=ps, m=m: nc.vector.scalar_tensor_tensor(
                        xe.t[:, m, 1:n + 1], ps.t[:, :n], DRV(l, tt, 5, m), xe.t[:, m, 1:n + 1], op0=ALU.mult,
                        op1=ALU.add), r=[ps, xe, drv], w=[xe])
            oc = (LAT0 + s0) if kind == "lat" else CTX0
            S.dma("sp", XTv(1 - cur)[:, :, oc:oc + n], xe.t[:, :, 1:n + 1], xe, r=[xe])
        S.barrier()
        es.close()

    def mixer1(cur):
        l = 1
        es = ExitStack()
        wm = W["out1"]
        NK1 = KC + 8
        qt = S.sb("n_qt", [128, 8, 512], BF16, es)
        OT = S.sb("n_ot", [128, NK1, 512], BF16, es)
        xr = [S.sb("n_xr%d" % i, [128, wm.gw // 128, 512], F32, es) for i in range(1)]
        wts = [S.sb("n_w%d" % i, wm.tile_shape(), BF16, es) for i in range(1)]
        gcu = S.sb("n_gcu", [128, KC, 514], F32, es)
        gb = S.sb("n_gb", [128, KC, 512], F32, es)
        kch = [S.sb("n_kch%d" % i, [128, 2, T], BF16, es) for i in range(2)]
        vch = [S.sb("n_vch%d" % i, [128, NKT, 256], BF16, es) for i in range(2)]
        kdc = S.sb("n_kdc", [128, 8, 256], BF16, es)
        vdc = S.sb("n_vdc", [128, 4, 2, 256], BF16, es)
        PTs = [S.sb("n_pt%d" % i, [128, 512], BF16, es) for i in range(4)]
        rinv = S.sb("n_rinv", [128, 512], F32, es)
        o1 = S.sb("n_o1", [128, 512], F32, es)
        o2 = S.sb("n_o2", [128, 512], F32, es)
        S.dma("sp", kdc.t[:], KdTc.ap().rearrange("h p t -> p h t"), kdc, w=[kdc])
        S.dma("sp", vdc.t[:], Vdc[:, :, :, :], vdc, w=[vdc])
        Ob, Lb, Sb = [[PS[2], PS[3]], [PS[4], PS[5]]], [PS[6], PS[7]], [PS[0], PS[1]]
        wctr = [0]

        def scw(i, c):
            o = sm_off["scw"][0] + i * KC + c
            return smt.t[:, o:o + 1]

        for j in range(NT):
            S.dma("sp", qt.t[:], QT1.ap().rearrange("h p t -> p h t")[:, :, j * 512:(j + 1) * 512], qt, w=[qt])
            S.dma("sp", gcu.t[:], GCU.ap().rearrange("c p t -> p c t")[:, :, j * 512: j * 512 + 514], gcu, w=[gcu])
            S.dma("sp", gb.t[:], GB.ap().rearrange("c p t -> p c t")[:, :, j * 512:(j + 1) * 512], gb, w=[gb])
            for c in range(KC):
                S.act(lambda c=c: nc.scalar.activation(out=tmp.t[:, :512], in_=gcu.t[:, c, 1:513], func=AF.Copy,
                                                       scale=1.0), r=[gcu], w=[tmp])
                S.dve(lambda c=c: nc.vector.tensor_scalar(tmp.t[:, :512], tmp.t[:, :512], scw(1, c), None, op0=ALU.mult),
                      r=[tmp, smt], w=[tmp])
                S.dve(lambda c=c: nc.vector.scalar_tensor_tensor(
                    tmp.t[:, :512], gcu.t[:, c, 0:512], scw(0, c), tmp.t[:, :512], op0=ALU.mult, op1=ALU.add),
                    r=[gcu, smt, tmp], w=[tmp])
                S.dve(lambda c=c: nc.vector.scalar_tensor_tensor(
                    tmp.t[:, :512], gcu.t[:, c, 2:514], scw(2, c), tmp.t[:, :512], op0=ALU.mult, op1=ALU.add),
                    r=[gcu, smt, tmp], w=[tmp])
                S.dve(lambda c=c: nc.vector.tensor_tensor(OT.t[:, c, :], tmp.t[:, :512], gb.t[:, c, :], op=ALU.mult),
                      r=[tmp, gb], w=[OT])
            for h in range(4):
                for sub in range(2):
                    qs = [qt.t[:, 2 * h + c, sub * 256:(sub + 1) * 256] for c in range(2)]
                    keys = [dict(K=[kdc.t[:, 2 * h + c, kt * 128:(kt + 1) * 128] for c in range(2)],
                                 V=[vdc.t[:, h, kt, dv * 128:(dv + 1) * 128] for dv in range(2)], bias=None,
                                 deps=[kdc, vdc]) for kt in range(2)]

                    def ld(r, h=h):
                        kc_, vc_ = kch[r % 2], vch[r % 2]
                        for c in range(2):
                            S.dma("sp", kc_.t[:, c, :], KdT_all[2 * h + c][r * 128:(r + 1) * 128, :], kc_, w=[kc_])
                        for hf in range(2):
                            S.dma("sp", vc_.t[:, hf * NKH:(hf + 1) * NKH, :],
                                  Vd_all[2 * h + hf][r * 128:(r + 1) * 128, :].rearrange("p (k d) -> p k d", d=256),
                                  vc_, w=[vc_])
                    ld(0)
                    for r in range(4):
                        kc_, vc_ = kch[r % 2], vch[r % 2]
                        for kt in range(NKT):
                            keys.append(dict(K=[kc_.t[:, c, kt * 128:(kt + 1) * 128] for c in range(2)],
                                             V=[vc_.t[:, kt, dv * 128:(dv + 1) * 128] for dv in range(2)], bias=None,
                                             deps=[kc_, vc_],
                                             pre=((lambda r=r: ld(r + 1)) if (kt == 3 and r < 3) else None)))
                    attn(qs, [qt], 256, keys, 2, Ob, Lb, Sb, PTs)
                    for c in range(2):
                        S.dve(lambda c=c: nc.vector.reciprocal(rinv.t[:, c * 256:(c + 1) * 256], Lb[c].t[:, 0:256]),
                              r=[Lb[c], rinv], w=[rinv])
                    for dv in range(2):
                        S.dve(lambda dv=dv: nc.vector.tensor_tensor(
                            o1.t[:, dv * 256:(dv + 1) * 256], Ob[0][dv].t[:, 0:256], rinv.t[:, 0:256],
                            op=ALU.mult), r=[Ob[0][dv], rinv, o1], w=[o1])
                        S.dve(lambda dv=dv: nc.vector.tensor_tensor(
                            o2.t[:, dv * 256:(dv + 1) * 256], Ob[1][dv].t[:, 0:256], rinv.t[:, 256:512],
                            op=ALU.mult), r=[Ob[1][dv], rinv, o2], w=[o2])
                    S.dve(lambda: nc.vector.scalar_tensor_tensor(
                        o1.t[:, :512], o2.t[:, :512], lamt.t[:, 5:6], o1.t[:, :512], op0=ALU.mult, op1=ALU.add),
                        r=[o1, o2, lamt], w=[o1])
                    S.act(lambda: nc.scalar.activation(out=sqb.t[:, :512], in_=o1.t[:, :512], func=AF.Square),
                          r=[o1], w=[sqb])
                    aux = PS[0]
                    for dv in range(2):
                        S.pe(lambda dv=dv: nc.tensor.matmul(aux.t[:, :256], lhsT=onesb.t[:],
                                                            rhs=sqb.t[:, dv * 256:(dv + 1) * 256],
                                                            start=(dv == 0), stop=(dv == 1)), r=[sqb, onesb], w=[aux])
                    S.act(lambda: nc.scalar.activation(out=rstd.t[:, :256], in_=aux.t[:, :256], func=AF.Sqrt,
                                                       scale=1.0 / 256, bias=epst.t[:, 0:1]), r=[aux, epst], w=[rstd])
                    S.dve(lambda: nc.vector.reciprocal(rstd.t[:, :256], rstd.t[:, :256]), r=[rstd], w=[rstd])
                    for dv in range(2):
                        S.dve(lambda dv=dv: nc.vector.tensor_tensor(
                            tmp2.t[:, :256], o1.t[:, dv * 256:(dv + 1) * 256], rstd.t[:, :256], op=ALU.mult),
                            r=[o1, rstd], w=[tmp2])
                        S.act(lambda dv=dv, h=h, sub=sub: nc.scalar.activation(
                            out=OT.t[:, KC + 2 * h + dv, sub * 256:(sub + 1) * 256], in_=tmp2.t[:, :256],
                            func=AF.Identity, scale=lamt.t[:, 6 + dv:7 + dv]), r=[tmp2, lamt], w=[OT])
            outproj(es, 1, cur, OT, NK1, 512, LAT0 + j * 512, 0, j == 0, j == NT - 1, xr, wts, wctr)
        halo_exchange(KD, put_x_halo(cur))
        S.barrier()
        es.close()

    def final(cur, nonorm=False):
        es = ExitStack()
        xt = [S.sb("o_xt%d" % i, [128, KD, 512], F32, es) for i in range(2)]
        yT = S.sb("o_y", [128, KD, 512], F32, es)
        ot = [S.sb("o_ot%d" % i, [128, D], F32, es) for i in range(2)]
        oi = 0
        for j in range(NT):
            x = xt[j % 2]
            S.dma("sp", x.t[:], XTv(cur)[:, :, LAT0 + j * 512: LAT0 + (j + 1) * 512], x, w=[x])
            ps = PS[7]
            for k in range(KD):
                S.act(lambda k=k, x=x: nc.scalar.activation(out=sq.t[:, :512], in_=x.t[:, k, :], func=AF.Square),
                      r=[x], w=[sq])
                S.pe(lambda k=k: nc.tensor.matmul(ps.t[:, :512], lhsT=onesf.t[:], rhs=sq.t[:, :512],
                                                  start=(k == 0), stop=(k == KD - 1)), r=[sq, onesf], w=[ps])
            S.act(lambda: nc.scalar.activation(out=rstd.t[:, :512], in_=ps.t[:, :512], func=AF.Sqrt, scale=1.0 / D,
                                               bias=epst.t[:, 0:1]), r=[ps, epst], w=[rstd])
            S.dve(lambda: nc.vector.reciprocal(rstd.t[:, :512], rstd.t[:, :512]), r=[rstd], w=[rstd])
            for k in range(KD):
                if nonorm:
                    S.dve(lambda k=k, x=x: nc.vector.tensor_copy(yT.t[:, k, :], x.t[:, k, :]), r=[x], w=[yT])
                    continue
                S.dve(lambda k=k, x=x: nc.vector.scalar_tensor_tensor(
                    yT.t[:, k, :], x.t[:, k, :], sm("fg", k), rstd.t[:, :512], op0=ALU.mult, op1=ALU.mult),
                    r=[x, rstd, smt], w=[yT])
            for s in range(4):
                o = ot[oi % 2]
                oi += 1
                for k4 in range(0, KD, 4):
                    pb = PS[(k4 // 4) % 6]
                    for kk in range(min(4, KD - k4)):
                        k = k4 + kk
                        S.pe(lambda k=k, kk=kk, pb=pb, s=s: nc.tensor.transpose(
                            pb.t[:, kk * 128:(kk + 1) * 128], yT.t[:, k, s * 128:(s + 1) * 128], ident),
                            r=[yT, cst], w=[pb])
                    wd_ = min(4, KD - k4) * 128
                    if (k4 // 4) % 2 == 0:
                        S.dve(lambda k4=k4, pb=pb, o=o, wd_=wd_: nc.vector.tensor_copy(
                            o.t[:, k4 * 128:k4 * 128 + wd_], pb.t[:, :wd_]), r=[pb], w=[o])
                    else:
                        S.act(lambda k4=k4, pb=pb, o=o, wd_=wd_: nc.scalar.copy(
                            o.t[:, k4 * 128:k4 * 128 + wd_], pb.t[:, :wd_]), r=[pb], w=[o])
                S.dma("sp", out_tm[j * 512 + s * 128: j * 512 + (s + 1) * 128, :], o.t[:], o, r=[o])
        S.barrier()
        es.close()

    def gather_kv0():
        S.barrier(pool=True)
        for i in range(2):
            S.coll("AllGather", ALU.bypass, GRP, KbT_src[i].ap().opt(), KbT_all[i].ap().opt(), collb)
            S.coll("AllGather", ALU.bypass, GRP, Vb_src[i].ap().opt(), Vb_all[i].ap().opt(), collb)
        S.barrier(pool=True)
        stage_weights(["in1", "out1", "g1", "u1", "d1"])

    def gather_kv1():
        S.barrier(pool=True)
        for i in range(8):
            S.coll("AllGather", ALU.bypass, GRP, KdT_src[i].ap().opt(), KdT_all[i].ap().opt(), collb)
            S.coll("AllGather", ALU.bypass, GRP, Vd_src[i].ap().opt(), Vd_all[i].ap().opt(), collb)

        def put(left, right):
            S.dma("sp", GCU.ap().rearrange("c p t -> p c t")[:, :, 0:1], left, hxo, r=[hxo])
            S.dma("sp", GCU.ap().rearrange("c p t -> p c t")[:, :, T + 1:T + 2], right, hxo, r=[hxo])
        halo_exchange(KC, put)
        S.barrier(pool=True)

    stop = cfg.stop if hasattr(cfg, "stop") else 99
    inproj(0, 0)
    if STOP == 7:
        return nc, W
    gather_kv0()
    if STOP == 8:
        return nc, W
    mixer0(0)
    if STOP == 9:
        if DBG:
            final(0, nonorm=True)
        return nc, W
    ffn(0, 0)
    if STOP == 10:
        if DBG:
            final(1, nonorm=True)
        return nc, W
    inproj(1, 1)
    if STOP == 11:
        return nc, W
    gather_kv1()
    if STOP == 12:
        return nc, W
    mixer1(1)
    if STOP == 13:
        if DBG:
            final(1, nonorm=True)
        return nc, W
    ffn(1, 1)
    if STOP == 14:
        if DBG:
            final(0, nonorm=True)
        return nc, W
    final(0)
    return nc, W


def small_layout(cfg):
    KD, KF, KC, L, NCH6 = cfg.KD, cfg.KF, cfg.KC, cfg.L, cfg.NCH6
    items = [("cv", KD * 2), ("m8", 8), ("mb", 2), ("mL", 4), ("mR", 4), ("nhl", 1), ("nhr", 1),
             ("istop", 1), ("ntop", 1), ("isbot", 1), ("nbot", 1),
             ("adab", L * NCH6), ("ng", L * 2 * KD), ("fg", KD), ("qg", 1), ("kg", 1),
             ("lq1", 1), ("lk1", 1), ("lq2", 1), ("lk2", 1), ("slg", 2), ("scw", 3 * KC),
             ("fcw", L * 3 * KF), ("fcb", L * KF)]
    off, o = {}, 0
    for n, w in items:
        off[n] = (o, w)
        o += w
    return off, o


def _pl(v, nch):
    return np.ascontiguousarray(np.asarray(v, np.float32).reshape(nch, 128).T)


def host_inputs(cfg, inp):
    D, DFF, T, KD, KF, KC, L, NCH6 = cfg.D, cfg.DFF, cfg.T, cfg.KD, cfg.KF, cfg.KC, cfg.L, cfg.NCH6
    f = lambda a: np.asarray(a, np.float32)
    x, c, ctx, c_ctx = f(inp["x"]), f(inp["c"]), f(inp["ctx"]), f(inp["c_ctx"])
    off, smw = small_layout(cfg)
    ident = np.eye(128, dtype=np.float32)
    Rm = np.zeros((128, 128), np.float32)
    for do in range(128):
        if (do % 64) < 32:
            Rm[do + 32, do] = -1.0
        else:
            Rm[do - 32, do] = 1.0
    consts = np.concatenate([ident, Rm], axis=1)
    inv = (1.0 / (10000.0 ** (np.arange(0, 64, 2, dtype=np.float32) / 64.0))).astype(np.float32)
    rpb = f(inp["na_rpb"])[0]
    qc = np.arange(64)
    cs = np.clip(qc - 8, 0, 48)
    kc = np.arange(64)
    valid = (kc[:, None] >= cs[None, :]) & (kc[:, None] < cs[None, :] + 16)
    cidx = np.clip(kc[:, None] - qc[None, :] + 15, 0, 30)
    braw = np.empty((8, 15, 64, 64), np.float32)
    for nd in range(15):
        braw[:, nd] = np.where(valid[None], rpb[:, 14 - nd][:, cidx], np.float32(NEG))
    wsrc = {"in0": f(inp["ab_w_in"])[0], "out0": f(inp["ab_w_out"])[0], "in1": f(inp["cd_w_in"])[0],
            "out1": f(inp["cd_w_out"])[0]}
    for l in range(L):
        wsrc["g%d" % l] = f(inp["ffn_w_gate"])[l]
        wsrc["u%d" % l] = f(inp["ffn_w_up"])[l]
        wsrc["d%d" % l] = f(inp["ffn_w_down"])[l]
    ada_w = f(inp["ada_w"])
    maps = []
    for core in range(8):
        b, q = core // 4, core % 4
        m = {}
        m["x_tm"] = np.ascontiguousarray(x[b, q * T:(q + 1) * T])
        xh = np.zeros((512, D), np.float32)
        if q > 0:
            xh[0:256] = x[b, q * T - 256:q * T]
        if q < 3:
            xh[256:512] = x[b, (q + 1) * T:(q + 1) * T + 256]
        m["xh_tm"] = xh
        m["ctx_tm"] = np.ascontiguousarray(ctx[b])
        sm = np.zeros((128, smw), np.float32)

        def put(name, arr):
            o, w = off[name]
            sm[:, o:o + w] = np.asarray(arr, np.float32).reshape(128, w)
        cv = np.stack([_pl(c[b], KD), _pl(c_ctx, KD)], axis=2)
        put("cv", cv.reshape(128, KD * 2))
        e8 = np.zeros(8, np.float32); e8[core] = 1
        put("m8", np.tile(e8, (128, 1)))
        eb = np.zeros(2, np.float32); eb[b] = 1
        put("mb", np.tile(eb, (128, 1)))
        el = np.zeros(4, np.float32); er = np.zeros(4, np.float32)
        if q > 0:
            el[q - 1] = 1
        if q < 3:
            er[q + 1] = 1
        put("mL", np.tile(el, (128, 1))); put("mR", np.tile(er, (128, 1)))
        put("nhl", np.full((128, 1), 1.0 if q == 0 else 0.0)); put("nhr", np.full((128, 1), 1.0 if q == 3 else 0.0))
        put("istop", np.full((128, 1), 1.0 if q == 0 else 0.0)); put("ntop", np.full((128, 1), 0.0 if q == 0 else 1.0))
        put("isbot", np.full((128, 1), 1.0 if q == 3 else 0.0)); put("nbot", np.full((128, 1), 0.0 if q == 3 else 1.0))
        put("adab", np.concatenate([_pl(f(inp["ada_b"])[l], NCH6) for l in range(L)], axis=1))
        put("ng", np.concatenate([_pl(f(inp["norm_g"])[l, i], KD) for l in range(L) for i in range(2)], axis=1))
        put("fg", _pl(f(inp["final_g"]), KD))
        put("qg", f(inp["gqa_q_gain"])[0].reshape(128, 1)); put("kg", f(inp["gqa_k_gain"])[0].reshape(128, 1))
        for nm in ("lq1", "lk1", "lq2", "lk2"):
            put(nm, f(inp["diff_" + nm])[0].reshape(128, 1))
        put("slg", _pl(f(inp["diff_subln_g"])[0], 2))
        put("scw", np.concatenate([_pl(f(inp["sconv_w"])[0, j], KC) for j in range(3)], axis=1))
        put("fcw", np.concatenate([_pl(f(inp["ffn_conv_w"])[l, j], KF) for l in range(L) for j in range(3)], axis=1))
        put("fcb", np.concatenate([_pl(f(inp["ffn_conv_b"])[l], KF) for l in range(L)], axis=1))
        m["small"] = sm
        t = np.arange(q * T, (q + 1) * T)
        ang_r = (t // GW_).astype(np.float32)[:, None] * inv
        ang_c = (t % GW_).astype(np.float32)[:, None] * inv
        ang = np.concatenate([ang_r, ang_r, ang_c, ang_c], axis=-1)
        m["cosT"] = np.ascontiguousarray(np.cos(ang).astype(np.float32).T)
        m["sinT"] = np.ascontiguousarray(np.sin(ang).astype(np.float32).T)
        m["consts"] = consts
        m["braw"] = braw
        ncs = 6 * D // 4
        m["adaw"] = np.ascontiguousarray(ada_w[:, :, q * ncs:(q + 1) * ncs])
        maps.append(m)
    return maps, wsrc


_CACHE = {}


def run(cfg, inp):
    key = (cfg.D, cfg.DFF, cfg.SEQ, cfg.CW)
    if key not in _CACHE:
        _CACHE[key] = build(cfg)
    nc, W = _CACHE[key]
    maps, wsrc = host_inputs(cfg, inp)
    for core in range(8):
        q = core % 4
        for k, wm in W.items():
            maps[core][wm.name + "_f32"] = wm.host(wsrc[k], q)
    res = run_bass_kernel_spmd(nc, maps, core_ids=list(range(8)))
    T = cfg.T
    out = np.empty((2, cfg.SEQ, cfg.D), np.float32)
    for core in range(8):
        b, q = core // 4, core % 4
        out[b, q * T:(q + 1) * T] = np.asarray(res.results[core]["out_tm"], np.float32)
    return out


def kernel(**inputs):
    return run(Cfg(), inputs)
```

```python
import math
from contextlib import ExitStack
import numpy as np
import concourse.bass as bass
import concourse.mybir as mybir
from concourse.bass_utils import run_bass_kernel_spmd

F32 = mybir.dt.float32
BF16 = mybir.dt.bfloat16
AF = mybir.ActivationFunctionType
ALU = mybir.AluOpType
EPS = 1e-6
NEG = -30000.0
GW_ = 64


class Cfg:
    def __init__(self, D=2048, DFF=5632, SEQ=16384, CW=1024):
        self.D, self.DFF, self.SEQ, self.CW = D, DFF, SEQ, CW
        self.KD, self.KF, self.KC = D // 128, DFF // 128, CW // 128
        self.T = SEQ // 4
        self.NT = self.T // 512
        self.NCH6 = 6 * D // 128
        self.L = 2
        self.N0 = 4608
        self.N1 = 3 * CW + 3072
        self.FT = 510
        self.XW = self.T + 2 + 258


class Ev:
    __slots__ = ("sem", "val")

    def __init__(self, sem, val):
        self.sem, self.val = sem, val


class Buf:
    def __init__(self, t, name):
        self.t, self.name = t, name
        self.lw = {}
        self.rd = []
        self.dsem = None
        self.dcnt = 0


class Sched:
    def __init__(self, nc):
        self.nc = nc
        self.eng = {"pe": nc.tensor, "act": nc.scalar, "dve": nc.vector, "pool": nc.gpsimd, "sp": nc.sync}
        self.esem = {e: nc.alloc_semaphore(name="e_" + e) for e in self.eng}
        self.cnt = {e: 0 for e in self.eng}
        self.seen = {e: {} for e in self.eng}
        self.bufs = []
        self.nb = 0

    def buf(self, t, name=None):
        b = Buf(t, name or "b%d" % len(self.bufs))
        self.bufs.append(b)
        return b

    def sb(self, name, shape, dt, es=None):
        self.uid = getattr(self, "uid", 0) + 1
        name = "%s_%d" % (name, self.uid)
        if es is not None:
            return self.buf(es.enter_context(self.nc.sbuf_tensor(name, list(shape), dt)), name)
        return self.buf(self.nc.alloc_sbuf_tensor(name, list(shape), dt), name)

    def ps(self, name):
        return self.buf(self.nc.alloc_psum_tensor(name, [128, 512], F32), name)

    def _wait(self, e, ev):
        if e == "pe" and ev.sem is self.esem["pe"]:
            return
        k = id(ev.sem)
        if self.seen[e].get(k, 0) >= ev.val:
            return
        self.eng[e].wait_ge(ev.sem, ev.val)
        self.seen[e][k] = ev.val

    def _deps(self, e, r, w):
        for b in r:
            for ev in b.lw.values():
                self._wait(e, ev)
        for b in w:
            for ev in b.lw.values():
                self._wait(e, ev)
            for ev in b.rd:
                self._wait(e, ev)

    def _post(self, ev, key, r, w):
        for b in r:
            b.rd.append(ev)
            if len(b.rd) > 24:
                b.rd = b.rd[-24:] if False else b.rd
        for b in w:
            b.lw[key] = ev
            b.rd = []

    def _skip(self):
        import os, sys
        if not hasattr(self, "limit"):
            self.limit = int(os.environ.get("KLIMIT", "1000000000"))
            self.total = 0
            self.log = []
        self.total += 1
        if self.total > self.limit:
            return True
        f = sys._getframe(2)
        ln = [f.f_lineno]
        while f.f_back is not None and len(ln) < 4:
            f = f.f_back
            ln.append(f.f_lineno)
        self.log.append((self.total, ln))
        return False

    def op(self, e, fn, r=(), w=()):
        if self._skip():
            return None
        self._deps(e, r, w)
        ins = fn()
        self.cnt[e] += 1
        ins.then_inc(self.esem[e], 1)
        ev = Ev(self.esem[e], self.cnt[e])
        for b in r:
            b.rd = [x for x in b.rd if x.sem is not ev.sem]
        self._post(ev, e, r, w)
        return ins

    def pe(self, fn, r=(), w=()):
        return self.op("pe", fn, r, w)

    def act(self, fn, r=(), w=()):
        return self.op("act", fn, r, w)

    def dve(self, fn, r=(), w=()):
        return self.op("dve", fn, r, w)

    def pool(self, fn, r=(), w=()):
        return self.op("pool", fn, r, w)

    def dma(self, q, out, in_, sbuf, r=(), w=()):
        if self._skip():
            return None
        self._deps(q, r, w)
        if sbuf.dsem is None:
            sbuf.dsem = self.nc.alloc_semaphore(name="d_" + sbuf.name)
        ins = self.eng[q].dma_start(out=out, in_=in_, allow_slow_non_contiguous=True)
        ins.then_inc(sbuf.dsem, 16)
        sbuf.dcnt += 16
        ev = Ev(sbuf.dsem, sbuf.dcnt)
        for b in r:
            b.rd = [x for x in b.rd if x.sem is not ev.sem]
        self._post(ev, "d%d" % id(sbuf), r, w)
        return ins

    def coll(self, kind, op, groups, src, dst, cbuf, tok=None):
        e = "pool"
        if self._skip():
            return None
        if cbuf.dsem is None:
            cbuf.dsem = self.nc.alloc_semaphore(name="c_" + cbuf.name)
        ins = self.nc.gpsimd.collective_compute(kind, op, replica_groups=groups, ins=[src], outs=[dst])
        ins.then_inc(cbuf.dsem)
        cbuf.dcnt += 1
        self.nc.gpsimd.wait_ge(cbuf.dsem, cbuf.dcnt)
        tb = tok if tok is not None else cbuf
        self.op("pool", lambda: self.nc.gpsimd.memset(tb.t[:], 0.0), w=[tb])
        return ins

    def barrier(self, pool=False):
        engs = [e for e in self.eng if pool or e != "pool"]
        evs = [Ev(self.esem[e], self.cnt[e]) for e in engs if self.cnt[e] > 0]
        for b in self.bufs:
            if b.dsem is not None and b.dcnt > 0 and not (b.name.startswith("coll") or b.name.startswith("wcast")):
                evs.append(Ev(b.dsem, b.dcnt))
        for e in engs:
            for ev in evs:
                if ev.sem is self.esem[e]:
                    continue
                self._wait(e, ev)
        for b in self.bufs:
            if getattr(b, "keep", False):
                continue
            b.lw = {}
            b.rd = []
        self.nb += 1


class WMat:
    def __init__(self, nc, name, K, N, gw):
        self.name, self.K, self.N, self.gw = name, K, N, gw
        self.KCH = K // 128
        self.KL = self.KCH // 4
        assert self.KL * 4 == self.KCH and N % gw == 0
        self.G = N // gw
        self.rows, self.cols = self.G * 128, self.KL * gw
        self.src = nc.dram_tensor(name + "_f32", [self.rows, self.cols], F32, kind="ExternalInput")
        self.gpc = max(1, (1 << 20) // (128 * self.cols * 2))
        self.chunks = []
        g0 = 0
        while g0 < self.G:
            ng = min(self.gpc, self.G - g0)
            b16 = nc.dram_tensor("%s_b16_%d" % (name, g0), [ng * 128, self.cols], BF16)
            full = nc.dram_tensor("%s_full_%d" % (name, g0), [4 * ng * 128, self.cols], BF16)
            self.chunks.append((g0, ng, b16, full))
            g0 += ng

    def host(self, w, r):
        KL, gw, G = self.KL, self.gw, self.G
        ws = w[r * KL * 128:(r + 1) * KL * 128, :]
        a = ws.reshape(KL, 128, G, gw).transpose(2, 1, 0, 3)
        return np.ascontiguousarray(a.reshape(G * 128, KL * gw))

    def tile_shape(self):
        return [128, 4, self.KL * self.gw]

    def src_ap(self, g):
        g0, ng, b16, full = self.chunks[g // self.gpc]
        v = full.ap().rearrange("(r g p) c -> g p r c", r=4, g=ng, p=128)
        return v[g - g0]

    def lhs(self, wt, k, c0, c1):
        r, kl = k // self.KL, k % self.KL
        return wt.t[:, r, kl * self.gw + c0: kl * self.gw + c1]


class Stream:
    def __init__(self, S, bufs, loads):
        self.S, self.bufs, self.loads = S, bufs, loads
        self.issued = 0

    def get(self, i):
        nb = len(self.bufs)
        while self.issued < min(len(self.loads), i + nb):
            j = self.issued
            self.loads[j](self.bufs[j % nb])
            self.issued += 1
        return self.bufs[i % nb]


def build(cfg):
    r = _build(cfg)
    return r


def _build(cfg):
    import os
    STOP = int(os.environ.get('KSTOP', '99'))
    DBG = int(os.environ.get('KDBG', '0'))
    nc = bass.Bass("TRN2", target_bir_lowering=False)
    S = Sched(nc)
    global _LASTS
    _LASTS = S
    D, DFF, T, KD, KF, KC, L = cfg.D, cfg.DFF, cfg.T, cfg.KD, cfg.KF, cfg.KC, cfg.L
    NCH6, XW, NT = cfg.NCH6, cfg.XW, cfg.NT
    NKT = T // 128
    GRP = [[0, 1, 2, 3], [4, 5, 6, 7]]
    ALL8 = [list(range(8))]
    SC = 128 ** -0.5

    x_tm = nc.dram_tensor("x_tm", [T, D], F32, kind="ExternalInput")
    xh_tm = nc.dram_tensor("xh_tm", [512, D], F32, kind="ExternalInput")
    ctx_tm = nc.dram_tensor("ctx_tm", [256, D], F32, kind="ExternalInput")
    out_tm = nc.dram_tensor("out_tm", [T, D], F32, kind="ExternalOutput")
    sm_off, sm_w = small_layout(cfg)
    small = nc.dram_tensor("small", [128, sm_w], F32, kind="ExternalInput")
    cos_d = nc.dram_tensor("cosT", [128, T], F32, kind="ExternalInput")
    sin_d = nc.dram_tensor("sinT", [128, T], F32, kind="ExternalInput")
    cst_d = nc.dram_tensor("consts", [128, 256], F32, kind="ExternalInput")
    braw_d = nc.dram_tensor("braw", [8, 15, 64, 64], F32, kind="ExternalInput")
    adaw_d = nc.dram_tensor("adaw", [L, D, NCH6 * 32], F32, kind="ExternalInput")
    NCA = NCH6 // 4
    W = {}
    W["in0"] = WMat(nc, "w_in0", D, cfg.N0, 256)
    W["out0"] = WMat(nc, "w_out0", 2048, D, min(512, D))
    W["in1"] = WMat(nc, "w_in1", D, cfg.N1, 256)
    W["out1"] = WMat(nc, "w_out1", cfg.CW + 1024, D, min(512, D))
    for l in range(L):
        W["g%d" % l] = WMat(nc, "w_g%d" % l, D, DFF, 256)
        W["u%d" % l] = WMat(nc, "w_u%d" % l, D, DFF, 256)
        W["d%d" % l] = WMat(nc, "w_d%d" % l, DFF, D, 256)
    for key_, wm_ in W.items():
        wm_.key = key_

    XT = [nc.dram_tensor("XT%d" % i, [KD, 128, XW], F32) for i in range(2)]
    XH = nc.dram_tensor("XH", [KD, 128, 512], F32)
    LAT0, CTX0 = 1, T + 3
    QT0 = nc.dram_tensor("QT0", [16, 128, T + 256], BF16)
    KaT = nc.dram_tensor("KaT", [8, 128, T + 512], BF16)
    KaTc = nc.dram_tensor("KaTc", [8, 128, 256], BF16)
    NKE = (T + 512) // 128
    Va = nc.dram_tensor("Va", [128, 8, NKE, 128], BF16)
    Vac = nc.dram_tensor("Vac", [128, 8, 2, 128], BF16)
    KbT_src = [nc.dram_tensor("KbT_src%d" % i, [128, T], BF16) for i in range(2)]
    KbT_all = [nc.dram_tensor("KbT_all%d" % i, [4 * 128, T], BF16) for i in range(2)]
    KbTc = nc.dram_tensor("KbTc", [2, 128, 256], BF16)
    Vb_src = [nc.dram_tensor("Vb_src%d" % i, [128, NKT * 128], BF16) for i in range(2)]
    Vb_all = [nc.dram_tensor("Vb_all%d" % i, [4 * 128, NKT * 128], BF16) for i in range(2)]
    Vbc = nc.dram_tensor("Vbc", [128, 2, 2, 128], BF16)
    QT1 = nc.dram_tensor("QT1", [8, 128, T], BF16)
    KdT_src = [nc.dram_tensor("KdT_src%d" % i, [128, T], BF16) for i in range(8)]
    KdT_all = [nc.dram_tensor("KdT_all%d" % i, [4 * 128, T], BF16) for i in range(8)]
    KdTc = nc.dram_tensor("KdTc", [8, 128, 256], BF16)
    NKH = NKT // 2
    Vd_src = [nc.dram_tensor("Vd_src%d" % i, [128, NKH * 256], BF16) for i in range(8)]
    Vd_all = [nc.dram_tensor("Vd_all%d" % i, [4 * 128, NKH * 256], BF16) for i in range(8)]
    Vdc = nc.dram_tensor("Vdc", [128, 4, 2, 256], BF16)
    GCU = nc.dram_tensor("GCU", [KC, 128, T + 2], F32)
    GB = nc.dram_tensor("GB", [KC, 128, T], F32)
    HXW = 2 * max(KD, KC)
    hx_src = nc.dram_tensor("hx_src", [128, HXW], F32)
    hx_all = nc.dram_tensor("hx_all", [4 * 128, HXW], F32)
    NAtab = nc.dram_tensor("NAtab", [3, 8, 128, 1536], BF16)
    MODW = L * 2 * (NCH6 // 4)
    modsrc = nc.dram_tensor("modsrc", [128, MODW], F32)
    moddst = nc.dram_tensor("moddst", [4 * 128, MODW], F32)

    smt = S.sb("smt", [128, sm_w], F32)
    cst = S.sb("cst", [128, 256], F32)
    cstb = S.sb("cstb", [128, 256], BF16)
    onesb = S.sb("onesb", [128, 128], BF16)
    onesf = S.sb("onesf", [128, 128], F32)
    drv = S.sb("drv", [128, L * 2 * 6 * KD + 8], F32)
    lamt = S.sb("lamt", [128, 8], F32)
    fixw = S.sb("fixw", [128, L * 4 * KF], F32)
    PS = [S.ps("ps%d" % i) for i in range(8)]
    collb = S.sb("coll", [128, 1], F32)
    collb.name = "coll"
    gsc = S.sb("gsc", [128, 2], F32)
    wtok = {}
    for key_, wm_ in W.items():
        for ci_ in range(len(wm_.chunks)):
            tk_ = S.sb("wtok", [128, 1], F32)
            tk_.keep = True
            wtok[(key_, ci_)] = tk_

    def sm(name, a=None, b=None):
        o, w = sm_off[name]
        if a is None:
            return smt.t[:, o:o + w]
        return smt.t[:, o + a:o + (b if b is not None else a + 1)]

    ident = cst.t[:, 0:128]
    Rmb = cstb.t[:, 128:256]
    identb = cstb.t[:, 0:128]

    def DRV(l, tt, i, k=None):
        o = ((l * 2 + tt) * 6 + i) * KD
        if k is None:
            return drv.t[:, o:o + KD]
        return drv.t[:, o + k:o + k + 1]

    es0 = ExitStack()
    S.dma("sp", smt.t[:], small[:, :], smt, w=[smt])
    S.dma("sp", cst.t[:], cst_d[:, :], cst, w=[cst])
    S.dve(lambda: nc.vector.tensor_copy(cstb.t[:], cst.t[:]), r=[cst], w=[cstb])
    S.dve(lambda: nc.vector.memset(onesb.t[:], 1.0), w=[onesb])
    S.dve(lambda: nc.vector.memset(onesf.t[:], 1.0), w=[onesf])

    wcast = S.buf(None, "wcast")

    def stage_weights(keys):
        for key in keys:
            wm = W[key]
            for ci, (g0, ng, b16, full) in enumerate(wm.chunks):
                S.dma("pool", b16[:, :], wm.src[g0 * 128:(g0 + ng) * 128, :], wcast)
                if wcast.dsem is not None:
                    nc.gpsimd.wait_ge(wcast.dsem, wcast.dcnt)
                tk = wtok[(key, ci)]
                S.coll("AllGather", ALU.bypass, GRP, b16.ap().opt(), full.ap().opt(), collb, tok=tk)

    stage_weights(["in0"])
    if STOP == 2:
        return nc, W
    scv = S.sb("scv", [128, KD * 2], F32, es0)
    S.act(lambda: nc.scalar.activation(out=scv.t[:], in_=sm("cv"), func=AF.Silu), r=[smt], w=[scv])
    modloc = S.sb("modloc", [128, L * 2 * NCA], F32, es0)
    Z = S.sb("Z", [128, 4, L * 2 * NCA], F32, es0)
    CB = 4
    adaw = [S.sb("adaw%d" % i, [128, KD, CB * 128], F32, es0) for i in range(2)]
    ai = 0
    for l in range(L):
        ps = PS[l % 2]
        for cb in range(0, NCA, CB):
            nb = min(CB, NCA - cb)
            aw = adaw[ai % 2]
            ai += 1
            S.dma("sp", aw.t[:, :, :nb * 128],
                  adaw_d.ap()[l, :, cb * 128:(cb + nb) * 128].rearrange("(k p) n -> p k n", p=128), aw, w=[aw])
            for ci in range(nb):
                cc = cb + ci
                for k in range(KD):
                    S.pe(lambda k=k, ci=ci, cc=cc, ps=ps, aw=aw: nc.tensor.matmul(
                        ps.t[:, cc * 2:cc * 2 + 2], lhsT=aw.t[:, k, ci * 128:(ci + 1) * 128],
                        rhs=scv.t[:, k * 2:(k + 1) * 2], start=(k == 0), stop=(k == KD - 1)), r=[aw, scv], w=[ps])
        for v in range(2):
            o = (l * 2 + v) * NCA
            S.dve(lambda o=o, v=v, ps=ps: nc.vector.tensor_copy(
                modloc.t[:, o:o + NCA], ps.t[:, v:2 * NCA:2]), r=[ps], w=[modloc])
    S.dma("sp", modsrc[:, :], modloc.t[:], modloc, r=[modloc])
    if modloc.dsem is not None:
        nc.gpsimd.wait_ge(modloc.dsem, modloc.dcnt)
    S.coll("AllGather", ALU.bypass, GRP, modsrc.ap().opt(), moddst.ap().opt(), collb)
    S.dma("sp", Z.t[:], moddst.ap().rearrange("(r p) c -> p r c", p=128), Z, r=[collb], w=[Z])
    stage_weights(["out0", "g0", "u0", "d0"])
    modv = S.sb("modv", [128, L * 2 * NCH6], F32, es0)
    for l in range(L):
        for v in range(2):
            for r in range(4):
                o = (l * 2 + v) * NCH6 + r * NCA
                zi = (l * 2 + v) * NCA
                ab = sm("adab", l * NCH6 + r * NCA, l * NCH6 + (r + 1) * NCA)
                S.dve(lambda o=o, zi=zi, r=r, ab=ab: nc.vector.tensor_tensor(
                    modv.t[:, o:o + NCA], Z.t[:, r, zi:zi + NCA], ab, op=ALU.add), r=[Z, smt], w=[modv])
        for tt in range(2):
            mo = (l * 2 + tt) * NCH6
            for half in range(2):
                sh = modv.t[:, mo + (3 * half) * KD: mo + (3 * half + 1) * KD]
                scl = modv.t[:, mo + (3 * half + 1) * KD: mo + (3 * half + 2) * KD]
                gt = modv.t[:, mo + (3 * half + 2) * KD: mo + (3 * half + 3) * KD]
                ng = sm("ng", (l * 2 + half) * KD, (l * 2 + half + 1) * KD)
                S.dve(lambda l=l, tt=tt, half=half, scl=scl, ng=ng: nc.vector.scalar_tensor_tensor(
                    DRV(l, tt, 3 * half), scl, 1.0, ng, op0=ALU.add, op1=ALU.mult), r=[modv, smt], w=[drv])
                S.dve(lambda l=l, tt=tt, half=half, sh=sh: nc.vector.tensor_copy(DRV(l, tt, 3 * half + 1), sh),
                      r=[modv], w=[drv])
                S.dve(lambda l=l, tt=tt, half=half, gt=gt: nc.vector.tensor_copy(DRV(l, tt, 3 * half + 2), gt),
                      r=[modv], w=[drv])
    if STOP == 3:
        return nc, W
    lam_init = 0.8 - 0.6 * math.exp(-0.3 * 1)
    S.dve(lambda: nc.vector.tensor_tensor(lamt.t[:, 0:1], sm("lq1"), sm("lk1"), op=ALU.mult), r=[smt], w=[lamt])
    S.dve(lambda: nc.vector.tensor_tensor(lamt.t[:, 1:2], sm("lq2"), sm("lk2"), op=ALU.mult), r=[smt, lamt], w=[lamt])
    S.pe(lambda: nc.tensor.matmul(PS[0].t[:, 0:2], lhsT=onesf.t[:], rhs=lamt.t[:, 0:2], start=True, stop=True),
         r=[onesf, lamt], w=[PS[0]])
    S.act(lambda: nc.scalar.activation(out=lamt.t[:, 2:4], in_=PS[0].t[:, 0:2], func=AF.Exp), r=[PS[0]], w=[lamt])
    S.dve(lambda: nc.vector.tensor_tensor(lamt.t[:, 4:5], lamt.t[:, 2:3], lamt.t[:, 3:4], op=ALU.subtract),
          r=[lamt], w=[lamt])
    S.dve(lambda: nc.vector.tensor_scalar(lamt.t[:, 5:6], lamt.t[:, 4:5], lam_init, -1.0, op0=ALU.add, op1=ALU.mult),
          r=[lamt], w=[lamt])
    S.dve(lambda: nc.vector.tensor_scalar(lamt.t[:, 6:8], sm("slg"), 1.0 - lam_init, None, op0=ALU.mult),
          r=[smt, lamt], w=[lamt])
    S.dve(lambda: nc.vector.tensor_scalar(gsc.t[:, 0:1], sm("qg"), SC, None, op0=ALU.mult), r=[smt], w=[gsc])
    S.dve(lambda: nc.vector.tensor_copy(gsc.t[:, 1:2], sm("kg")), r=[smt, gsc], w=[gsc])
    for l in range(L):
        for i, (wi, mk) in enumerate([(0, "nhl"), (2, "nhr")]):
            o = (l * 4 + i) * KF
            wv = sm("fcw", (l * 3 + wi) * KF, (l * 3 + wi + 1) * KF)
            S.dve(lambda o=o, wv=wv, mk=mk: nc.vector.tensor_scalar(
                fixw.t[:, o:o + KF], wv, sm(mk), -1.0, op0=ALU.mult, op1=ALU.mult), r=[smt, fixw], w=[fixw])
            o2 = (l * 4 + 2 + i) * KF
            S.dve(lambda o2=o2, wv=wv: nc.vector.tensor_scalar(
                fixw.t[:, o2:o2 + KF], wv, -1.0, None, op0=ALU.mult), r=[smt, fixw], w=[fixw])

    if STOP == 4:
        return nc, W
    S.barrier()
    es0.close()
    es0 = ExitStack()
    tabs = [S.sb("tab%d" % i, [128, 8, 1536], BF16, es0) for i in range(3)]
    tmpb = S.sb("tmpb", [128, 8, 1536], BF16, es0)
    tstage = S.sb("tstage", [128, 8, 1536], F32, es0)

    def rowvalid(var, t, kr2, qr4):
        krel = -4 + 2 * t + kr2
        if var == 0:
            return 0 <= krel <= 7
        if var == 2:
            return -4 <= krel <= 3
        return -4 <= krel - qr4 <= 3

    for var in range(3):
        tb = tstage
        S.dve(lambda tb=tb: nc.vector.memset(tb.t[:], NEG), w=[tb])
        for t in range(6):
            for kr2 in range(2):
                q = 0
                while q < 4:
                    if not rowvalid(var, t, kr2, q):
                        q += 1
                        continue
                    q1 = q
                    while q1 + 1 < 4 and rowvalid(var, t, kr2, q1 + 1):
                        q1 += 1
                    nd0 = q + 11 - 2 * t - kr2
                    n = q1 - q + 1
                    for h in range(8):
                        src = braw_d.ap()[h, nd0:nd0 + n].rearrange("a k q -> k a q")
                        dst = tb.t[kr2 * 64:(kr2 + 1) * 64, h, t * 256 + q * 64: t * 256 + (q1 + 1) * 64]
                        dst = dst.rearrange("k (a q) -> k a q", a=n)
                        S.dma("sp", dst, src, tb, w=[tb])
                    q = q1 + 1
        for h in range(8):
            S.dve(lambda h=h, var=var: nc.vector.tensor_copy(tabs[var].t[:, h, :], tstage.t[:, h, :]),
                  r=[tstage], w=[tabs[var]])
    for vi, (src_t, mk, nmk) in enumerate([(tabs[0], "istop", "ntop"), (None, None, None), (tabs[2], "isbot", "nbot")]):
        for h in range(8):
            if src_t is None:
                S.dma("sp", NAtab[vi, h], tabs[1].t[:, h, :], tabs[1], r=[tabs[1]])
                continue
            S.dve(lambda h=h, nmk=nmk: nc.vector.tensor_scalar(
                tmpb.t[:, h, :], tabs[1].t[:, h, :], sm(nmk), None, op0=ALU.mult), r=[tabs[1], smt], w=[tmpb])
            S.dve(lambda h=h, mk=mk, src_t=src_t: nc.vector.scalar_tensor_tensor(
                tmpb.t[:, h, :], src_t.t[:, h, :], sm(mk), tmpb.t[:, h, :], op0=ALU.mult, op1=ALU.add),
                r=[src_t, smt, tmpb], w=[tmpb])
            S.dma("sp", NAtab[vi, h], tmpb.t[:, h, :], tmpb, r=[tmpb])
    S.barrier()
    es0.close()
    if STOP == 5:
        return nc, W

    epst = S.sb("epst", [128, 1], F32)
    S.dve(lambda: nc.vector.memset(epst.t[:], EPS), w=[epst])
    hxg = S.sb("hxg", [128, 4, HXW], F32)
    hxo = S.sb("hxo", [128, HXW], F32)
    hxs = S.sb("hxs", [128, HXW], F32)
    sq = S.sb("sq", [128, 512], F32)
    sqb = S.sb("sqb", [128, 512], BF16)
    rstd = S.sb("rstd", [128, 512], F32)
    tmp = S.sb("tmp", [128, 512], F32)
    tmp2 = S.sb("tmp2", [128, 512], F32)
    qn = S.sb("qn", [128, 512], F32)
    qb = S.sb("qb", [128, 512], BF16)
    stg = [S.sb("stg%d" % i, [128, 512], BF16) for i in range(3)]
    stf = [S.sb("stf%d" % i, [128, 512], F32) for i in range(2)]
    cnt = {"stg": 0, "stf": 0, "ps": 0}

    def nxt(lst, key):
        cnt[key] += 1
        return lst[cnt[key] % len(lst)]

    def XTv(i):
        return XT[i].ap().rearrange("k p t -> p k t")

    def transpose_in2(src_tm, ntok, dstv, col0, xin, xo):
        for j in range(0, ntok, 512):
            w = min(512, ntok - j)
            ns = w // 128
            for s in range(ns):
                xi = xin[s]
                S.dma("sp", xi.t[:], src_tm[j + s * 128: j + (s + 1) * 128, :], xi, w=[xi])
            o = xo[(j // 512) % len(xo)]
            for k in range(KD):
                ps = PS[k % 6]
                for s in range(ns):
                    xi = xin[s]
                    S.pe(lambda xi=xi, k=k, ps=ps, s=s: nc.tensor.transpose(
                        ps.t[:, s * 128:(s + 1) * 128], xi.t[:, k * 128:(k + 1) * 128], ident),
                        r=[xi, cst], w=[ps])
                if k % 2 == 0:
                    S.dve(lambda o=o, k=k, ps=ps, w=w: nc.vector.tensor_copy(o.t[:, k, :w], ps.t[:, :w]),
                          r=[ps], w=[o])
                else:
                    S.act(lambda o=o, k=k, ps=ps, w=w: nc.scalar.copy(o.t[:, k, :w], ps.t[:, :w]),
                          r=[ps], w=[o])
            S.dma("sp", dstv[:, :, col0 + j: col0 + j + w], o.t[:, :, :w], o, r=[o])

    def norm_mod(xt, wdt, scale_fn, bias_fn, ht, c0=0):
        ps = PS[7]
        for k in range(KD):
            S.act(lambda k=k: nc.scalar.activation(out=sqb.t[:, :wdt], in_=xt.t[:, k, c0:c0 + wdt], func=AF.Square),
                  r=[xt], w=[sqb])
            S.pe(lambda k=k: nc.tensor.matmul(ps.t[:, :wdt], lhsT=onesb.t[:], rhs=sqb.t[:, :wdt],
                                              start=(k == 0), stop=(k == KD - 1)), r=[sqb, onesb], w=[ps])
        S.act(lambda: nc.scalar.activation(out=rstd.t[:, :wdt], in_=ps.t[:, :wdt], func=AF.Sqrt,
                                           scale=1.0 / D, bias=epst.t[:, 0:1]), r=[ps, epst], w=[rstd])
        S.dve(lambda: nc.vector.reciprocal(rstd.t[:, :wdt], rstd.t[:, :wdt]), r=[rstd], w=[rstd])
        for k in range(KD):
            S.dve(lambda k=k: nc.vector.tensor_tensor(tmp.t[:, :wdt], xt.t[:, k, c0:c0 + wdt], rstd.t[:, :wdt],
                                                      op=ALU.mult), r=[xt, rstd], w=[tmp])
            if bias_fn is None:
                S.act(lambda k=k: nc.scalar.activation(out=ht.t[:, k, :wdt], in_=tmp.t[:, :wdt], func=AF.Identity,
                                                       scale=scale_fn(k)), r=[tmp, drv, smt], w=[ht])
            else:
                S.act(lambda k=k: nc.scalar.activation(out=ht.t[:, k, :wdt], in_=tmp.t[:, :wdt], func=AF.Identity,
                                                       scale=scale_fn(k), bias=bias_fn(k)), r=[tmp, drv, smt], w=[ht])

    def halo_exchange(n, put):
        S.dma("sp", hx_src[:, 0:2 * n], hxs.t[:, 0:2 * n], hxs, r=[hxs])
        if hxs.dsem is not None:
            nc.gpsimd.wait_ge(hxs.dsem, hxs.dcnt)
        S.coll("AllGather", ALU.bypass, GRP, hx_src.ap().opt(), hx_all.ap().opt(), collb)
        S.dma("sp", hxg.t[:], hx_all.ap().rearrange("(r p) c -> p r c", p=128), hxg, r=[collb], w=[hxg])
        for side, mk, off in ((0, "mL", n), (1, "mR", 0)):
            for r in range(4):
                if r == 0:
                    S.dve(lambda side=side, mk=mk, off=off, r=r: nc.vector.tensor_scalar(
                        hxo.t[:, side * n:(side + 1) * n], hxg.t[:, r, off:off + n], sm(mk, r), None, op0=ALU.mult),
                        r=[hxg, smt, hxo], w=[hxo])
                else:
                    S.dve(lambda side=side, mk=mk, off=off, r=r: nc.vector.scalar_tensor_tensor(
                        hxo.t[:, side * n:(side + 1) * n], hxg.t[:, r, off:off + n], sm(mk, r),
                        hxo.t[:, side * n:(side + 1) * n], op0=ALU.mult, op1=ALU.add),
                        r=[hxg, smt, hxo], w=[hxo])
        put(hxo.t[:, 0:n].rearrange("p (k o) -> p k o", o=1), hxo.t[:, n:2 * n].rearrange("p (k o) -> p k o", o=1))

    def qk_post(ps, w, gain, do_norm, rope, scale, outb):
        aux = PS[6]
        if do_norm:
            S.act(lambda: nc.scalar.activation(out=sqb.t[:, :w], in_=ps.t[:, :w], func=AF.Square), r=[ps], w=[sqb])
            S.pe(lambda: nc.tensor.matmul(aux.t[:, :w], lhsT=onesb.t[:], rhs=sqb.t[:, :w], start=True, stop=True),
                 r=[sqb, onesb], w=[aux])
            S.act(lambda: nc.scalar.activation(out=rstd.t[:, :w], in_=aux.t[:, :w], func=AF.Sqrt, scale=1.0 / 128,
                                               bias=epst.t[:, 0:1]), r=[aux, epst], w=[rstd])
            S.dve(lambda: nc.vector.reciprocal(rstd.t[:, :w], rstd.t[:, :w]), r=[rstd], w=[rstd])
            dst = qn if rope is not None else outb
            S.dve(lambda: nc.vector.scalar_tensor_tensor(dst.t[:, :w], ps.t[:, :w], gain, rstd.t[:, :w],
                                                         op0=ALU.mult, op1=ALU.mult), r=[ps, rstd, gsc], w=[dst])
        else:
            dst = qn if rope is not None else outb
            S.act(lambda: nc.scalar.activation(out=dst.t[:, :w], in_=ps.t[:, :w], func=AF.Copy, scale=scale),
                  r=[ps], w=[dst])
        if rope is not None:
            cs, sn = rope
            S.act(lambda: nc.scalar.copy(qb.t[:, :w], qn.t[:, :w]), r=[qn], w=[qb])
            S.pe(lambda: nc.tensor.matmul(aux.t[:, :w], lhsT=Rmb, rhs=qb.t[:, :w], start=True, stop=True),
                 r=[qb, cstb], w=[aux])
            S.dve(lambda: nc.vector.tensor_tensor(tmp.t[:, :w], qn.t[:, :w], cs.t[:, :w], op=ALU.mult),
                  r=[qn, cs], w=[tmp])
            S.dve(lambda: nc.vector.tensor_tensor(tmp2.t[:, :w], aux.t[:, :w], sn.t[:, :w], op=ALU.mult),
                  r=[aux, sn], w=[tmp2])
            S.dve(lambda: nc.vector.tensor_tensor(outb.t[:, :w], tmp.t[:, :w], tmp2.t[:, :w], op=ALU.add),
                  r=[tmp, tmp2], w=[outb])

    def attn(Qs, qdeps, Wq, keys, ndv, Ob, Lb, Sb, PTs, lacc=None):
        ncomp = len(Qs)
        N = len(keys)
        LA = 2
        for n in range(N + LA):
            if n < N:
                kt = keys[n]
                if kt.get("pre") is not None:
                    kt["pre"]()
                sbk = Sb[n % len(Sb)]
                pt = PTs[n % len(PTs)]
                for c in range(ncomp):
                    S.pe(lambda kt=kt, c=c, sbk=sbk: nc.tensor.matmul(
                        sbk.t[:, c * Wq:(c + 1) * Wq], lhsT=kt["K"][c], rhs=Qs[c], start=True,
                        stop=(kt["bias"] is None)), r=list(kt["deps"]) + list(qdeps), w=[sbk])
                    if kt["bias"] is not None:
                        S.pe(lambda kt=kt, c=c, sbk=sbk: nc.tensor.matmul(
                            sbk.t[:, c * Wq:(c + 1) * Wq], lhsT=identb, rhs=kt["bias"], start=False, stop=True),
                            r=list(kt["deps"]) + [cstb], w=[sbk])
                S.act(lambda sbk=sbk, pt=pt: nc.scalar.activation(out=pt.t[:, :ncomp * Wq], in_=sbk.t[:, :ncomp * Wq],
                                                                  func=AF.Exp), r=[sbk], w=[pt])
            if n >= LA:
                m = n - LA
                kt = keys[m]
                pt = PTs[m % len(PTs)]
                for c in range(ncomp):
                    for dv in range(ndv):
                        S.pe(lambda kt=kt, c=c, dv=dv, pt=pt, m=m: nc.tensor.matmul(
                            Ob[c][dv].t[:, 0:Wq], lhsT=kt["V"][dv], rhs=pt.t[:, c * Wq:(c + 1) * Wq],
                            start=(m == 0), stop=(m == N - 1)), r=list(kt["deps"]) + [pt], w=[Ob[c][dv]])
                    if lacc is None:
                        S.pe(lambda c=c, pt=pt, m=m: nc.tensor.matmul(
                            Lb[c].t[:, 0:Wq], lhsT=onesb.t[:], rhs=pt.t[:, c * Wq:(c + 1) * Wq],
                            start=(m == 0), stop=(m == N - 1)), r=[pt, onesb], w=[Lb[c]])
                if lacc is not None:
                    if m == 0:
                        S.dve(lambda pt=pt: nc.vector.tensor_copy(lacc.t[:, :ncomp * Wq], pt.t[:, :ncomp * Wq]),
                              r=[pt], w=[lacc])
                    else:
                        S.dve(lambda pt=pt: nc.vector.tensor_tensor(
                            lacc.t[:, :ncomp * Wq], lacc.t[:, :ncomp * Wq], pt.t[:, :ncomp * Wq], op=ALU.add),
                            r=[pt, lacc], w=[lacc])

    def load_w(wm, wt, g):
        S.dma("sp", wt.t[:], wm.src_ap(g), wt, r=[wtok[(wm.key, g // wm.gpc)]], w=[wt])

    esx = ExitStack()
    xin = [S.sb("xin%d" % i, [128, D], F32, esx) for i in range(4)]
    xo = [S.sb("xo%d" % i, [128, KD, 512], F32, esx) for i in range(2)]
    if not os.environ.get("KNOTR"):
        transpose_in2(x_tm, T, XTv(0), LAT0, xin, xo)
        transpose_in2(ctx_tm, 256, XTv(0), CTX0, xin, xo)
        transpose_in2(xh_tm, 512, XH.ap().rearrange("k p t -> p k t"), 0, xin, xo)
    zt = S.sb("zt", [128, KD, 1], F32, esx)
    S.dve(lambda: nc.vector.memset(zt.t[:], 0.0), w=[zt])
    if not os.environ.get("KNOZT"):
        S.dma("sp", XTv(0)[:, :, CTX0 - 1:CTX0], zt.t[:], zt, r=[zt])
        S.dma("sp", XTv(0)[:, :, CTX0 + 256:CTX0 + 257], zt.t[:], zt, r=[zt])
    S.barrier()
    esx.close()
    if STOP == 6:
        return nc, W

    def inproj(l, cur):
        es = ExitStack()
        wm = W["in%d" % l]
        xt = [S.sb("ip_xt%d" % i, [128, KD, 512], F32, es) for i in range(2)]
        ht = S.sb("ip_ht", [128, KD, 512], BF16, es)
        wts = [S.sb("ip_w%d" % i, wm.tile_shape(), BF16, es) for i in range(2)]
        cs = S.sb("ip_cos", [128, 512], F32, es)
        sn = S.sb("ip_sin", [128, 512], F32, es)
        ub = S.sb("ip_ub", [128, max(KC, 1), 512], F32, es) if l == 1 else None
        tiles = [("lat", j) for j in range(NT)] + [("ctx", 0)] + ([("halo", 0)] if l == 0 else [])
        wi = 0
        for ti, (kind, j) in enumerate(tiles):
            w = 512 if kind != "ctx" else 256
            x = xt[ti % 2]
            if kind == "lat":
                S.dma("sp", x.t[:, :, :w], XTv(cur)[:, :, LAT0 + j * 512: LAT0 + j * 512 + w], x, w=[x])
                S.dma("sp", cs.t[:], cos_d[:, j * 512:(j + 1) * 512], cs, w=[cs])
                S.dma("sp", sn.t[:], sin_d[:, j * 512:(j + 1) * 512], sn, w=[sn])
            elif kind == "ctx":
                S.dma("sp", x.t[:, :, :w], XTv(cur)[:, :, CTX0: CTX0 + w], x, w=[x])
            else:
                S.dma("sp", x.t[:, :, :w], XH.ap().rearrange("k p t -> p k t"), x, w=[x])
            tt = 1 if kind == "ctx" else 0
            norm_mod(x, w, lambda k: DRV(l, tt, 0, k), lambda k: DRV(l, tt, 1, k), ht)
            if l == 0:
                if kind == "halo":
                    groups = list(range(8, 16))
                else:
                    groups = list(range(18))
            else:
                if kind == "ctx":
                    groups = list(range((3 * KC + 8) // 2, (3 * KC + 24) // 2))
                else:
                    groups = list(range(wm.G))
            for g in groups:
                wt = wts[wi % 2]
                wi += 1
                load_w(wm, wt, g)
                c0 = 2 * g
                kinds = [chunk_kind(l, c0), chunk_kind(l, c0 + 1)]
                if kinds[0][0] in ("av", "bv", "dv"):
                    for s in range(w // 128):
                        ps = nxt(PS[0:6], "ps")
                        for k in range(KD):
                            S.pe(lambda k=k, s=s, ps=ps, wt=wt: nc.tensor.matmul(
                                ps.t[:, 0:256], lhsT=ht.t[:, k, s * 128:(s + 1) * 128], rhs=wm.lhs(wt, k, 0, 256),
                                start=(k == 0), stop=(k == KD - 1)), r=[ht, wt], w=[ps])
                        ob = nxt(stg, "stg")
                        S.act(lambda ob=ob, ps=ps: nc.scalar.copy(ob.t[:, 0:256], ps.t[:, 0:256]), r=[ps], w=[ob])
                        kk, idx = kinds[0]
                        if kk == "av":
                            src = ob.t[:, 0:256].rearrange("p (h d) -> p h d", h=2)
                            if kind == "lat":
                                dst = Va[:, idx:idx + 2, 2 + j * 4 + s, :]
                            elif kind == "halo":
                                dst = Va[:, idx:idx + 2, (s if s < 2 else NKE - 4 + s), :]
                            else:
                                dst = Vac[:, idx:idx + 2, s, :]
                        elif kk == "bv":
                            src = ob.t[:, 0:256].rearrange("p (h d) -> p h d", h=2)
                            if kind == "lat":
                                S.dma("sp", Vb_src[0].ap().rearrange("p (k d) -> p k d", d=128)[:, j * 4 + s, :],
                                      ob.t[:, 0:128], ob, r=[ob])
                                S.dma("sp", Vb_src[1].ap().rearrange("p (k d) -> p k d", d=128)[:, j * 4 + s, :],
                                      ob.t[:, 128:256], ob, r=[ob])
                                continue
                            else:
                                dst = Vbc[:, :, s, :]
                        else:
                            src = ob.t[:, 0:256]
                            if kind == "lat":
                                kt_ = j * 4 + s
                                dst = Vd_src[idx * 2 + kt_ // NKH].ap().rearrange("p (k d) -> p k d", d=256)[:, kt_ % NKH, :]
                            else:
                                dst = Vdc[:, idx, s, :]
                        S.dma("sp", dst, src, ob, r=[ob])
                    continue
                for ci in range(2):
                    kk, idx = kinds[ci]
                    ps = nxt(PS[0:6], "ps")
                    for k in range(KD):
                        S.pe(lambda k=k, ps=ps, wt=wt, ci=ci: nc.tensor.matmul(
                            ps.t[:, :w], lhsT=wm.lhs(wt, k, ci * 128, (ci + 1) * 128), rhs=ht.t[:, k, :w],
                            start=(k == 0), stop=(k == KD - 1)), r=[ht, wt], w=[ps])
                    rope = (cs, sn) if kind == "lat" else None
                    if kk == "u":
                        S.act(lambda ps=ps, idx=idx: nc.scalar.copy(ub.t[:, idx, :w], ps.t[:, :w]), r=[ps], w=[ub])
                        continue
                    if kk == "gb":
                        of = nxt(stf, "stf")
                        S.act(lambda ps=ps, of=of: nc.scalar.copy(of.t[:, :w], ps.t[:, :w]), r=[ps], w=[of])
                        S.dma("sp", GB[idx, :, j * 512: j * 512 + w], of.t[:, :w], of, r=[of])
                        continue
                    if kk == "gc":
                        of = nxt(stf, "stf")
                        S.dve(lambda ps=ps, of=of, idx=idx: nc.vector.tensor_tensor(
                            of.t[:, :w], ps.t[:, :w], ub.t[:, idx, :w], op=ALU.mult), r=[ps, ub], w=[of])
                        S.dma("sp", GCU[idx, :, 1 + j * 512: 1 + j * 512 + w], of.t[:, :w], of, r=[of])
                        if j == 0:
                            S.dve(lambda of=of, idx=idx: nc.vector.tensor_copy(hxs.t[:, idx:idx + 1], of.t[:, 0:1]),
                                  r=[of, hxs], w=[hxs])
                        if j == NT - 1:
                            S.dve(lambda of=of, idx=idx: nc.vector.tensor_copy(
                                hxs.t[:, KC + idx:KC + idx + 1], of.t[:, w - 1:w]), r=[of, hxs], w=[hxs])
                        continue
                    ob = nxt(stg, "stg")
                    if kk == "aq":
                        qk_post(ps, w, None, False, None, SC, ob)
                        dst = QT0[idx, :, (j * 512 if kind == "lat" else T): (j * 512 if kind == "lat" else T) + w]
                    elif kk == "bq":
                        qk_post(ps, w, gsc.t[:, 0:1], True, rope, None, ob)
                        dst = QT0[8 + idx, :, (j * 512 if kind == "lat" else T): (j * 512 if kind == "lat" else T) + w]
                    elif kk == "ak":
                        qk_post(ps, w, None, False, None, 1.0, ob)
                        if kind == "lat":
                            dst = KaT[idx, :, 256 + j * 512: 256 + j * 512 + w]
                        elif kind == "ctx":
                            dst = KaTc[idx, :, :]
                        else:
                            S.dma("sp", KaT[idx, :, 0:256], ob.t[:, 0:256], ob, r=[ob])
                            dst = None
                            S.dma("sp", KaT[idx, :, T + 256: T + 512], ob.t[:, 256:512], ob, r=[ob])
                    elif kk == "bk":
                        qk_post(ps, w, gsc.t[:, 1:2], True, rope, None, ob)
                        dst = KbT_src[idx][:, j * 512: j * 512 + w] if kind == "lat" else KbTc[idx, :, :]
                    elif kk == "dq":
                        qk_post(ps, w, None, False, rope, SC, ob)
                        dst = QT1[idx, :, j * 512: j * 512 + w]
                    elif kk == "dk":
                        qk_post(ps, w, None, False, rope, 1.0, ob)
                        dst = KdT_src[idx][:, j * 512: j * 512 + w] if kind == "lat" else KdTc[idx, :, :]
                    if dst is not None:
                        S.dma("sp", dst, ob.t[:, :w], ob, r=[ob])
        S.barrier()
        es.close()

    def chunk_kind(l, c):
        if l == 0:
            if c < 8:
                return ("aq", c)
            if c < 16:
                return ("bq", c - 8)
            if c < 24:
                return ("ak", c - 16)
            if c < 32:
                return ("av", c - 24)
            if c < 34:
                return ("bk", c - 32)
            return ("bv", c - 34)
        if c < KC:
            return ("u", c)
        if c < 2 * KC:
            return ("gb", c - KC)
        if c < 3 * KC:
            return ("gc", c - 2 * KC)
        c -= 3 * KC
        if c < 8:
            return ("dq", c)
        if c < 16:
            return ("dk", c - 8)
        return ("dv", (c - 16) // 2)

    def outproj(es, l, cur, OT, nk, w, col0, tt, first, last, xr, wts, wctr):
        wm = W["out%d" % l]
        for g in range(wm.G):
            wt = wts[wctr[0] % len(wts)]
            wctr[0] += 1
            load_w(wm, wt, g)
            nm = wm.gw // 128
            x = xr[g % len(xr)]
            S.dma("sp", x.t[:, :nm, :w], XTv(cur)[:, g * nm:(g + 1) * nm, col0: col0 + w], x, w=[x])
            for mi in range(nm):
                m = g * nm + mi
                ps = nxt(PS[0:4], "ps")
                for k in range(nk):
                    S.pe(lambda k=k, ps=ps, wt=wt, mi=mi: nc.tensor.matmul(
                        ps.t[:, :w], lhsT=wm.lhs(wt, k, mi * 128, (mi + 1) * 128), rhs=OT.t[:, k, :w],
                        start=(k == 0), stop=(k == nk - 1)), r=[OT, wt], w=[ps])
                S.dve(lambda ps=ps, x=x, mi=mi, m=m: nc.vector.scalar_tensor_tensor(
                    x.t[:, mi, :w], ps.t[:, :w], DRV(l, tt, 2, m), x.t[:, mi, :w], op0=ALU.mult, op1=ALU.add),
                    r=[ps, x, drv], w=[x])
                if first:
                    S.dve(lambda x=x, mi=mi, m=m: nc.vector.tensor_copy(hxs.t[:, m:m + 1], x.t[:, mi, 0:1]),
                          r=[x, hxs], w=[hxs])
                if last:
                    S.dve(lambda x=x, mi=mi, m=m: nc.vector.tensor_copy(hxs.t[:, KD + m:KD + m + 1], x.t[:, mi, w - 1:w]),
                          r=[x, hxs], w=[hxs])
            S.dma("sp", XTv(cur)[:, g * nm:(g + 1) * nm, col0: col0 + w], x.t[:, :nm, :w], x, r=[x])

    def put_x_halo(cur):
        def put(left, right):
            S.dma("sp", XTv(cur)[:, :, LAT0 - 1:LAT0], left, hxo, r=[hxo])
            S.dma("sp", XTv(cur)[:, :, LAT0 + T:LAT0 + T + 1], right, hxo, r=[hxo])
        return put

    def mixer0(cur):
        l = 0
        es = ExitStack()
        wm = W["out0"]
        qt = S.sb("m_qt", [128, 16, 512], BF16, es)
        OT = S.sb("m_ot", [128, 16, 512], BF16, es)
        xr = [S.sb("m_xr%d" % i, [128, wm.gw // 128, 512], F32, es) for i in range(2)]
        wts = [S.sb("m_w%d" % i, wm.tile_shape(), BF16, es) for i in range(2)]
        kna = [S.sb("m_kna%d" % i, [128, 1024], BF16, es) for i in range(2)]
        vna = [S.sb("m_vna%d" % i, [128, 8, 128], BF16, es) for i in range(2)]
        tab = [S.sb("m_tab%d" % i, [128, 2, 1536], BF16, es) for i in range(2)]
        kch = [S.sb("m_kch%d" % i, [128, T], BF16, es) for i in range(2)]
        vch = [S.sb("m_vch%d" % i, [128, NKT, 128], BF16, es) for i in range(2)]
        kac = S.sb("m_kac", [128, 8, 256], BF16, es)
        vac = S.sb("m_vac", [128, 8, 2, 128], BF16, es)
        kbc = S.sb("m_kbc", [128, 2, 256], BF16, es)
        vbc = S.sb("m_vbc", [128, 2, 2, 128], BF16, es)
        PTs = [S.sb("m_pt%d" % i, [128, 512], BF16, es) for i in range(4)]
        rinv = S.sb("m_rinv", [128, 512], F32, es)
        S.dma("sp", kac.t[:], KaTc.ap().rearrange("h p t -> p h t"), kac, w=[kac])
        S.dma("sp", vac.t[:], Vac[:, :, :, :], vac, w=[vac])
        S.dma("sp", kbc.t[:], KbTc.ap().rearrange("g p t -> p g t"), kbc, w=[kbc])
        S.dma("sp", vbc.t[:], Vbc[:, :, :, :], vbc, w=[vbc])
        Ob, Lb, Sb = [[PS[4]]], [PS[5]], [PS[0], PS[1], PS[2], PS[3]]
        wctr = [0]
        ci = [0]

        def finish(Wq, dst_ap):
            S.dve(lambda: nc.vector.reciprocal(rinv.t[:, :Wq], Lb[0].t[:, :Wq]), r=[Lb[0]], w=[rinv])
            S.dve(lambda: nc.vector.tensor_tensor(dst_ap, Ob[0][0].t[:, :Wq], rinv.t[:, :Wq], op=ALU.mult),
                  r=[Ob[0][0], rinv], w=[OT])

        def ctx_keys_a(h):
            return [dict(K=[kac.t[:, h, kt * 128:(kt + 1) * 128]], V=[vac.t[:, h, kt, :]], bias=None, deps=[kac, vac])
                    for kt in range(2)]

        def ctx_keys_b(g):
            return [dict(K=[kbc.t[:, g, kt * 128:(kt + 1) * 128]], V=[vbc.t[:, g, kt, :]], bias=None, deps=[kbc, vbc])
                    for kt in range(2)]

        for j in range(NT):
            S.dma("sp", qt.t[:], QT0.ap().rearrange("h p t -> p h t")[:, :, j * 512:(j + 1) * 512], qt, w=[qt])
            for h in range(8):
                kn, vn, tb = kna[h % 2], vna[h % 2], tab[h % 2]
                S.dma("sp", kn.t[:], KaT[h, :, j * 512: j * 512 + 1024], kn, w=[kn])
                S.dma("sp", vn.t[:], Va[:, h, j * 4: j * 4 + 8, :], vn, w=[vn])
                for sub in range(2):
                    i = 2 * j + sub
                    var = 0 if i == 0 else (2 if i == 2 * NT - 1 else 1)
                    S.dma("sp", tb.t[:, sub, :], NAtab[var, h], tb, w=[tb])
                for sub in range(2):
                    keys = ctx_keys_a(h)
                    for t in range(6):
                        kt = 2 * sub + t
                        keys.append(dict(K=[kn.t[:, kt * 128:(kt + 1) * 128]], V=[vn.t[:, kt, :]],
                                         bias=tb.t[:, sub, t * 256:(t + 1) * 256], deps=[kn, vn, tb]))
                    attn([qt.t[:, h, sub * 256:(sub + 1) * 256]], [qt], 256, keys, 1, Ob, Lb, Sb, PTs)
                    finish(256, OT.t[:, h, sub * 256:(sub + 1) * 256])
            for h in range(8):
                g = h // 4
                keys = ctx_keys_b(g)

                def ld(r, g=g):
                    kc_, vc_ = kch[r % 2], vch[r % 2]
                    S.dma("sp", kc_.t[:], KbT_all[g][r * 128:(r + 1) * 128, :], kc_, w=[kc_])
                    S.dma("sp", vc_.t[:], Vb_all[g][r * 128:(r + 1) * 128, :].rearrange("p (k d) -> p k d", d=128),
                          vc_, w=[vc_])
                ld(0)
                for r in range(4):
                    kc_, vc_ = kch[r % 2], vch[r % 2]
                    for kt in range(NKT):
                        keys.append(dict(K=[kc_.t[:, kt * 128:(kt + 1) * 128]], V=[vc_.t[:, kt, :]], bias=None,
                                         deps=[kc_, vc_],
                                         pre=((lambda r=r: ld(r + 1)) if (kt == 3 and r < 3) else None)))
                attn([qt.t[:, 8 + h, :]], [qt], 512, keys, 1, Ob, Lb, Sb, PTs)
                finish(512, OT.t[:, 8 + h, :])
            outproj(es, 0, cur, OT, 16, 512, LAT0 + j * 512, 0, j == 0, j == NT - 1, xr, wts, wctr)
        S.dma("sp", qt.t[:, :, 0:256], QT0.ap().rearrange("h p t -> p h t")[:, :, T:T + 256], qt, w=[qt])
        for h in range(8):
            attn([qt.t[:, h, 0:256]], [qt], 256, ctx_keys_a(h), 1, Ob, Lb, Sb, PTs)
            finish(256, OT.t[:, h, 0:256])
        for h in range(8):
            attn([qt.t[:, 8 + h, 0:256]], [qt], 256, ctx_keys_b(h // 4), 1, Ob, Lb, Sb, PTs)
            finish(256, OT.t[:, 8 + h, 0:256])
        outproj(es, 0, cur, OT, 16, 256, CTX0, 1, False, False, xr, wts, wctr)
        halo_exchange(KD, put_x_halo(cur))
        S.barrier()
        es.close()


    def ffn(l, cur):
        es = ExitStack()
        wg, wu, wd = W["g%d" % l], W["u%d" % l], W["d%d" % l]
        xe = S.sb("f_xe", [128, KD, 512], F32, es)
        h2 = S.sb("f_h2", [128, KD, 512], BF16, es)
        wgt = [S.sb("f_wg%d" % i, wg.tile_shape(), BF16, es) for i in range(2)]
        wut = [S.sb("f_wu%d" % i, wu.tile_shape(), BF16, es) for i in range(2)]
        wdt = [S.sb("f_wd%d" % i, wd.tile_shape(), BF16, es) for i in range(2)]
        actT = S.sb("f_act", [128, KF, 512], BF16, es)
        tb = [S.sb("f_tb%d" % i, [128, 512], F32, es) for i in range(2)]
        sbb = [S.sb("f_sb%d" % i, [128, 512], F32, es) for i in range(2)]
        FT = cfg.FT
        tiles = []
        s0 = 0
        while s0 < T:
            n = min(FT, T - s0)
            tiles.append(("lat", s0, n))
            s0 += n
        if l < L - 1:
            tiles.append(("ctx", 0, 256))
        wi = 0

        def fcw(i, f):
            o = sm_off["fcw"][0] + (l * 3 + i) * KF + f
            return smt.t[:, o:o + 1]

        def fcb(f):
            o = sm_off["fcb"][0] + l * KF + f
            return smt.t[:, o:o + 1]

        def fx(i, f):
            o = (l * 4 + i) * KF + f
            return fixw.t[:, o:o + 1]

        for ti, (kind, s0, n) in enumerate(tiles):
            tt = 0 if kind == "lat" else 1
            c0 = (LAT0 + s0 - 1) if kind == "lat" else (CTX0 - 1)
            cw = n + 2
            S.dma("sp", xe.t[:, :, :cw], XTv(cur)[:, :, c0:c0 + cw], xe, w=[xe])
            norm_mod(xe, cw, lambda k: DRV(l, tt, 3, k), lambda k: DRV(l, tt, 4, k), h2)
            for g in range(wg.G):
                a, b = wgt[wi % 2], wut[wi % 2]
                wi += 1
                load_w(wg, a, g)
                load_w(wu, b, g)
                for ci in range(wg.gw // 128):
                    f = g * (wg.gw // 128) + ci
                    pg = PS[(2 * f) % 6]
                    pu = PS[(2 * f + 1) % 6]
                    for k in range(KD):
                        S.pe(lambda k=k, a=a, pg=pg, ci=ci: nc.tensor.matmul(
                            pg.t[:, :cw], lhsT=wg.lhs(a, k, ci * 128, (ci + 1) * 128), rhs=h2.t[:, k, :cw],
                            start=(k == 0), stop=(k == KD - 1)), r=[a, h2], w=[pg])
                    for k in range(KD):
                        S.pe(lambda k=k, b=b, pu=pu, ci=ci: nc.tensor.matmul(
                            pu.t[:, :cw], lhsT=wu.lhs(b, k, ci * 128, (ci + 1) * 128), rhs=h2.t[:, k, :cw],
                            start=(k == 0), stop=(k == KD - 1)), r=[b, h2], w=[pu])
                    t_ = tb[f % 2]
                    s_ = sbb[f % 2]
                    S.act(lambda pg=pg, t_=t_, f=f: nc.scalar.activation(
                        out=t_.t[:, :n], in_=pg.t[:, 1:n + 1], func=AF.Identity, scale=fcw(1, f), bias=fcb(f)),
                        r=[pg, smt], w=[t_])
                    S.dve(lambda pg=pg, t_=t_, f=f: nc.vector.scalar_tensor_tensor(
                        t_.t[:, :n], pg.t[:, 0:n], fcw(0, f), t_.t[:, :n], op0=ALU.mult, op1=ALU.add),
                        r=[pg, smt, t_], w=[t_])
                    S.dve(lambda pg=pg, t_=t_, f=f: nc.vector.scalar_tensor_tensor(
                        t_.t[:, :n], pg.t[:, 2:n + 2], fcw(2, f), t_.t[:, :n], op0=ALU.mult, op1=ALU.add),
                        r=[pg, smt, t_], w=[t_])
                    fl = (0 if kind == "lat" else 2) if (kind == "ctx" or s0 == 0) else None
                    fr = (1 if kind == "lat" else 3) if (kind == "ctx" or s0 + n == T) else None
                    if fl is not None:
                        S.dve(lambda pg=pg, t_=t_, f=f, fl=fl: nc.vector.scalar_tensor_tensor(
                            t_.t[:, 0:1], pg.t[:, 0:1], fx(fl, f), t_.t[:, 0:1], op0=ALU.mult, op1=ALU.add),
                            r=[pg, fixw, t_], w=[t_])
                    if fr is not None:
                        S.dve(lambda pg=pg, t_=t_, f=f, fr=fr: nc.vector.scalar_tensor_tensor(
                            t_.t[:, n - 1:n], pg.t[:, n + 1:n + 2], fx(fr, f), t_.t[:, n - 1:n], op0=ALU.mult,
                            op1=ALU.add), r=[pg, fixw, t_], w=[t_])
                    S.act(lambda t_=t_, s_=s_: nc.scalar.activation(out=s_.t[:, :n], in_=t_.t[:, :n], func=AF.Silu),
                          r=[t_], w=[s_])
                    S.dve(lambda s_=s_, pu=pu, f=f: nc.vector.tensor_tensor(
                        actT.t[:, f, :n], s_.t[:, :n], pu.t[:, 1:n + 1], op=ALU.mult), r=[s_, pu], w=[actT])
            for g in range(wd.G):
                d_ = wdt[g % 2]
                load_w(wd, d_, g)
                for mi in range(wd.gw // 128):
                    m = g * (wd.gw // 128) + mi
                    ps = PS[m % 6]
                    for f in range(KF):
                        S.pe(lambda f=f, d_=d_, ps=ps, mi=mi: nc.tensor.matmul(
                            ps.t[:, :n], lhsT=wd.lhs(d_, f, mi * 128, (mi + 1) * 128), rhs=actT.t[:, f, :n],
                            start=(f == 0), stop=(f == KF - 1)), r=[d_, actT], w=[ps])
                    S.dve(lambda ps=ps, m=m: nc.vector.scalar_tensor_tensor(
                        xe.t[:, m, 1:n + 1], ps.t[:, :n], DRV(l, tt, 5, m), xe.t[:, m, 1:n + 1], op0=ALU.mult,
                        op1=ALU.add), r=[ps, xe, drv], w=[xe])
            oc = (LAT0 + s0) if kind == "lat" else CTX0
            S.dma("sp", XTv(1 - cur)[:, :, oc:oc + n], xe.t[:, :, 1:n + 1], xe, r=[xe])
        S.barrier()
        es.close()

    def mixer1(cur):
        l = 1
        es = ExitStack()
        wm = W["out1"]
        NK1 = KC + 8
        qt = S.sb("n_qt", [128, 8, 512], BF16, es)
        OT = S.sb("n_ot", [128, NK1, 512], BF16, es)
        xr = [S.sb("n_xr%d" % i, [128, wm.gw // 128, 512], F32, es) for i in range(1)]
        wts = [S.sb("n_w%d" % i, wm.tile_shape(), BF16, es) for i in range(1)]
        gcu = S.sb("n_gcu", [128, KC, 514], F32, es)
        gb = S.sb("n_gb", [128, KC, 512], F32, es)
        kch = [S.sb("n_kch%d" % i, [128, 2, T], BF16, es) for i in range(2)]
        vch = [S.sb("n_vch%d" % i, [128, NKT, 256], BF16, es) for i in range(2)]
        kdc = S.sb("n_kdc", [128, 8, 256], BF16, es)
        vdc = S.sb("n_vdc", [128, 4, 2, 256], BF16, es)
        PTs = [S.sb("n_pt%d" % i, [128, 512], BF16, es) for i in range(4)]
        rinv = S.sb("n_rinv", [128, 512], F32, es)
        lacc = S.sb("n_lacc", [128, 512], F32, es)
        lhi = S.sb("n_lhi", [128, 512], BF16, es)
        llo = S.sb("n_llo", [128, 512], BF16, es)
        o1 = S.sb("n_o1", [128, 512], F32, es)
        o2 = S.sb("n_o2", [128, 512], F32, es)
        S.dma("sp", kdc.t[:], KdTc.ap().rearrange("h p t -> p h t"), kdc, w=[kdc])
        S.dma("sp", vdc.t[:], Vdc[:, :, :, :], vdc, w=[vdc])
        Ob, Lb, Sb = [[PS[2], PS[3]], [PS[4], PS[5]]], [PS[6], PS[7]], [PS[0], PS[1]]
        wctr = [0]

        def scw(i, c):
            o = sm_off["scw"][0] + i * KC + c
            return smt.t[:, o:o + 1]

        for j in range(NT):
            S.dma("sp", qt.t[:], QT1.ap().rearrange("h p t -> p h t")[:, :, j * 512:(j + 1) * 512], qt, w=[qt])
            S.dma("sp", gcu.t[:], GCU.ap().rearrange("c p t -> p c t")[:, :, j * 512: j * 512 + 514], gcu, w=[gcu])
            S.dma("sp", gb.t[:], GB.ap().rearrange("c p t -> p c t")[:, :, j * 512:(j + 1) * 512], gb, w=[gb])
            for c in range(KC):
                S.act(lambda c=c: nc.scalar.activation(out=tmp.t[:, :512], in_=gcu.t[:, c, 1:513], func=AF.Copy,
                                                       scale=1.0), r=[gcu], w=[tmp])
                S.dve(lambda c=c: nc.vector.tensor_scalar(tmp.t[:, :512], tmp.t[:, :512], scw(1, c), None, op0=ALU.mult),
                      r=[tmp, smt], w=[tmp])
                S.dve(lambda c=c: nc.vector.scalar_tensor_tensor(
                    tmp.t[:, :512], gcu.t[:, c, 0:512], scw(0, c), tmp.t[:, :512], op0=ALU.mult, op1=ALU.add),
                    r=[gcu, smt, tmp], w=[tmp])
                S.dve(lambda c=c: nc.vector.scalar_tensor_tensor(
                    tmp.t[:, :512], gcu.t[:, c, 2:514], scw(2, c), tmp.t[:, :512], op0=ALU.mult, op1=ALU.add),
                    r=[gcu, smt, tmp], w=[tmp])
                S.dve(lambda c=c: nc.vector.tensor_tensor(OT.t[:, c, :], tmp.t[:, :512], gb.t[:, c, :], op=ALU.mult),
                      r=[tmp, gb], w=[OT])
            for h in range(4):
                for sub in range(2):
                    qs = [qt.t[:, 2 * h + c, sub * 256:(sub + 1) * 256] for c in range(2)]
                    keys = [dict(K=[kdc.t[:, 2 * h + c, kt * 128:(kt + 1) * 128] for c in range(2)],
                                 V=[vdc.t[:, h, kt, dv * 128:(dv + 1) * 128] for dv in range(2)], bias=None,
                                 deps=[kdc, vdc]) for kt in range(2)]

                    def ld(r, h=h):
                        kc_, vc_ = kch[r % 2], vch[r % 2]
                        for c in range(2):
                            S.dma("sp", kc_.t[:, c, :], KdT_all[2 * h + c][r * 128:(r + 1) * 128, :], kc_, w=[kc_])
                        for hf in range(2):
                            S.dma("sp", vc_.t[:, hf * NKH:(hf + 1) * NKH, :],
                                  Vd_all[2 * h + hf][r * 128:(r + 1) * 128, :].rearrange("p (k d) -> p k d", d=256),
                                  vc_, w=[vc_])
                    ld(0)
                    for r in range(4):
                        kc_, vc_ = kch[r % 2], vch[r % 2]
                        for kt in range(NKT):
                            keys.append(dict(K=[kc_.t[:, c, kt * 128:(kt + 1) * 128] for c in range(2)],
                                             V=[vc_.t[:, kt, dv * 128:(dv + 1) * 128] for dv in range(2)], bias=None,
                                             deps=[kc_, vc_],
                                             pre=((lambda r=r: ld(r + 1)) if (kt == 3 and r < 3) else None)))
                    attn(qs, [qt], 256, keys, 2, Ob, Lb, Sb, PTs, lacc=lacc)
                    S.act(lambda: nc.scalar.copy(lhi.t[:, :512], lacc.t[:, :512]), r=[lacc], w=[lhi])
                    S.dve(lambda: nc.vector.tensor_tensor(llo.t[:, :512], lacc.t[:, :512], lhi.t[:, :512],
                                                          op=ALU.subtract), r=[lacc, lhi], w=[llo])
                    for c in range(2):
                        S.pe(lambda c=c: nc.tensor.matmul(Lb[c].t[:, 0:256], lhsT=onesb.t[:],
                                                          rhs=lhi.t[:, c * 256:(c + 1) * 256], start=True, stop=False),
                             r=[lhi, onesb], w=[Lb[c]])
                        S.pe(lambda c=c: nc.tensor.matmul(Lb[c].t[:, 0:256], lhsT=onesb.t[:],
                                                          rhs=llo.t[:, c * 256:(c + 1) * 256], start=False, stop=True),
                             r=[llo, onesb], w=[Lb[c]])
                    for c in range(2):
                        S.dve(lambda c=c: nc.vector.reciprocal(rinv.t[:, c * 256:(c + 1) * 256], Lb[c].t[:, 0:256]),
                              r=[Lb[c], rinv], w=[rinv])
                    for dv in range(2):
                        S.dve(lambda dv=dv: nc.vector.tensor_tensor(
                            o1.t[:, dv * 256:(dv + 1) * 256], Ob[0][dv].t[:, 0:256], rinv.t[:, 0:256],
                            op=ALU.mult), r=[Ob[0][dv], rinv, o1], w=[o1])
                        S.dve(lambda dv=dv: nc.vector.tensor_tensor(
                            o2.t[:, dv * 256:(dv + 1) * 256], Ob[1][dv].t[:, 0:256], rinv.t[:, 256:512],
                            op=ALU.mult), r=[Ob[1][dv], rinv, o2], w=[o2])
                    S.dve(lambda: nc.vector.scalar_tensor_tensor(
                        o1.t[:, :512], o2.t[:, :512], lamt.t[:, 5:6], o1.t[:, :512], op0=ALU.mult, op1=ALU.add),
                        r=[o1, o2, lamt], w=[o1])
                    S.act(lambda: nc.scalar.activation(out=sqb.t[:, :512], in_=o1.t[:, :512], func=AF.Square),
                          r=[o1], w=[sqb])
                    aux = PS[0]
                    for dv in range(2):
                        S.pe(lambda dv=dv: nc.tensor.matmul(aux.t[:, :256], lhsT=onesb.t[:],
                                                            rhs=sqb.t[:, dv * 256:(dv + 1) * 256],
                                                            start=(dv == 0), stop=(dv == 1)), r=[sqb, onesb], w=[aux])
                    S.act(lambda: nc.scalar.activation(out=rstd.t[:, :256], in_=aux.t[:, :256], func=AF.Sqrt,
                                                       scale=1.0 / 256, bias=epst.t[:, 0:1]), r=[aux, epst], w=[rstd])
                    S.dve(lambda: nc.vector.reciprocal(rstd.t[:, :256], rstd.t[:, :256]), r=[rstd], w=[rstd])
                    for dv in range(2):
                        S.dve(lambda dv=dv: nc.vector.tensor_tensor(
                            tmp2.t[:, :256], o1.t[:, dv * 256:(dv + 1) * 256], rstd.t[:, :256], op=ALU.mult),
                            r=[o1, rstd], w=[tmp2])
                        S.act(lambda dv=dv, h=h, sub=sub: nc.scalar.activation(
                            out=OT.t[:, KC + 2 * h + dv, sub * 256:(sub + 1) * 256], in_=tmp2.t[:, :256],
                            func=AF.Identity, scale=lamt.t[:, 6 + dv:7 + dv]), r=[tmp2, lamt], w=[OT])
            outproj(es, 1, cur, OT, NK1, 512, LAT0 + j * 512, 0, j == 0, j == NT - 1, xr, wts, wctr)
        halo_exchange(KD, put_x_halo(cur))
        S.barrier()
        es.close()

    def final(cur, nonorm=False):
        es = ExitStack()
        xt = [S.sb("o_xt%d" % i, [128, KD, 512], F32, es) for i in range(2)]
        yT = S.sb("o_y", [128, KD, 512], F32, es)
        ot = [S.sb("o_ot%d" % i, [128, D], F32, es) for i in range(2)]
        oi = 0
        for j in range(NT):
            x = xt[j % 2]
            S.dma("sp", x.t[:], XTv(cur)[:, :, LAT0 + j * 512: LAT0 + (j + 1) * 512], x, w=[x])
            ps = PS[7]
            for k in range(KD):
                S.act(lambda k=k, x=x: nc.scalar.activation(out=sq.t[:, :512], in_=x.t[:, k, :], func=AF.Square),
                      r=[x], w=[sq])
                S.pe(lambda k=k: nc.tensor.matmul(ps.t[:, :512], lhsT=onesf.t[:], rhs=sq.t[:, :512],
                                                  start=(k == 0), stop=(k == KD - 1)), r=[sq, onesf], w=[ps])
            S.act(lambda: nc.scalar.activation(out=rstd.t[:, :512], in_=ps.t[:, :512], func=AF.Sqrt, scale=1.0 / D,
                                               bias=epst.t[:, 0:1]), r=[ps, epst], w=[rstd])
            S.dve(lambda: nc.vector.reciprocal(rstd.t[:, :512], rstd.t[:, :512]), r=[rstd], w=[rstd])
            for k in range(KD):
                if nonorm:
                    S.dve(lambda k=k, x=x: nc.vector.tensor_copy(yT.t[:, k, :], x.t[:, k, :]), r=[x], w=[yT])
                    continue
                S.dve(lambda k=k, x=x: nc.vector.scalar_tensor_tensor(
                    yT.t[:, k, :], x.t[:, k, :], sm("fg", k), rstd.t[:, :512], op0=ALU.mult, op1=ALU.mult),
                    r=[x, rstd, smt], w=[yT])
            for s in range(4):
                o = ot[oi % 2]
                oi += 1
                for k4 in range(0, KD, 4):
                    pb = PS[(k4 // 4) % 6]
                    for kk in range(min(4, KD - k4)):
                        k = k4 + kk
                        S.pe(lambda k=k, kk=kk, pb=pb, s=s: nc.tensor.transpose(
                            pb.t[:, kk * 128:(kk + 1) * 128], yT.t[:, k, s * 128:(s + 1) * 128], ident),
                            r=[yT, cst], w=[pb])
                    wd_ = min(4, KD - k4) * 128
                    if (k4 // 4) % 2 == 0:
                        S.dve(lambda k4=k4, pb=pb, o=o, wd_=wd_: nc.vector.tensor_copy(
                            o.t[:, k4 * 128:k4 * 128 + wd_], pb.t[:, :wd_]), r=[pb], w=[o])
                    else:
                        S.act(lambda k4=k4, pb=pb, o=o, wd_=wd_: nc.scalar.copy(
                            o.t[:, k4 * 128:k4 * 128 + wd_], pb.t[:, :wd_]), r=[pb], w=[o])
                S.dma("sp", out_tm[j * 512 + s * 128: j * 512 + (s + 1) * 128, :], o.t[:], o, r=[o])
        S.barrier()
        es.close()

    def gather_kv0():
        S.barrier(pool=True)
        for i in range(2):
            S.coll("AllGather", ALU.bypass, GRP, KbT_src[i].ap().opt(), KbT_all[i].ap().opt(), collb)
            S.coll("AllGather", ALU.bypass, GRP, Vb_src[i].ap().opt(), Vb_all[i].ap().opt(), collb)
        S.barrier(pool=True)
        stage_weights(["in1", "out1", "g1", "u1", "d1"])

    def gather_kv1():
        S.barrier(pool=True)
        for i in range(8):
            S.coll("AllGather", ALU.bypass, GRP, KdT_src[i].ap().opt(), KdT_all[i].ap().opt(), collb)
            S.coll("AllGather", ALU.bypass, GRP, Vd_src[i].ap().opt(), Vd_all[i].ap().opt(), collb)

        def put(left, right):
            S.dma("sp", GCU.ap().rearrange("c p t -> p c t")[:, :, 0:1], left, hxo, r=[hxo])
            S.dma("sp", GCU.ap().rearrange("c p t -> p c t")[:, :, T + 1:T + 2], right, hxo, r=[hxo])
        halo_exchange(KC, put)
        S.barrier(pool=True)

    stop = cfg.stop if hasattr(cfg, "stop") else 99
    inproj(0, 0)
    if STOP == 7:
        return nc, W
    gather_kv0()
    if STOP == 8:
        return nc, W
    mixer0(0)
    if STOP == 9:
        if DBG:
            final(0, nonorm=True)
        return nc, W
    ffn(0, 0)
    if STOP == 10:
        if DBG:
            final(1, nonorm=True)
        return nc, W
    inproj(1, 1)
    if STOP == 11:
        return nc, W
    gather_kv1()
    if STOP == 12:
        return nc, W
    mixer1(1)
    if STOP == 13:
        if DBG:
            final(1, nonorm=True)
        return nc, W
    ffn(1, 1)
    if STOP == 14:
        if DBG:
            final(0, nonorm=True)
        return nc, W
    final(0)
    return nc, W


def small_layout(cfg):
    KD, KF, KC, L, NCH6 = cfg.KD, cfg.KF, cfg.KC, cfg.L, cfg.NCH6
    items = [("cv", KD * 2), ("m8", 8), ("mb", 2), ("mL", 4), ("mR", 4), ("nhl", 1), ("nhr", 1),
             ("istop", 1), ("ntop", 1), ("isbot", 1), ("nbot", 1),
             ("adab", L * NCH6), ("ng", L * 2 * KD), ("fg", KD), ("qg", 1), ("kg", 1),
             ("lq1", 1), ("lk1", 1), ("lq2", 1), ("lk2", 1), ("slg", 2), ("scw", 3 * KC),
             ("fcw", L * 3 * KF), ("fcb", L * KF)]
    off, o = {}, 0
    for n, w in items:
        off[n] = (o, w)
        o += w
    return off, o


def _pl(v, nch):
    return np.ascontiguousarray(np.asarray(v, np.float32).reshape(nch, 128).T)


def host_inputs(cfg, inp):
    D, DFF, T, KD, KF, KC, L, NCH6 = cfg.D, cfg.DFF, cfg.T, cfg.KD, cfg.KF, cfg.KC, cfg.L, cfg.NCH6
    f = lambda a: np.asarray(a, np.float32)
    x, c, ctx, c_ctx = f(inp["x"]), f(inp["c"]), f(inp["ctx"]), f(inp["c_ctx"])
    off, smw = small_layout(cfg)
    ident = np.eye(128, dtype=np.float32)
    Rm = np.zeros((128, 128), np.float32)
    for do in range(128):
        if (do % 64) < 32:
            Rm[do + 32, do] = -1.0
        else:
            Rm[do - 32, do] = 1.0
    consts = np.concatenate([ident, Rm], axis=1)
    inv = (1.0 / (10000.0 ** (np.arange(0, 64, 2, dtype=np.float32) / 64.0))).astype(np.float32)
    rpb = f(inp["na_rpb"])[0]
    qc = np.arange(64)
    cs = np.clip(qc - 8, 0, 48)
    kc = np.arange(64)
    valid = (kc[:, None] >= cs[None, :]) & (kc[:, None] < cs[None, :] + 16)
    cidx = np.clip(kc[:, None] - qc[None, :] + 15, 0, 30)
    braw = np.empty((8, 15, 64, 64), np.float32)
    for nd in range(15):
        braw[:, nd] = np.where(valid[None], rpb[:, 14 - nd][:, cidx], np.float32(NEG))
    wsrc = {"in0": f(inp["ab_w_in"])[0], "out0": f(inp["ab_w_out"])[0], "in1": f(inp["cd_w_in"])[0],
            "out1": f(inp["cd_w_out"])[0]}
    for l in range(L):
        wsrc["g%d" % l] = f(inp["ffn_w_gate"])[l]
        wsrc["u%d" % l] = f(inp["ffn_w_up"])[l]
        wsrc["d%d" % l] = f(inp["ffn_w_down"])[l]
    ada_w = f(inp["ada_w"])
    maps = []
    for core in range(8):
        b, q = core // 4, core % 4
        m = {}
        m["x_tm"] = np.ascontiguousarray(x[b, q * T:(q + 1) * T])
        xh = np.zeros((512, D), np.float32)
        if q > 0:
            xh[0:256] = x[b, q * T - 256:q * T]
        if q < 3:
            xh[256:512] = x[b, (q + 1) * T:(q + 1) * T + 256]
        m["xh_tm"] = xh
        m["ctx_tm"] = np.ascontiguousarray(ctx[b])
        sm = np.zeros((128, smw), np.float32)

        def put(name, arr):
            o, w = off[name]
            sm[:, o:o + w] = np.asarray(arr, np.float32).reshape(128, w)
        cv = np.stack([_pl(c[b], KD), _pl(c_ctx, KD)], axis=2)
        put("cv", cv.reshape(128, KD * 2))
        e8 = np.zeros(8, np.float32); e8[core] = 1
        put("m8", np.tile(e8, (128, 1)))
        eb = np.zeros(2, np.float32); eb[b] = 1
        put("mb", np.tile(eb, (128, 1)))
        el = np.zeros(4, np.float32); er = np.zeros(4, np.float32)
        if q > 0:
            el[q - 1] = 1
        if q < 3:
            er[q + 1] = 1
        put("mL", np.tile(el, (128, 1))); put("mR", np.tile(er, (128, 1)))
        put("nhl", np.full((128, 1), 1.0 if q == 0 else 0.0)); put("nhr", np.full((128, 1), 1.0 if q == 3 else 0.0))
        put("istop", np.full((128, 1), 1.0 if q == 0 else 0.0)); put("ntop", np.full((128, 1), 0.0 if q == 0 else 1.0))
        put("isbot", np.full((128, 1), 1.0 if q == 3 else 0.0)); put("nbot", np.full((128, 1), 0.0 if q == 3 else 1.0))
        put("adab", np.concatenate([_pl(f(inp["ada_b"])[l], NCH6) for l in range(L)], axis=1))
        put("ng", np.concatenate([_pl(f(inp["norm_g"])[l, i], KD) for l in range(L) for i in range(2)], axis=1))
        put("fg", _pl(f(inp["final_g"]), KD))
        put("qg", f(inp["gqa_q_gain"])[0].reshape(128, 1)); put("kg", f(inp["gqa_k_gain"])[0].reshape(128, 1))
        for nm in ("lq1", "lk1", "lq2", "lk2"):
            put(nm, f(inp["diff_" + nm])[0].reshape(128, 1))
        put("slg", _pl(f(inp["diff_subln_g"])[0], 2))
        put("scw", np.concatenate([_pl(f(inp["sconv_w"])[0, j], KC) for j in range(3)], axis=1))
        put("fcw", np.concatenate([_pl(f(inp["ffn_conv_w"])[l, j], KF) for l in range(L) for j in range(3)], axis=1))
        put("fcb", np.concatenate([_pl(f(inp["ffn_conv_b"])[l], KF) for l in range(L)], axis=1))
        m["small"] = sm
        t = np.arange(q * T, (q + 1) * T)
        ang_r = (t // GW_).astype(np.float32)[:, None] * inv
        ang_c = (t % GW_).astype(np.float32)[:, None] * inv
        ang = np.concatenate([ang_r, ang_r, ang_c, ang_c], axis=-1)
        m["cosT"] = np.ascontiguousarray(np.cos(ang).astype(np.float32).T)
        m["sinT"] = np.ascontiguousarray(np.sin(ang).astype(np.float32).T)
        m["consts"] = consts
        m["braw"] = braw
        ncs = 6 * D // 4
        m["adaw"] = np.ascontiguousarray(ada_w[:, :, q * ncs:(q + 1) * ncs])
        maps.append(m)
    return maps, wsrc


_CACHE = {}


def run(cfg, inp):
    key = (cfg.D, cfg.DFF, cfg.SEQ, cfg.CW)
    if key not in _CACHE:
        _CACHE[key] = build(cfg)
    nc, W = _CACHE[key]
    maps, wsrc = host_inputs(cfg, inp)
    for core in range(8):
        q = core % 4
        for k, wm in W.items():
            maps[core][wm.name + "_f32"] = wm.host(wsrc[k], q)
    res = run_bass_kernel_spmd(nc, maps, core_ids=list(range(8)))
    T = cfg.T
    out = np.empty((2, cfg.SEQ, cfg.D), np.float32)
    for core in range(8):
        b, q = core // 4, core % 4
        out[b, q * T:(q + 1) * T] = np.asarray(res.results[core]["out_tm"], np.float32)
    return out


def kernel(**inputs):
    return run(Cfg(), inputs)
```
